# Optimizing a Trainium2 kernel written in Bass

```python
import jax, jax.numpy as jnp
from jax import lax
import numpy as np

D_MODEL = 1024
BATCH = 8
SEQ = 4096
DEPTH = 1

ATTN_HEADS = 8
ATTN_HEAD_DIM = 64
ATTN_WIDTH = ATTN_HEADS * ATTN_HEAD_DIM
DILATED_BRANCHES = ((128, 1), (512, 4), (2048, 16))
PAD_MULT = max(w for w, _ in DILATED_BRANCHES)
RET_HEADS = 4
RET_KEY_DIM = 64
RET_VALUE_DIM = 128
RET_QK_WIDTH = RET_HEADS * RET_KEY_DIM
RET_V_WIDTH = RET_HEADS * RET_VALUE_DIM
RET_CHUNK = 128
MIX_WIDTH = ATTN_WIDTH + RET_V_WIDTH
IN_SPLITS = (ATTN_WIDTH, ATTN_WIDTH, ATTN_WIDTH, ATTN_WIDTH,
             RET_QK_WIDTH, RET_QK_WIDTH, RET_V_WIDTH, RET_V_WIDTH)
IN_WIDTH = sum(IN_SPLITS)
ROPE_THETA = 10000.0
NORM_EPS = 1e-6
GN_EPS = 1e-5

kernel_name = "hymba_dilated_attn_retention_block"


def rms_norm(x, g):
    xf = x.astype(jnp.float32)
    y = xf * lax.rsqrt(jnp.mean(xf * xf, axis=-1, keepdims=True) + NORM_EPS)
    return (y * g.astype(jnp.float32)).astype(x.dtype)


def rope_half(x, pos):
    half = x.shape[-1] // 2
    inv = ROPE_THETA ** (-jnp.arange(half, dtype=jnp.float32) / half)
    ang = pos.astype(jnp.float32)[:, None] * inv[None, :]
    cos, sin = jnp.cos(ang).astype(x.dtype), jnp.sin(ang).astype(x.dtype)
    x1, x2 = x[..., :half], x[..., half:]
    return jnp.concatenate([x1 * cos - x2 * sin, x2 * cos + x1 * sin], axis=-1)


def retnet_rotate(x, pos):
    dk = x.shape[-1]
    half = dk // 2
    inv = 1.0 / (ROPE_THETA ** jnp.linspace(0.0, 1.0, half, dtype=jnp.float32))
    ang = pos.astype(jnp.float32)[:, None] * inv[None, :]
    cos, sin = jnp.cos(ang).astype(x.dtype), jnp.sin(ang).astype(x.dtype)
    xp = x.reshape(x.shape[:-1] + (half, 2))
    x1, x2 = xp[..., 0], xp[..., 1]
    out = jnp.stack([x1 * cos - x2 * sin, x2 * cos + x1 * sin], axis=-1)
    return out.reshape(x.shape)


def dilated_branch(q, k, v, window, dilation):
    B, H, Sp, Dh = q.shape
    steps = window // dilation
    L = Sp // dilation
    nb = L // steps

    def gather_stride(t):
        t = t.reshape(B, H, L, dilation, Dh).transpose(0, 1, 3, 2, 4)
        return t.reshape(B, H, dilation, nb, steps, Dh)

    def with_prev(t):
        prev = jnp.pad(t, ((0, 0), (0, 0), (0, 0), (1, 0), (0, 0), (0, 0)))[:, :, :, :-1]
        return jnp.concatenate([prev, t], axis=4)

    qb = gather_stride(q)
    kb = with_prev(gather_stride(k))
    vb = with_prev(gather_stride(v))
    s = jnp.einsum('bhrnqd,bhrnkd->bhrnqk', qb, kb,
                   preferred_element_type=jnp.float32) * (Dh ** -0.5)
    qi = jnp.arange(steps)[:, None]
    kj = jnp.arange(2 * steps)[None, :]
    dist = steps + qi - kj
    band = (dist >= 0) & (dist <= steps)
    has_prev = (jnp.arange(nb) > 0)[:, None, None]
    mask = band[None] & (has_prev | (kj >= steps)[None])
    s = jnp.where(mask, s, jnp.finfo(jnp.float32).min)
    m = jnp.max(s, axis=-1, keepdims=True)
    p = jnp.exp(s - m)
    den = jnp.sum(p, axis=-1, keepdims=True)
    o = jnp.einsum('bhrnqk,bhrnkd->bhrnqd', p.astype(v.dtype), vb,
                   preferred_element_type=jnp.float32) / den
    lse = (m + jnp.log(den))[..., 0]
    o = o.reshape(B, H, dilation, L, Dh).transpose(0, 1, 3, 2, 4).reshape(B, H, Sp, Dh)
    lse = lse.reshape(B, H, dilation, L).transpose(0, 1, 3, 2).reshape(B, H, Sp)
    return o, lse


def dilated_attention(q, k, v):
    B, H, S, Dh = q.shape
    Sp = -(-S // PAD_MULT) * PAD_MULT
    pad = ((0, 0), (0, 0), (0, Sp - S), (0, 0))
    pos = jnp.arange(Sp)
    qp = rope_half(jnp.pad(q, pad), pos)
    kp = rope_half(jnp.pad(k, pad), pos)
    vp = jnp.pad(v, pad)
    outs, lses = [], []
    for window, dilation in DILATED_BRANCHES:
        o, lse = dilated_branch(qp, kp, vp, window, dilation)
        outs.append(o)
        lses.append(lse)
    w = jax.nn.softmax(jnp.stack(lses, axis=0), axis=0)
    o = jnp.sum(w[..., None] * jnp.stack(outs, axis=0), axis=0)
    return o[:, :, :S].astype(q.dtype)


def retention(q, k, v):
    B, H, S, Dk = q.shape
    Dv = v.shape[-1]
    C = RET_CHUNK
    N = S // C
    pos = jnp.arange(S)
    q = retnet_rotate(q, pos).astype(jnp.float32)
    k = (retnet_rotate(k, pos) * (Dk ** -0.5)).astype(jnp.float32)
    v = v.astype(jnp.float32)
    log_gamma = jnp.log1p(-(2.0 ** (-5.0 - jnp.arange(H, dtype=jnp.float32))))
    cpos = jnp.arange(C, dtype=jnp.float32)
    diff = cpos[:, None] - cpos[None, :]
    decay = jnp.where(diff[None] >= 0, jnp.exp(jnp.maximum(diff, 0.0)[None] * log_gamma[:, None, None]), 0.0)
    qc = q.reshape(B, H, N, C, Dk)
    kc = k.reshape(B, H, N, C, Dk)
    vc = v.reshape(B, H, N, C, Dv)
    s = jnp.einsum('bhnid,bhnjd->bhnij', qc, kc) * decay[None, :, None]
    o_intra = jnp.einsum('bhnij,bhnjv->bhniv', s, vc)
    k_dec = jnp.exp((C - 1 - cpos)[None, :] * log_gamma[:, None])
    q_dec = jnp.exp((cpos + 1)[None, :] * log_gamma[:, None])
    chunk_decay = jnp.exp(C * log_gamma)[None, :, None, None]
    kv = jnp.einsum('bhnjd,hj,bhnjv->nbhdv', kc, k_dec, vc)

    def step(state, kv_n):
        return state * chunk_decay + kv_n, state

    _, state_prev = lax.scan(step, jnp.zeros((B, H, Dk, Dv), jnp.float32), kv)
    o_inter = jnp.einsum('bhnid,hi,nbhdv->bhniv', qc, q_dec, state_prev)
    o = (o_intra + o_inter).reshape(B, H, S, Dv)
    mu = jnp.mean(o, axis=-1, keepdims=True)
    var = jnp.mean(jnp.square(o - mu), axis=-1, keepdims=True)
    return (o - mu) * lax.rsqrt(var + GN_EPS)


def split_heads(t, n_heads):
    B, S, W = t.shape
    return t.reshape(B, S, n_heads, W // n_heads).transpose(0, 2, 1, 3)


def merge_heads(t):
    B, H, S, Dh = t.shape
    return t.transpose(0, 2, 1, 3).reshape(B, S, H * Dh)


def hybrid_mixer(h, w_in, ret_gn_gain, w_out):
    proj = h @ w_in
    cuts = [int(c) for c in np.cumsum(IN_SPLITS)[:-1]]
    qa, ka, va, ga, qr, kr, vr, gr = jnp.split(proj, cuts, axis=-1)
    attn = merge_heads(dilated_attention(split_heads(qa, ATTN_HEADS),
                                         split_heads(ka, ATTN_HEADS),
                                         split_heads(va, ATTN_HEADS)))
    ret = merge_heads(retention(split_heads(qr, RET_HEADS),
                                split_heads(kr, RET_HEADS),
                                split_heads(vr, RET_HEADS)))
    ret = (ret * ret_gn_gain.astype(jnp.float32)).astype(h.dtype)
    mixed = jnp.concatenate([jax.nn.silu(ga) * attn, jax.nn.silu(gr) * ret], axis=-1)
    return mixed @ w_out


def setup_inputs(seed: int = 0) -> dict:
    key = jax.random.key(seed)
    ks = jax.random.split(key, 7)
    x = jax.random.normal(ks[0], (BATCH, SEQ, D_MODEL), jnp.float32)
    norm_gain = 1.0 + 0.02 * jax.random.normal(ks[1], (DEPTH, D_MODEL), jnp.float32)
    w_in = jax.random.normal(ks[2], (DEPTH, D_MODEL, IN_WIDTH), jnp.float32) * D_MODEL ** -0.5
    ret_gn_gain = 1.0 + 0.02 * jax.random.normal(ks[3], (DEPTH, RET_V_WIDTH), jnp.float32)
    w_out = jax.random.normal(ks[4], (DEPTH, MIX_WIDTH, D_MODEL), jnp.float32) * MIX_WIDTH ** -0.5
    final_gain = 1.0 + 0.02 * jax.random.normal(ks[5], (D_MODEL,), jnp.float32)
    return {"x": x, "norm_gain": norm_gain, "w_in": w_in, "ret_gn_gain": ret_gn_gain,
            "w_out": w_out, "final_gain": final_gain}


def reference(x, norm_gain, w_in, ret_gn_gain, w_out, final_gain):
    for layer in range(DEPTH):
        h = rms_norm(x, norm_gain[layer])
        x = x + hybrid_mixer(h, w_in[layer], ret_gn_gain[layer], w_out[layer])
    return rms_norm(x, final_gain)
```

```python
import numpy as np
import ml_dtypes
import concourse.bass as bass
import concourse.mybir as mybir
from concourse.bass_utils import run_bass_kernel_spmd

dt = mybir.dt
F32 = dt.float32
BF16 = dt.bfloat16
AF = mybir.ActivationFunctionType
ALU = mybir.AluOpType
AX = mybir.AxisListType


class Sched:
    COMPUTE = ("pe", "act", "dve", "pool")
    EPOCH = 400
    STRICT = True

    def __init__(self, nc, dma_ring=None):
        self.nc = nc
        self.ops = []
        self.lastw = {}
        self.readers = {}
        self.last_on = {}
        self.extra = {}
        self.dma_live = []
        self.dma_ring = dma_ring or {"sp": 24, "pool": 12, "act": 6}

    def op(self, eng, fn, r=(), w=(), dma=False):
        i = len(self.ops)
        deps = set(self.extra.pop(eng, ()))
        for k in r:
            s = self.lastw.get(k)
            if s is not None:
                deps.add((s, 0))
        for k in w:
            s = self.lastw.get(k)
            if s is not None:
                deps.add((s, 1))
            for s in self.readers.get(k, ()):
                deps.add((s, 2))
        keep = set()
        for (s, kind) in deps:
            so = self.ops[s]
            if so["dma"] or dma:
                keep.add(s)
            elif so["eng"] == eng:
                if eng != "pe" and (kind == 0 or self.STRICT):
                    keep.add(s)
            else:
                keep.add(s)
        keep.discard(i)
        self.ops.append(dict(eng=eng, fn=fn, deps=sorted(keep), dma=dma, signal=dma))
        for s in keep:
            self.ops[s]["signal"] = True
        for k in w:
            self.lastw[k] = i
            self.readers[k] = []
        for k in r:
            lst = self.readers.setdefault(k, [])
            if not dma:
                lst[:] = [s for s in lst if self.ops[s]["dma"] or self.ops[s]["eng"] != eng]
            lst.append(i)
        self.last_on[eng] = i
        if dma:
            self.dma_live.append(i)
        return i

    def barrier(self):
        srcs = set(self.last_on.values()) | set(self.dma_live)
        self.dma_live = []
        for e in ("pe", "act", "dve", "pool", "sp"):
            self.extra.setdefault(e, set())
            for s in srcs:
                so = self.ops[s]
                if so["eng"] == e and not so["dma"]:
                    continue
                self.extra[e].add((s, 0))
                so["signal"] = True

    def emit(self, sems, dma_sems):
        nc = self.nc
        cnt = {e: 0 for e in self.COMPUTE}
        dcnt = {q: 0 for q in dma_sems}
        dval = {q: [0] * len(dma_sems[q]) for q in dma_sems}
        for o in self.ops:
            if o["dma"]:
                q = o["eng"]
                j = dcnt[q] % len(dma_sems[q])
                dcnt[q] += 1
                o["ring_prev"] = (dma_sems[q][j], dval[q][j]) if dval[q][j] > 0 else None
                dval[q][j] += 16
                o["done"] = (dma_sems[q][j], dval[q][j])
            elif o["signal"]:
                k = cnt[o["eng"]]
                cnt[o["eng"]] += 1
                o["done"] = (sems[o["eng"]][k // self.EPOCH], k % self.EPOCH + 1)
        by_eng = {}
        for o in self.ops:
            by_eng.setdefault(o["eng"], []).append(o)
        final_waits = []
        for q in dma_sems:
            for j, s in enumerate(dma_sems[q]):
                if dval[q][j] > 0:
                    final_waits.append((s, dval[q][j]))
        handles = {"pe": nc.tensor, "act": nc.scalar, "dve": nc.vector, "pool": nc.gpsimd, "sp": nc.sync}
        ops_all = self.ops

        def run(eng, e):
            waited = {}

            def wait(sem, val):
                key = id(sem)
                if waited.get(key, 0) < val:
                    e.wait_ge(sem, val)
                    waited[key] = val

            for o in by_eng.get(eng, []):
                for s in o["deps"]:
                    sem, val = ops_all[s]["done"]
                    wait(sem, val)
                if o["dma"] and o["ring_prev"] is not None:
                    wait(*o["ring_prev"])
                ins = o["fn"](e)
                if o["signal"]:
                    ins.then_inc(o["done"][0], 16 if o["dma"] else 1)
            if eng == "sp":
                for (s, v) in final_waits:
                    wait(s, v)

        with nc.Block() as block:
            @block.sync
            def _(e):
                run("sp", e)

            @block.tensor
            def _(e):
                run("pe", e)

            @block.scalar
            def _(e):
                run("act", e)

            @block.vector
            def _(e):
                run("dve", e)

            @block.gpsimd
            def _(e):
                run("pool", e)


SEQ = 4096
DM = 1024
NCORES = 8
THETA = 10000.0
_bf = ml_dtypes.bfloat16


def _host_constants():
    c = {}
    c["ident"] = np.eye(128, dtype=np.float32).astype(_bf)
    permm = np.zeros((128, 128), np.float32)
    for m in range(128):
        k = m + 32 if (m % 64) < 32 else m - 32
        permm[k, m] = 1.0
    c["permm"] = permm.astype(_bf)
    kk = np.arange(128)[:, None]
    qq = np.arange(128)[None, :]
    prev = (kk >= qq).astype(np.float32)
    own = (qq >= kk).astype(np.float32)
    c["amask"] = np.tile(np.concatenate([own, prev], 1), (1, 4)).astype(_bf)
    c["rmask"] = np.tile(own, (1, 4)).astype(_bf)
    pos = np.arange(SEQ, dtype=np.float64)
    j = np.arange(128) % 32
    sign = np.where((np.arange(128) % 64) >= 32, 1.0, -1.0)[:, None]
    inv = THETA ** (-(np.arange(32, dtype=np.float64)) / 32.0)
    ang = pos[None, :] * inv[j][:, None]
    c["ropeA"] = np.stack([np.cos(ang), np.sin(ang) * sign]).astype(np.float32)
    invr = 1.0 / (THETA ** np.linspace(0.0, 1.0, 32, dtype=np.float64))
    angr = pos[None, :] * invr[j][:, None]
    cosR = np.cos(angr)
    sinR = np.sin(angr) * sign
    lg = np.log1p(-(2.0 ** (-5.0 - np.arange(4, dtype=np.float64))))
    i_in = (np.arange(SEQ) % 128).astype(np.float64)
    ropeR = np.zeros((8, 128, SEQ), np.float32)
    for ft in range(2):
        for hp in range(2):
            h = 2 * ft + hp
            rows = slice(64 * hp, 64 * hp + 64)
            dq = np.exp((i_in + 1.0) * lg[h])[None, :]
            dk = (np.exp(-(i_in + 1.0) * lg[h]) / 8.0)[None, :]
            ropeR[ft * 4 + 0, rows] = cosR[rows] * dq
            ropeR[ft * 4 + 1, rows] = sinR[rows] * dq
            ropeR[ft * 4 + 2, rows] = cosR[rows] * dk
            ropeR[ft * 4 + 3, rows] = sinR[rows] * dk
    c["ropeR"] = ropeR
    gC = np.exp(128.0 * lg)
    gck = np.zeros((128, 256), np.float32)
    for h in range(4):
        gck[:, h * 64:(h + 1) * 64] = gC[h]
    c["gck"] = gck
    gsc = np.zeros((128, 2), np.float32)
    for cb in range(2):
        gsc[:64, cb] = gC[2 * cb]
        gsc[64:, cb] = gC[2 * cb + 1]
    c["gsc"] = gsc
    return c


def _win_cols():
    cols = []
    for p in range(4):
        for base in (0, 512, 1024, 1536):
            cols += list(range(base + 128 * p, base + 128 * p + 128))
    permh = [2 * i for i in range(32)] + [2 * i + 1 for i in range(32)]
    for base in (2048, 2304):
        for h in range(4):
            cols += [base + 64 * h + cc for cc in permh]
    cols += list(range(2560, 3584))
    return np.asarray(cols)


def _tok(buf, b, n):
    if b == 0:
        return buf[:, 128 * n:128 * n + 128]
    if b == 1:
        r, m = n // 8, n % 8
        s = r + 512 * m
        return buf[:, s:s + 509:4]
    r, m = n // 2, n % 2
    s = r + 2048 * m
    return buf[:, s:s + 2033:16]


def _tok2(buf, b, n, nblk):
    cntk = 128 * nblk
    if b == 0:
        return buf[:, 128 * n:128 * n + cntk]
    if b == 1:
        r, m = n // 8, n % 8
        s = r + 512 * m
        return buf[:, s:s + 4 * (cntk - 1) + 1:4]
    r, m = n // 2, n % 2
    s = r + 2048 * m
    return buf[:, s:s + 16 * (cntk - 1) + 1:16]


def _tok_keys(name, b, n):
    if b == 0:
        return [(name, n // 4)]
    if b == 1:
        return [(name, n % 8)]
    return [(name, 4 * (n % 2) + i) for i in range(4)]


BPC = (32, 8, 2)


def build(stop_after=None, dumps=()):
    nc = bass.Bass("TRN2", target_bir_lowering=False)
    from contextlib import ExitStack
    es = ExitStack()

    def din(name, shape, d=F32):
        return nc.dram_tensor(name, shape, d, kind="ExternalInput").ap()

    x_d = din("x", [SEQ, DM])
    win_d = din("w_in", [DM, 3584])
    wout_d = din("w_out", [DM, DM])
    ng_d = din("ng", [128, 8])
    gn_d = din("gn", [128, 4])
    fg_d = din("fg", [128, DM])
    ident_d = din("ident", [128, 128], BF16)
    permm_d = din("permm", [128, 128], BF16)
    amask_d = din("amask", [128, 1024], BF16)
    rmask_d = din("rmask", [128, 512], BF16)
    ropeA_d = din("ropeA", [2, 128, SEQ])
    ropeR_d = din("ropeR", [8, 128, SEQ])
    gck_d = din("gck", [128, 256])
    gsc_d = din("gsc", [128, 2])
    out_d = nc.dram_tensor("out", [SEQ, DM], F32, kind="ExternalOutput").ap()
    mix_d = nc.dram_tensor("mixs", [8, 128, SEQ], BF16, kind="Internal").ap()
    grs_d = nc.dram_tensor("grs", [4, 128, SEQ], BF16, kind="Internal").ap()
    dn_d = nc.dram_tensor("dns", [2, SEQ], F32, kind="Internal").ap()
    rn_d = nc.dram_tensor("rns", [2, SEQ], F32, kind="Internal").ap()
    win_v = win_d.rearrange("(c p) f -> p c f", p=128)
    wout_v = wout_d.rearrange("(c p) f -> p c f", p=128)

    ARENA_BYTES = 211968
    arena = es.enter_context(nc.sbuf_tensor("arena", [128, ARENA_BYTES // 4], F32))
    psum = es.enter_context(nc.psum_tensor("psum", [128, 4096], F32))
    sems = {e: [es.enter_context(nc.semaphore(f"s_{e}{i}")) for i in range(n)]
            for e, n in (("pe", 12), ("act", 6), ("dve", 12), ("pool", 4))}
    dsem = {q: [es.enter_context(nc.semaphore(f"d_{q}{i}")) for i in range(n)] for q, n in (("sp", 28), ("pool", 8))}
    S = Sched(nc)

    class Mem:
        def __init__(self, base):
            self.off = base

        def f32(self, n):
            o = self.off
            self.off += 4 * n
            assert self.off <= ARENA_BYTES, self.off
            return arena[:, o // 4:o // 4 + n]

        def bf(self, n):
            o = self.off
            self.off += 2 * n
            assert self.off % 4 == 0 and self.off <= ARENA_BYTES, self.off
            return arena[:, o // 4:o // 4 + n // 2].bitcast(BF16)

    def bank(i, n=512):
        return psum[:, 512 * i:512 * i + n]

    def bankbf(i):
        return psum[:, 512 * i:512 * i + 512].bitcast(BF16)

    dump_list = []

    def dump(name, ap, shape, d, keys):
        if name not in dumps:
            return
        t = nc.dram_tensor("dbg_" + name, shape, d, kind="ExternalOutput").ap()
        S.op("sp", lambda e, t=t, ap=ap: e.dma_start(out=t, in_=ap), r=keys, dma=True)
        dump_list.append(name)

    M = Mem(0)
    hT = M.bf(8 * SEQ).rearrange("p (c t) -> p c t", c=8)
    ident = M.bf(128)
    permm = M.bf(128)
    amask = M.bf(1024)
    rmask = M.bf(512)
    gck = M.f32(256)
    ng = M.f32(8)
    gn = M.f32(4)
    gsc = M.f32(2)
    st_ss = M.f32(32)
    st_ms = M.f32(32)
    st_rs = M.f32(32)
    state = M.f32(256)
    state_bf = M.bf(256)
    bnst = M.f32(24)
    bnmv = M.f32(8)
    rstd4 = M.f32(4)
    mhalf = M.f32(4)
    dtile = [M.f32(32) for _ in range(2)]
    PBASE = (M.off + 63) // 64 * 64

    S.op("sp", lambda e: e.dma_start(out=ident, in_=ident_d), w=["ident"], dma=True)
    S.op("sp", lambda e: e.dma_start(out=permm, in_=permm_d), w=["permm"], dma=True)
    S.op("sp", lambda e: e.dma_start(out=amask, in_=amask_d), w=["amask"], dma=True)
    S.op("sp", lambda e: e.dma_start(out=rmask, in_=rmask_d), w=["rmask"], dma=True)
    S.op("sp", lambda e: e.dma_start(out=gck, in_=gck_d), w=["gck"], dma=True)
    S.op("sp", lambda e: e.dma_start(out=ng, in_=ng_d), w=["ng"], dma=True)
    S.op("sp", lambda e: e.dma_start(out=gn, in_=gn_d), w=["gn"], dma=True)
    S.op("sp", lambda e: e.dma_start(out=gsc, in_=gsc_d), w=["gsc"], dma=True)
    S.op("dve", lambda e: e.memset(st_ss, 0.0), w=["st_ss"])
    S.op("dve", lambda e: e.memset(mhalf, -0.5), w=["mhalf"])

    M = Mem(PBASE)
    xs = [M.f32(DM) for _ in range(8)]
    junks = [M.bf(DM) for _ in range(4)]
    xn = [M.bf(DM) for _ in range(4)]
    def pro_a(tt):
        sl = tt % 8
        S.op("sp", lambda e: e.dma_start(out=xs[sl], in_=x_d[128 * tt:128 * tt + 128, :]), w=[("xs", sl)], dma=True)
        S.op("act", lambda e: e.activation(out=junks[tt % 4], in_=xs[sl], func=AF.Square, accum_out=st_ss[:, tt:tt + 1]),
             r=[("xs", sl), "st_ss"], w=[("junk", tt % 4), ("ss", tt)])
        S.op("dve", lambda e: e.tensor_scalar(out=st_ms[:, tt:tt + 1], in0=st_ss[:, tt:tt + 1], scalar1=1.0 / DM, scalar2=1e-6,
                                              op0=ALU.mult, op1=ALU.add), r=[("ss", tt)], w=[("ms", tt)])
        S.op("pool", lambda e: e.tensor_tensor(out=st_rs[:, tt:tt + 1], in0=st_ms[:, tt:tt + 1], in1=mhalf[:, 0:1], op=ALU.pow),
             r=[("ms", tt), "mhalf"], w=[("rs", tt)])

    def pro_b(tt):
        xsl = tt % 8
        sl = tt % 4
        pb = tt % 4
        S.op("act", lambda e: e.activation(out=xn[sl], in_=xs[xsl], func=AF.Copy, scale=st_rs[:, tt:tt + 1]),
             r=[("xs", xsl), ("rs", tt)], w=[("xn", sl)])
        for c in range(8):
            S.op("pe", lambda e, c=c: e.transpose(out=bankbf(pb)[:, 128 * c:128 * c + 128], in_=xn[sl][:, 128 * c:128 * c + 128], identity=ident),
                 r=[("xn", sl), "ident"], w=[("ps", pb)])
        S.op("dve", lambda e: e.tensor_copy(out=hT[:, :, 128 * tt:128 * tt + 128], in_=bankbf(pb).rearrange("p (c t) -> p c t", c=8)),
             w=[("ps", pb), ("hT", tt // 4)])

    for step in range(32 + 2):
        if step < 32:
            pro_a(step)
        if step >= 2:
            pro_b(step - 2)
    dump("hT", hT, [128, 8, SEQ], BF16, [("hT", i) for i in range(8)])
    if stop_after == "prologue":
        S.emit(sems, dsem)
        return nc, dump_list

    S.barrier()

    cnt = {"w": 0, "rope": 0, "pb": 0, "vt": 0, "sg": 0, "pt": 0, "ob": 0}

    def sub(ap, extra):
        return bass.AP(ap.tensor, ap.offset + extra[0], [list(ap.ap[0])] + [list(x) for x in extra[1]])

    def load_w(wst, wbf, col0, ncols, tag, src, scale=True):
        for j in range(ncols // 128):
            sl = cnt["w"] % len(wst)
            cnt["w"] += 1
            S.op("sp", lambda e, sl=sl, j=j: e.dma_start(out=wst[sl], in_=src[:, :, col0 + 128 * j:col0 + 128 * j + 128]),
                 w=[("wst", sl)], dma=True)
            if scale:
                S.op("pool", lambda e, sl=sl, j=j: e.tensor_tensor(out=wbf[:, :, 128 * j:128 * j + 128], in0=wst[sl],
                                                                    in1=ng.unsqueeze(2).to_broadcast([128, 8, 128]), op=ALU.mult),
                     r=[("wst", sl), "ng"], w=[(tag, j)])
            else:
                S.op("pool", lambda e, sl=sl, j=j: e.tensor_copy(out=wbf[:, :, 128 * j:128 * j + 128], in_=wst[sl]),
                     r=[("wst", sl)], w=[(tag, j)])

    def rope(pb, dst, dstkey, cos, sin, ckeys, qraw, t1, t2):
        sl = cnt["rope"] % 2
        cnt["rope"] += 1
        rb = 4 + sl
        aeng = "pool" if (cnt["rope"] // 2) % 2 == 0 else "dve"
        S.op("act", lambda e: e.activation(out=qraw[sl], in_=bank(pb), func=AF.Copy), w=[("ps", pb), ("qraw", sl)])
        S.op("dve", lambda e: e.tensor_tensor(out=t1[sl], in0=bank(pb), in1=cos, op=ALU.mult), r=[ckeys[0]], w=[("ps", pb), ("t1", sl)])

        def stage_b():
            S.op("pe", lambda e: e.matmul(bank(rb), lhsT=permm, rhs=qraw[sl], start=True, stop=True), r=["permm", ("qraw", sl)], w=[("ps", rb)])
            S.op("dve", lambda e: e.tensor_tensor(out=t2[sl], in0=bank(rb), in1=sin, op=ALU.mult), r=[ckeys[1]], w=[("ps", rb), ("t2", sl)])
            S.op(aeng, lambda e: e.tensor_tensor(out=dst, in0=t1[sl], in1=t2[sl], op=ALU.add), r=[("t1", sl), ("t2", sl)], w=[dstkey])
        return stage_b

    pend = []

    def flush():
        while pend:
            pend.pop(0)()

    M = Mem(PBASE)
    wst = [M.f32(1024).rearrange("p (c f) -> p c f", c=8) for _ in range(1)]
    wbf = M.bf(8 * 512).rearrange("p (c f) -> p c f", c=8)
    cosT = [M.f32(512) for _ in range(2)]
    sinT = [M.f32(512) for _ in range(2)]
    qraw = [M.bf(512) for _ in range(2)]
    t1 = [M.f32(512) for _ in range(2)]
    t2 = [M.f32(512) for _ in range(2)]
    qT = M.bf(SEQ)
    kT = M.bf(SEQ)
    vT = M.bf(SEQ)
    gTs = [M.bf(SEQ) for _ in range(2)]
    Vtok = [M.bf(32 * 192) for _ in range(2)]
    PT = [M.bf(1024) for _ in range(4)]
    accs = [M.f32(SEQ), M.f32(SEQ)]
    PHASEA_END = M.off

    for vs in range(2):
        S.op("pool", lambda e, vs=vs: e.memset(sub(Vtok[vs], (64, [[192, 32], [1, 64]])), 1.0), w=[("Vtok", vs)])

    def acc_view(acc, b, g):
        if b == 0:
            return acc[:, 512 * g:512 * g + 512], [("acc", g)], None
        if b == 1:
            r4, m0 = g // 2, 4 * (g % 2)
            s = r4 + 512 * m0
            return acc[:, s:s + 2045:4], [("acc", m0 + i) for i in range(4)], None
        return sub(acc, (2 * g, [[1, 2], [16, 256]])), [("acc", i) for i in range(8)], "p (r l) -> p r l"

    load_w(wst, wbf, 0, 512, "wbf", win_v)
    tails = []
    for p in range(4):
        gT = gTs[p % 2]
        gkey = "gT%d" % (p % 2)
        for tt in range(8):
            tsl = tt % 2
            S.op("sp", lambda e, tt=tt, tsl=tsl: e.dma_start(out=cosT[tsl], in_=ropeA_d[0, :, 512 * tt:512 * tt + 512]), w=[("cos", tsl)], dma=True)
            S.op("sp", lambda e, tt=tt, tsl=tsl: e.dma_start(out=sinT[tsl], in_=ropeA_d[1, :, 512 * tt:512 * tt + 512]), w=[("sin", tsl)], dma=True)
            for f in range(4):
                pb = cnt["pb"] % 4
                cnt["pb"] += 1
                for c in range(8):
                    S.op("pe", lambda e, c=c, f=f, tt=tt, pb=pb: e.matmul(bank(pb), lhsT=wbf[:, c, 128 * f:128 * f + 128],
                                                                          rhs=hT[:, c, 512 * tt:512 * tt + 512], start=(c == 0), stop=(c == 7)),
                         r=[("wbf", f), ("hT", tt)], w=[("ps", pb)])
                flush()
                tsel = slice(512 * tt, 512 * tt + 512)
                if f < 2:
                    pend.append(rope(pb, (qT if f == 0 else kT)[:, tsel], ("qT" if f == 0 else "kT", tt), cosT[tsl], sinT[tsl],
                                     [("cos", tsl), ("sin", tsl)], qraw, t1, t2))
                elif f == 2:
                    S.op("dve", lambda e, pb=pb, tsel=tsel: e.tensor_copy(out=vT[:, tsel], in_=bank(pb)), w=[("ps", pb), ("vT", tt)])
                else:
                    S.op("act", lambda e, pb=pb, tsel=tsel, gT=gT: e.activation(out=gT[:, tsel], in_=bank(pb), func=AF.Silu), w=[("ps", pb), (gkey, tt)])
            if tt == 2:
                while tails:
                    tails.pop(0)()
        flush()
        if p == 0:
            dump("qT0", qT, [128, SEQ], BF16, [("qT", i) for i in range(8)])
            dump("kT0", kT, [128, SEQ], BF16, [("kT", i) for i in range(8)])
            dump("vT0", vT, [128, SEQ], BF16, [("vT", i) for i in range(8)])
            dump("gT0", gT, [128, SEQ], BF16, [(gkey, i) for i in range(8)])
            if stop_after == "proj0":
                S.emit(sems, dsem)
                return nc, dump_list
        if p < 3:
            load_w(wst, wbf, 512 * (p + 1), 512, "wbf", win_v)

        def vtrans_ops(b, vs):
            ops = []
            for g8 in range(4):
                tb = 7
                grp = []
                for j in range(8):
                    n = 8 * g8 + j
                    grp.append(("pe", lambda e, j=j, n=n, tb=tb, b=b: e.transpose(out=bankbf(tb)[:, 128 * j:128 * j + 128], in_=_tok(vT, b, n), identity=ident),
                                _tok_keys("vT", b, n) + ["ident"], [("ps", tb)]))
                grp.append(("dve", lambda e, g8=g8, tb=tb, vs=vs: e.tensor_copy(out=sub(Vtok[vs], (192 * 8 * g8, [[192, 8], [128, 2], [1, 64]])),
                                                                          in_=bankbf(tb).rearrange("p (n h d) -> p n h d", n=8, h=2)),
                            [], [("ps", tb), ("Vtok", vs)]))
                ops.append(grp)
            return ops

        def emit_ops(lst):
            for (eng, fn, r, w) in lst:
                S.op(eng, fn, r=r, w=w)

        groups = []
        for b in range(3):
            vs = b % 2
            for h in range(2):
                for g in range(8):
                    groups.append((b, h, g, vs))
        vt_pending = {}
        emit_ops([o for grp in vtrans_ops(0, 0) for o in grp])

        def s_ops(i):
            b, h, g, vs = groups[i]
            r0 = 64 * h
            sg = i % 3
            Sps = psum[:, 1024 * sg:1024 * sg + 1024]
            for j in range(4):
                kb = 4 * g + j
                last = (kb % BPC[b] == BPC[b] - 1)
                sb_ = ("ps", 2 * sg + j // 2)
                if not last:
                    S.op("pe", lambda e, j=j, kb=kb, Sps=Sps, b=b, r0=r0: e.matmul(Sps[:, 256 * j:256 * j + 256], lhsT=_tok(kT[r0:r0 + 64], b, kb),
                                                                            rhs=_tok2(qT[r0:r0 + 64], b, kb, 2), start=True, stop=True),
                         r=_tok_keys("kT", b, kb) + _tok_keys("qT", b, kb) + _tok_keys("qT", b, kb + 1), w=[sb_])
                else:
                    S.op("pe", lambda e, j=j, kb=kb, Sps=Sps, b=b, r0=r0: e.matmul(Sps[:, 256 * j:256 * j + 128], lhsT=_tok(kT[r0:r0 + 64], b, kb),
                                                                            rhs=_tok(qT[r0:r0 + 64], b, kb), start=True, stop=True),
                         r=_tok_keys("kT", b, kb) + _tok_keys("qT", b, kb), w=[sb_])

        def ew_ops(i):
            b, h, g, vs = groups[i]
            sg = i % 3
            pt = i % 4
            Sps = psum[:, 1024 * sg:1024 * sg + 1024]
            S.op("act", lambda e, Sps=Sps, pt=pt: e.activation(out=PT[pt], in_=Sps, func=AF.Exp, scale=0.125),
                 w=[("ps", 2 * sg), ("ps", 2 * sg + 1), ("PT", pt)])
            meng = "dve"
            S.op(meng, lambda e, pt=pt: e.tensor_tensor(out=PT[pt], in0=PT[pt], in1=amask, op=ALU.mult), r=["amask", ("PT", pt)], w=[("PT", pt)])

        def pv_ops(i):
            b, h, g, vs = groups[i]
            vc0 = 0 if h == 0 else 64
            pt = i % 4
            ob = 6
            acc = accs[h]
            n0 = 4 * g
            started = False

            def vaug(n):
                return Vtok[vs][:, 192 * n + vc0:192 * n + vc0 + 128]

            def mm(o0, ncols, n, rhs, keys, stop):
                nonlocal started
                st = not started
                started = True
                S.op("pe", lambda e: e.matmul(bank(ob)[:, o0:o0 + ncols], lhsT=vaug(n), rhs=rhs, start=st, stop=stop, skip_group_check=True),
                     r=keys + [("Vtok", vs)], w=[("ps", ob)])

            if n0 % BPC[b] != 0:
                ptp = (i - 1) % 4
                mm(0, 128, n0 - 1, PT[ptp][:, 896:1024], [("PT", ptp)], False)
            for j in range(4):
                kb = n0 + j
                last = (kb % BPC[b] == BPC[b] - 1)
                if (not last) and j < 3:
                    mm(128 * j, 256, kb, PT[pt][:, 256 * j:256 * j + 256], [("PT", pt)], False)
                else:
                    mm(128 * j, 128, kb, PT[pt][:, 256 * j:256 * j + 128], [("PT", pt)], j == 3)
            view, akeys, rr = acc_view(acc, b, g)
            akeys = [(k[0] + str(h), k[1]) for k in akeys]
            src = bank(ob) if rr is None else bank(ob).rearrange(rr, r=2)
            if b == 0:
                S.op("act", lambda e, view=view, src=src: e.activation(out=view, in_=src, func=AF.Copy), w=[("ps", ob)] + akeys)
            else:
                S.op("dve", lambda e, view=view, src=src: e.tensor_tensor(out=view, in0=src, in1=view, op=ALU.add),
                     r=akeys, w=[("ps", ob)] + akeys)

        def den_ops(h):
            drow = 64 if h == 0 else 0
            ak = [("acc" + str(h), i) for i in range(8)]
            S.op("pool", lambda e, h=h, drow=drow: e.dma_start(out=dn_d[h:h + 1, :], in_=accs[h][drow:drow + 1, :]), r=ak, w=[("dn", h)], dma=True)
            S.op("pool", lambda e, h=h: e.dma_start(out=dtile[h], in_=dn_d[h].rearrange("(p j) -> p j", j=32)), r=[("dn", h)], w=[("dt", h)], dma=True)
            S.op("dve", lambda e, h=h: e.reciprocal(out=dtile[h], in_=dtile[h]), r=[("dt", h)], w=[("dt", h)])
            S.op("pool", lambda e, h=h: e.dma_start(out=rn_d[h].rearrange("(p j) -> p j", j=32), in_=dtile[h]), r=[("dt", h)], w=[("rn", h)], dma=True)

        NG = len(groups)
        LOOK = 3
        for i in range(min(LOOK, NG)):
            s_ops(i)
        for i in range(min(LOOK - 1, NG)):
            ew_ops(i)
        for i in range(NG):
            b, h, g, vs = groups[i]
            if h == 1 and b < 2 and g % 2 == 0:
                if g == 0:
                    vt_pending[b + 1] = vtrans_ops(b + 1, (b + 1) % 2)
                emit_ops(vt_pending[b + 1][g // 2])
            pv_ops(i)
            if i + LOOK < NG:
                s_ops(i + LOOK)
            if i + LOOK - 1 < NG:
                ew_ops(i + LOOK - 1)
            if b == 2 and g == 7:
                den_ops(h)
        if p == 0:
            dump("acc0", accs[0], [128, SEQ], F32, [("acc0", i) for i in range(8)])
            dump("acc1", accs[1], [128, SEQ], F32, [("acc1", i) for i in range(8)])
        for h in range(2):
            o = 1 - h
            rows = slice(0, 64) if h == 0 else slice(64, 128)
            ok = [("acc" + str(o), i) for i in range(8)]
            S.op("pool", lambda e, h=h, o=o, rows=rows: e.dma_start(out=accs[o][rows, :], in_=rn_d[h:h + 1, :].to_broadcast([64, SEQ])),
                 r=[("rn", h)], w=ok, dma=True)
        def make_tail(p, gT, gkey):
            def tail():
                for half in range(2):
                    cs = slice(2048 * half, 2048 * half + 2048)
                    k0 = [("acc0", 4 * half + i) for i in range(4)]
                    k1 = [("acc1", 4 * half + i) for i in range(4)]
                    gk = [(gkey, 4 * half + i) for i in range(4)]
                    S.op("dve", lambda e, cs=cs: e.tensor_tensor(out=accs[0][:, cs], in0=accs[0][:, cs], in1=accs[1][:, cs], op=ALU.mult), r=k0 + k1, w=k0)
                    S.op("dve", lambda e, cs=cs: e.tensor_tensor(out=gT[:, cs], in0=accs[0][:, cs], in1=gT[:, cs], op=ALU.mult), r=k0 + gk, w=gk)
                    S.op("sp", lambda e, cs=cs: e.dma_start(out=mix_d[p][:, cs], in_=gT[:, cs]), r=gk, w=[("mixd", p)], dma=True)
            return tail

        tails.append(make_tail(p, gT, gkey))
        if p == 3 or stop_after == "attn0":
            while tails:
                tails.pop(0)()
        if p == 0:
            dump("mixT0", gT, [128, SEQ], BF16, [(gkey, i) for i in range(8)])
            if stop_after == "attn0":
                S.emit(sems, dsem)
                return nc, dump_list
    S.barrier()
    for pp in range(4):
        dump("mixd%d" % pp, mix_d[pp], [128, SEQ], BF16, [("mixd", pp)])
    if stop_after == "phaseA":
        S.emit(sems, dsem)
        return nc, dump_list
    M = Mem(PBASE)
    wstB = [M.f32(1024).rearrange("p (c f) -> p c f", c=8) for _ in range(2)]
    wbfB = M.bf(8 * 512).rearrange("p (c f) -> p c f", c=8)
    TB = M.off
    tabs = [[M.f32(512) for _ in range(2)] for _ in range(4)]
    qrawB = [M.bf(512) for _ in range(2)]
    t1B = [M.f32(512) for _ in range(2)]
    t2B = [M.f32(512) for _ in range(2)]
    grt = [M.bf(4 * 512).rearrange("p (f t) -> p f t", f=4) for _ in range(2)]
    TB_END = M.off
    qrT = [M.bf(SEQ) for _ in range(2)]
    krT = [M.bf(SEQ) for _ in range(2)]
    Vr = M.bf(32 * 512).rearrange("p (n v) -> p n v", n=32)
    WO_OFF = M.off
    wo = M.bf(8 * DM).rearrange("p (c f) -> p c f", c=8)

    load_w(wstB, wbfB, 2048, 512, "wbfB", win_v)
    for tt in range(8):
        for ft in range(2):
            tsl = (2 * tt + ft) % 2
            for k4 in range(4):
                S.op("sp", lambda e, tt=tt, ft=ft, k4=k4, tsl=tsl: e.dma_start(out=tabs[k4][tsl], in_=ropeR_d[4 * ft + k4, :, 512 * tt:512 * tt + 512]),
                     w=[("tab", k4, tsl)], dma=True)
            for qk in range(2):
                f = 2 * qk + ft
                pb = cnt["pb"] % 4
                cnt["pb"] += 1
                for c in range(8):
                    S.op("pe", lambda e, c=c, f=f, tt=tt, pb=pb: e.matmul(bank(pb), lhsT=wbfB[:, c, 128 * f:128 * f + 128],
                                                                          rhs=hT[:, c, 512 * tt:512 * tt + 512], start=(c == 0), stop=(c == 7)),
                         r=[("wbfB", f), ("hT", tt)], w=[("ps", pb)])
                flush()
                dst = (qrT if qk == 0 else krT)[ft][:, 512 * tt:512 * tt + 512]
                pend.append(rope(pb, dst, ("qrT" if qk == 0 else "krT", ft, tt), tabs[2 * qk][tsl], tabs[2 * qk + 1][tsl],
                                 [("tab", 2 * qk, tsl), ("tab", 2 * qk + 1, tsl)], qrawB, t1B, t2B))
    flush()
    dump("qrT0", qrT[0], [128, SEQ], BF16, [("qrT", 0, i) for i in range(8)])
    dump("krT0", krT[0], [128, SEQ], BF16, [("krT", 0, i) for i in range(8)])
    if stop_after == "B1":
        S.emit(sems, dsem)
        return nc, dump_list
    load_w(wstB, wbfB, 2560, 512, "wbfB", win_v)
    for n in range(32):
        pb = cnt["pb"] % 4
        cnt["pb"] += 1
        for c in range(8):
            S.op("pe", lambda e, c=c, n=n, pb=pb: e.matmul(bank(pb), lhsT=hT[:, c, 128 * n:128 * n + 128], rhs=wbfB[:, c, :], start=(c == 0), stop=(c == 7)),
                 r=[("wbfB", i) for i in range(4)] + [("hT", n // 4)], w=[("ps", pb)])
        if n % 2 == 0:
            S.op("dve", lambda e, n=n, pb=pb: e.tensor_copy(out=Vr[:, n, :], in_=bank(pb)), w=[("ps", pb), ("Vr", n)])
        else:
            S.op("act", lambda e, n=n, pb=pb: e.activation(out=Vr[:, n, :], in_=bank(pb), func=AF.Copy), w=[("ps", pb), ("Vr", n)])
    load_w(wstB, wbfB, 3072, 512, "wbfB", win_v)
    for tt in range(8):
        gs = tt % 2
        for ft in range(4):
            pb = cnt["pb"] % 4
            cnt["pb"] += 1
            for c in range(8):
                S.op("pe", lambda e, c=c, ft=ft, tt=tt, pb=pb: e.matmul(bank(pb), lhsT=wbfB[:, c, 128 * ft:128 * ft + 128],
                                                                        rhs=hT[:, c, 512 * tt:512 * tt + 512], start=(c == 0), stop=(c == 7)),
                     r=[("wbfB", ft), ("hT", tt)], w=[("ps", pb)])
            S.op("act", lambda e, ft=ft, pb=pb, gs=gs: e.activation(out=grt[gs][:, ft, :], in_=bank(pb), func=AF.Silu), w=[("ps", pb), ("grt", gs, ft)])
            S.op("dve", lambda e, ft=ft, gs=gs: e.tensor_scalar(out=grt[gs][:, ft, :], in0=grt[gs][:, ft, :], scalar1=gn[:, ft:ft + 1], scalar2=None, op0=ALU.mult),
                 r=[("grt", gs, ft), "gn"], w=[("grt", gs, ft)])
        S.op("sp", lambda e, tt=tt, gs=gs: e.dma_start(out=grs_d[:, :, 512 * tt:512 * tt + 512].rearrange("f p t -> p f t"), in_=grt[gs]),
             r=[("grt", gs, ft) for ft in range(4)], w=[("grs", tt)], dma=True)
    if stop_after == "B3":
        S.emit(sems, dsem)
        return nc, dump_list
    S.barrier()

    M = Mem(TB)
    Ktok = [M.bf(256) for _ in range(2)]
    sT = [M.bf(512) for _ in range(2)]
    onb = [M.bf(512) for _ in range(2)]
    grl = [M.bf(4 * 512).rearrange("p (f t) -> p f t", f=4) for _ in range(2)]
    mixst = [M.bf(4 * 512).rearrange("p (f t) -> p f t", f=4) for _ in range(2)]
    assert M.off <= TB_END - 256
    mixA = Mem(0).bf(8 * SEQ).rearrange("p (c t) -> p c t", c=8)
    def mixa_load(c):
        S.op("sp", lambda e: e.dma_start(out=mixA[:, c, :], in_=mix_d[c]), r=[("mixd", c)], w=[("mixA", c)] + [("hT", i) for i in range(8)], dma=True)
    dump("mixa0e", mixA[:, 0:4, 0:512], [128, 4, 512], BF16, [("mixA", h) for h in range(4)])
    for pp in range(4):
        dump("mixe%d" % pp, mix_d[pp], [128, SEQ], BF16, [("mixd", pp)])
    def wo_chunk(j):
        sl = cnt["w"] % len(wstB)
        cnt["w"] += 1
        S.op("sp", lambda e: e.dma_start(out=wstB[sl], in_=wout_v[:, :, 128 * j:128 * j + 128]), w=[("wst", sl)], dma=True)
        S.op("pool", lambda e: e.tensor_copy(out=wo[:, :, 128 * j:128 * j + 128], in_=wstB[sl]), r=[("wst", sl)], w=[("wo", j)])
    S.op("dve", lambda e: e.memset(state, 0.0), w=["state"])
    nbias = bnst[:, 0:4]
    bnst2 = [bnst, Mem(TB_END - 256).f32(24)]
    bnmv2 = [bnmv, Mem(TB_END - 128).f32(8)]
    rstd2 = [rstd4, Mem(TB_END - 64).f32(4)]
    nb2 = [Mem(TB_END - 32).f32(4), Mem(TB_END - 16).f32(4)]

    OB = (2, 3, 6)

    def r_1(n):
        tsel = slice(128 * n, 128 * n + 128)
        ks = n % 2
        grp, sub4 = n // 4, n % 4
        gl = grp % 2
        if sub4 == 0:
            S.op("sp", lambda e: e.dma_start(out=grl[gl], in_=grs_d[:, :, 512 * grp:512 * grp + 512].rearrange("f p t -> p f t")),
                 r=[("grs", grp)], w=[("grl", gl)], dma=True)
        for ft in range(2):
            S.op("pe", lambda e, ft=ft: e.transpose(out=bankbf(0)[:, 128 * ft:128 * ft + 128], in_=krT[ft][:, tsel], identity=ident),
                 r=[("krT", ft, n // 4), "ident"], w=[("ps", 0)])
        S.op("act", lambda e: e.activation(out=Ktok[ks], in_=bankbf(0)[:, 0:256], func=AF.Copy), w=[("ps", 0), ("Ktok", ks)])
        S.op("pool", lambda e: e.tensor_tensor(out=Ktok[ks], in0=Ktok[ks], in1=gck, op=ALU.mult), r=["gck", ("Ktok", ks)], w=[("Ktok", ks)])
        for h in (0, 2, 1, 3):
            r0 = 64 * (h % 2)
            sbk = 1 if h % 2 == 0 else 7
            S.op("pe", lambda e, h=h, r0=r0, sbk=sbk: e.matmul(bank(sbk)[:, 128 * (h // 2):128 * (h // 2) + 128], lhsT=krT[h // 2][r0:r0 + 64, tsel],
                                                          rhs=qrT[h // 2][r0:r0 + 64, tsel], start=True, stop=True),
                 r=[("krT", h // 2, n // 4), ("qrT", h // 2, n // 4)], w=[("ps", sbk)])
        for hp, sbk in ((0, 1), (1, 7)):
            S.op("dve", lambda e, hp=hp, sbk=sbk: e.tensor_tensor(out=sT[ks][:, 256 * hp:256 * hp + 256], in0=bank(sbk)[:, 0:256], in1=rmask[:, 0:256], op=ALU.mult),
                 r=["rmask"], w=[("ps", sbk), ("sT", ks)])

    def r_2(n):
        tsel = slice(128 * n, 128 * n + 128)
        ks = n % 2
        ob = OB[n % 3]
        for h in range(4):
            r0 = 64 * (h % 2)
            S.op("pe", lambda e, h=h, r0=r0: e.matmul(bank(4)[r0:r0 + 64, 128 * (h // 2):128 * (h // 2) + 128], lhsT=Ktok[ks][:, 64 * h:64 * h + 64],
                                                 rhs=Vr[:, n, 128 * h:128 * h + 128], start=True, stop=True),
                 r=[("Ktok", ks), ("Vr", n)], w=[("ps", 4)])
        for h in range(4):
            r0 = 64 * (h % 2)
            sblk = 2 * (h % 2) + h // 2
            S.op("pe", lambda e, h=h, sblk=sblk: e.matmul(bank(ob)[:, 128 * h:128 * h + 128], lhsT=sT[ks][:, 128 * sblk:128 * sblk + 128],
                                                     rhs=Vr[:, n, 128 * h:128 * h + 128], start=True, stop=(n == 0)),
                 r=[("sT", ks), ("Vr", n)], w=[("ps", ob)])
            if n > 0:
                S.op("pe", lambda e, h=h, r0=r0: e.matmul(bank(ob)[:, 128 * h:128 * h + 128], lhsT=qrT[h // 2][r0:r0 + 64, tsel],
                                                     rhs=state_bf[r0:r0 + 64, 128 * (h // 2):128 * (h // 2) + 128], start=False, stop=True),
                     r=[("qrT", h // 2, n // 4), "state_bf"], w=[("ps", ob)])
        for cb in range(2):
            S.op("dve", lambda e, cb=cb: e.scalar_tensor_tensor(out=state[:, 128 * cb:128 * cb + 128], in0=state[:, 128 * cb:128 * cb + 128],
                                                                scalar=gsc[:, cb:cb + 1], in1=bank(4)[:, 128 * cb:128 * cb + 128], op0=ALU.mult, op1=ALU.add),
                 r=["state", "gsc"], w=["state", ("ps", 4)])
        S.op("act", lambda e: e.activation(out=state_bf, in_=state, func=AF.Copy), r=["state"], w=["state_bf"])

    def r_3(n):
        ks = n % 2
        ob = OB[n % 3]
        for h in range(4):
            S.op("dve", lambda e, h=h: e.bn_stats(out=bnst2[ks][:, 6 * h:6 * h + 6], in_=bank(ob)[:, 128 * h:128 * h + 128]), w=[("ps", ob), ("bnst", ks, h)])
        for h in range(4):
            S.op("dve", lambda e, h=h: e.bn_aggr(out=bnmv2[ks][:, 2 * h:2 * h + 2], in_=bnst2[ks][:, 6 * h:6 * h + 6]), r=[("bnst", ks, h)], w=[("bnmv", ks, h)])
        S.op("pool", lambda e: e.tensor_scalar(out=rstd2[ks], in0=bnmv2[ks][:, 1:8:2], scalar1=1e-5, scalar2=None, op0=ALU.add), r=[("bnmv", ks, h) for h in range(4)], w=[("rstd", ks)])
        S.op("pool", lambda e: e.tensor_tensor(out=rstd2[ks], in0=rstd2[ks], in1=mhalf, op=ALU.pow), r=[("rstd", ks), "mhalf"], w=[("rstd", ks)])
        S.op("pool", lambda e: e.tensor_tensor(out=nb2[ks], in0=bnmv2[ks][:, 0:8:2], in1=rstd2[ks], op=ALU.mult), r=[("bnmv", ks, h) for h in range(4)] + [("rstd", ks)], w=[("nb", ks)])
        S.op("pool", lambda e: e.tensor_scalar(out=nb2[ks], in0=nb2[ks], scalar1=-1.0, scalar2=None, op0=ALU.mult), r=[("nb", ks)], w=[("nb", ks)])

    def r_4(n):
        ks = n % 2
        ob = OB[n % 3]
        for h in range(4):
            S.op("act", lambda e, h=h: e.activation(out=onb[ks][:, 128 * h:128 * h + 128], in_=bank(ob)[:, 128 * h:128 * h + 128], func=AF.Identity,
                                                    bias=nb2[ks][:, h:h + 1], scale=rstd2[ks][:, h:h + 1]),
                 r=[("nb", ks), ("rstd", ks)], w=[("ps", ob), ("onb", ks, h)])

    def r_5(n):
        ks = n % 2
        grp, sub4 = n // 4, n % 4
        gl = grp % 2
        for h in range(4):
            S.op("pe", lambda e, h=h: e.transpose(out=bankbf(5)[:, 128 * h:128 * h + 128], in_=onb[ks][:, 128 * h:128 * h + 128], identity=ident),
                 r=[("onb", ks, h), "ident"], w=[("ps", 5)])
        S.op("dve", lambda e: e.tensor_tensor(out=mixA[:, 4:8, 128 * n:128 * n + 128], in0=bankbf(5)[:, 0:512].rearrange("p (h t) -> p h t", h=4),
                                              in1=grl[gl][:, :, 128 * sub4:128 * sub4 + 128], op=ALU.mult),
             r=[("grl", gl)], w=[("ps", 5)] + [("mixA", 4 + h) for h in range(4)])

    stages = ((r_1, 0), (r_2, 1), (r_3, 2), (r_4, 3), (r_5, 4))
    for step in range(32 + 4):
        for fn, sk in stages:
            if 0 <= step - sk < 32:
                fn(step - sk)
        if step % 4 == 2 and step // 4 < 8:
            wo_chunk(step // 4)
        if step % 4 == 1 and step // 4 < 4:
            mixa_load(step // 4)
    dump("mixr0", mixA[:, 4:8, 0:512], [128, 4, 512], BF16, [("mixA", 4 + h) for h in range(4)])
    dump("mixa0", mixA[:, 0:4, 0:512], [128, 4, 512], BF16, [("mixA", h) for h in range(4)])
    dump("wo", wo, [128, 8, DM], BF16, [("wo", j) for j in range(8)])
    if stop_after == "ret":
        S.emit(sems, dsem)
        return nc, dump_list
    S.barrier()

    M = Mem(PBASE + 16384)
    xr = [M.f32(DM) for _ in range(8)]
    yb = [M.f32(DM) for _ in range(4)]
    junk2s = [M.bf(DM) for _ in range(4)]
    fg = M.f32(DM)
    assert M.off <= WO_OFF
    S.op("sp", lambda e: e.dma_start(out=fg, in_=fg_d), w=["fg"], dma=True)
    S.op("dve", lambda e: e.memset(st_ss, 0.0), w=["st_ss"])
    def c_a(tt):
        sl = tt % 4
        xsl = tt % 8
        S.op("sp", lambda e: e.dma_start(out=xr[xsl], in_=x_d[128 * tt:128 * tt + 128, :]), w=[("xr", xsl)], dma=True)
        for nh in range(2):
            pb = (2 * tt + nh) % 8
            for c in range(8):
                S.op("pe", lambda e, c=c, nh=nh, pb=pb: e.matmul(bank(pb), lhsT=mixA[:, c, 128 * tt:128 * tt + 128], rhs=wo[:, c, 512 * nh:512 * nh + 512],
                                                                 start=(c == 0), stop=(c == 7)),
                     r=[("mixA", c)] + [("wo", 4 * nh + i) for i in range(4)], w=[("ps", pb)])
            S.op("dve", lambda e, nh=nh, pb=pb: e.tensor_tensor(out=yb[sl][:, 512 * nh:512 * nh + 512], in0=bank(pb), in1=xr[xsl][:, 512 * nh:512 * nh + 512], op=ALU.add),
                 r=[("xr", xsl)], w=[("ps", pb), ("yb", sl, nh)])
        S.op("act", lambda e: e.activation(out=junk2s[tt % 4], in_=yb[sl], func=AF.Square, accum_out=st_ss[:, tt:tt + 1]),
             r=[("yb", sl, 0), ("yb", sl, 1), "st_ss"], w=[("junk2", tt % 4), ("ss", tt)])

    def c_b(tt):
        S.op("dve", lambda e: e.tensor_scalar(out=st_ms[:, tt:tt + 1], in0=st_ss[:, tt:tt + 1], scalar1=1.0 / DM, scalar2=1e-6,
                                              op0=ALU.mult, op1=ALU.add), r=[("ss", tt)], w=[("ms", tt)])
        S.op("pool", lambda e: e.tensor_tensor(out=st_rs[:, tt:tt + 1], in0=st_ms[:, tt:tt + 1], in1=mhalf[:, 0:1], op=ALU.pow),
             r=[("ms", tt), "mhalf"], w=[("rs", tt)])

    def c_c(tt):
        sl = tt % 4
        S.op("dve", lambda e: e.scalar_tensor_tensor(out=yb[sl], in0=yb[sl], scalar=st_rs[:, tt:tt + 1], in1=fg, op0=ALU.mult, op1=ALU.mult),
             r=[("yb", sl, 0), ("yb", sl, 1), ("rs", tt), "fg"], w=[("yb", sl, 0), ("yb", sl, 1)])
        S.op("pool", lambda e: e.dma_start(out=out_d[128 * tt:128 * tt + 128, :], in_=yb[sl]), r=[("yb", sl, 0), ("yb", sl, 1)], w=[("out", tt)], dma=True)

    for step in range(32 + 2):
        if step < 32:
            c_a(step)
        if 1 <= step < 33:
            c_b(step - 1)
        if step >= 2:
            c_c(step - 2)
    S.emit(sems, dsem)
    return nc, dump_list


_CACHE = {}


def _prep_inputs(x, norm_gain, w_in, ret_gn_gain, w_out, final_gain):
    c = _host_constants()
    w_in_p = np.ascontiguousarray(np.asarray(w_in, np.float32)[0][:, _win_cols()])
    shared = dict(c)
    shared["w_in"] = w_in_p
    shared["w_out"] = np.ascontiguousarray(np.asarray(w_out, np.float32)[0])
    shared["ng"] = np.ascontiguousarray(np.asarray(norm_gain, np.float32)[0].reshape(8, 128).T)
    shared["gn"] = np.ascontiguousarray(np.asarray(ret_gn_gain, np.float32)[0].reshape(4, 128).T)
    shared["fg"] = np.ascontiguousarray(np.broadcast_to(np.asarray(final_gain, np.float32)[None, :], (128, DM)))
    x = np.asarray(x, np.float32)
    return [dict(shared, x=np.ascontiguousarray(x[b])) for b in range(x.shape[0])]


def kernel(x, norm_gain, w_in, ret_gn_gain, w_out, final_gain):
    in_maps = _prep_inputs(x, norm_gain, w_in, ret_gn_gain, w_out, final_gain)
    nc, _ = build()
    res = run_bass_kernel_spmd(nc, in_maps, core_ids=list(range(NCORES)))
    return np.stack([np.asarray(r["out"], np.float32) for r in res.results], axis=0)
```

```python
import numpy as np
import ml_dtypes
import concourse.bass as bass
import concourse.mybir as mybir
from concourse.bass_utils import run_bass_kernel_spmd

dt = mybir.dt
F32 = dt.float32
BF16 = dt.bfloat16
AF = mybir.ActivationFunctionType
ALU = mybir.AluOpType
AX = mybir.AxisListType


class Sched:
    COMPUTE = ("pe", "act", "dve", "pool")
    EPOCH = 400
    STRICT = True

    def __init__(self, nc, dma_ring=None):
        self.nc = nc
        self.ops = []
        self.lastw = {}
        self.readers = {}
        self.last_on = {}
        self.extra = {}
        self.dma_live = []
        self.dma_ring = dma_ring or {"sp": 24, "pool": 12, "act": 6}

    def op(self, eng, fn, r=(), w=(), dma=False):
        i = len(self.ops)
        deps = set(self.extra.pop(eng, ()))
        for k in r:
            s = self.lastw.get(k)
            if s is not None:
                deps.add((s, 0))
        for k in w:
            s = self.lastw.get(k)
            if s is not None:
                deps.add((s, 1))
            for s in self.readers.get(k, ()):
                deps.add((s, 2))
        keep = set()
        for (s, kind) in deps:
            so = self.ops[s]
            if so["dma"] or dma:
                keep.add(s)
            elif so["eng"] == eng:
                if eng != "pe" and (kind == 0 or self.STRICT):
                    keep.add(s)
            else:
                keep.add(s)
        keep.discard(i)
        self.ops.append(dict(eng=eng, fn=fn, deps=sorted(keep), dma=dma, signal=dma))
        for s in keep:
            self.ops[s]["signal"] = True
        for k in w:
            self.lastw[k] = i
            self.readers[k] = []
        for k in r:
            lst = self.readers.setdefault(k, [])
            if not dma:
                lst[:] = [s for s in lst if self.ops[s]["dma"] or self.ops[s]["eng"] != eng]
            lst.append(i)
        self.last_on[eng] = i
        if dma:
            self.dma_live.append(i)
        return i

    def barrier(self):
        srcs = set(self.last_on.values()) | set(self.dma_live)
        self.dma_live = []
        for e in ("pe", "act", "dve", "pool", "sp"):
            self.extra.setdefault(e, set())
            for s in srcs:
                so = self.ops[s]
                if so["eng"] == e and not so["dma"]:
                    continue
                self.extra[e].add((s, 0))
                so["signal"] = True

    def emit(self, sems, dma_sems):
        nc = self.nc
        cnt = {e: 0 for e in self.COMPUTE}
        dcnt = {q: 0 for q in dma_sems}
        dval = {q: [0] * len(dma_sems[q]) for q in dma_sems}
        for o in self.ops:
            if o["dma"]:
                q = o["eng"]
                j = dcnt[q] % len(dma_sems[q])
                dcnt[q] += 1
                o["ring_prev"] = (dma_sems[q][j], dval[q][j]) if dval[q][j] > 0 else None
                dval[q][j] += 16
                o["done"] = (dma_sems[q][j], dval[q][j])
            elif o["signal"]:
                k = cnt[o["eng"]]
                cnt[o["eng"]] += 1
                o["done"] = (sems[o["eng"]][k // self.EPOCH], k % self.EPOCH + 1)
        by_eng = {}
        for o in self.ops:
            by_eng.setdefault(o["eng"], []).append(o)
        final_waits = []
        for q in dma_sems:
            for j, s in enumerate(dma_sems[q]):
                if dval[q][j] > 0:
                    final_waits.append((s, dval[q][j]))
        handles = {"pe": nc.tensor, "act": nc.scalar, "dve": nc.vector, "pool": nc.gpsimd, "sp": nc.sync}
        ops_all = self.ops

        def run(eng, e):
            waited = {}

            def wait(sem, val):
                key = id(sem)
                if waited.get(key, 0) < val:
                    e.wait_ge(sem, val)
                    waited[key] = val

            for o in by_eng.get(eng, []):
                for s in o["deps"]:
                    sem, val = ops_all[s]["done"]
                    wait(sem, val)
                if o["dma"] and o["ring_prev"] is not None:
                    wait(*o["ring_prev"])
                ins = o["fn"](e)
                if o["signal"]:
                    ins.then_inc(o["done"][0], 16 if o["dma"] else 1)
            if eng == "sp":
                for (s, v) in final_waits:
                    wait(s, v)

        with nc.Block() as block:
            @block.sync
            def _(e):
                run("sp", e)

            @block.tensor
            def _(e):
                run("pe", e)

            @block.scalar
            def _(e):
                run("act", e)

            @block.vector
            def _(e):
                run("dve", e)

            @block.gpsimd
            def _(e):
                run("pool", e)


SEQ = 4096
DM = 1024
NCORES = 8
THETA = 10000.0
_bf = ml_dtypes.bfloat16


def _host_constants():
    c = {}
    c["ident"] = np.eye(128, dtype=np.float32).astype(_bf)
    permm = np.zeros((128, 128), np.float32)
    for m in range(128):
        k = m + 32 if (m % 64) < 32 else m - 32
        permm[k, m] = 1.0
    c["permm"] = permm.astype(_bf)
    kk = np.arange(128)[:, None]
    qq = np.arange(128)[None, :]
    prev = (kk >= qq).astype(np.float32)
    own = (qq >= kk).astype(np.float32)
    c["amask"] = np.tile(np.concatenate([own, prev], 1), (1, 4)).astype(_bf)
    c["rmask"] = np.tile(own, (1, 4)).astype(_bf)
    pos = np.arange(SEQ, dtype=np.float64)
    j = np.arange(128) % 32
    sign = np.where((np.arange(128) % 64) >= 32, 1.0, -1.0)[:, None]
    inv = THETA ** (-(np.arange(32, dtype=np.float64)) / 32.0)
    ang = pos[None, :] * inv[j][:, None]
    c["ropeA"] = np.stack([np.cos(ang), np.sin(ang) * sign]).astype(np.float32)
    invr = 1.0 / (THETA ** np.linspace(0.0, 1.0, 32, dtype=np.float64))
    angr = pos[None, :] * invr[j][:, None]
    cosR = np.cos(angr)
    sinR = np.sin(angr) * sign
    lg = np.log1p(-(2.0 ** (-5.0 - np.arange(4, dtype=np.float64))))
    i_in = (np.arange(SEQ) % 128).astype(np.float64)
    ropeR = np.zeros((8, 128, SEQ), np.float32)
    for ft in range(2):
        for hp in range(2):
            h = 2 * ft + hp
            rows = slice(64 * hp, 64 * hp + 64)
            dq = np.exp((i_in + 1.0) * lg[h])[None, :]
            dk = (np.exp(-(i_in + 1.0) * lg[h]) / 8.0)[None, :]
            ropeR[ft * 4 + 0, rows] = cosR[rows] * dq
            ropeR[ft * 4 + 1, rows] = sinR[rows] * dq
            ropeR[ft * 4 + 2, rows] = cosR[rows] * dk
            ropeR[ft * 4 + 3, rows] = sinR[rows] * dk
    c["ropeR"] = ropeR
    gC = np.exp(128.0 * lg)
    gck = np.zeros((128, 256), np.float32)
    for h in range(4):
        gck[:, h * 64:(h + 1) * 64] = gC[h]
    c["gck"] = gck
    gsc = np.zeros((128, 2), np.float32)
    for cb in range(2):
        gsc[:64, cb] = gC[2 * cb]
        gsc[64:, cb] = gC[2 * cb + 1]
    c["gsc"] = gsc
    return c


def _win_cols():
    cols = []
    for p in range(4):
        for base in (0, 512, 1024, 1536):
            cols += list(range(base + 128 * p, base + 128 * p + 128))
    permh = [2 * i for i in range(32)] + [2 * i + 1 for i in range(32)]
    for base in (2048, 2304):
        for h in range(4):
            cols += [base + 64 * h + cc for cc in permh]
    cols += list(range(2560, 3584))
    return np.asarray(cols)


def _tok(buf, b, n):
    if b == 0:
        return buf[:, 128 * n:128 * n + 128]
    if b == 1:
        r, m = n // 8, n % 8
        s = r + 512 * m
        return buf[:, s:s + 509:4]
    r, m = n // 2, n % 2
    s = r + 2048 * m
    return buf[:, s:s + 2033:16]


def _tok2(buf, b, n, nblk):
    cntk = 128 * nblk
    if b == 0:
        return buf[:, 128 * n:128 * n + cntk]
    if b == 1:
        r, m = n // 8, n % 8
        s = r + 512 * m
        return buf[:, s:s + 4 * (cntk - 1) + 1:4]
    r, m = n // 2, n % 2
    s = r + 2048 * m
    return buf[:, s:s + 16 * (cntk - 1) + 1:16]


def _tok_keys(name, b, n):
    if b == 0:
        return [(name, n // 4)]
    if b == 1:
        return [(name, n % 8)]
    return [(name, 4 * (n % 2) + i) for i in range(4)]


BPC = (32, 8, 2)


def build(stop_after=None, dumps=()):
    nc = bass.Bass("TRN2", target_bir_lowering=False)
    from contextlib import ExitStack
    es = ExitStack()

    def din(name, shape, d=F32):
        return nc.dram_tensor(name, shape, d, kind="ExternalInput").ap()

    x_d = din("x", [SEQ, DM])
    win_d = din("w_in", [DM, 3584])
    wout_d = din("w_out", [DM, DM])
    ng_d = din("ng", [128, 8])
    gn_d = din("gn", [128, 4])
    fg_d = din("fg", [128, DM])
    ident_d = din("ident", [128, 128], BF16)
    permm_d = din("permm", [128, 128], BF16)
    amask_d = din("amask", [128, 1024], BF16)
    rmask_d = din("rmask", [128, 512], BF16)
    ropeA_d = din("ropeA", [2, 128, SEQ])
    ropeR_d = din("ropeR", [8, 128, SEQ])
    gck_d = din("gck", [128, 256])
    gsc_d = din("gsc", [128, 2])
    out_d = nc.dram_tensor("out", [SEQ, DM], F32, kind="ExternalOutput").ap()
    mix_d = nc.dram_tensor("mixs", [8, 128, SEQ], BF16, kind="Internal").ap()
    grs_d = nc.dram_tensor("grs", [4, 128, SEQ], BF16, kind="Internal").ap()
    dn_d = nc.dram_tensor("dns", [2, SEQ], F32, kind="Internal").ap()
    rn_d = nc.dram_tensor("rns", [2, SEQ], F32, kind="Internal").ap()
    win_v = win_d.rearrange("(c p) f -> p c f", p=128)
    wout_v = wout_d.rearrange("(c p) f -> p c f", p=128)

    ARENA_BYTES = 211968
    arena = es.enter_context(nc.sbuf_tensor("arena", [128, ARENA_BYTES // 4], F32))
    psum = es.enter_context(nc.psum_tensor("psum", [128, 4096], F32))
    sems = {e: [es.enter_context(nc.semaphore(f"s_{e}{i}")) for i in range(n)]
            for e, n in (("pe", 12), ("act", 6), ("dve", 12), ("pool", 4))}
    dsem = {q: [es.enter_context(nc.semaphore(f"d_{q}{i}")) for i in range(n)] for q, n in (("sp", 28), ("pool", 8))}
    S = Sched(nc)

    class Mem:
        def __init__(self, base):
            self.off = base

        def f32(self, n):
            o = self.off
            self.off += 4 * n
            assert self.off <= ARENA_BYTES, self.off
            return arena[:, o // 4:o // 4 + n]

        def bf(self, n):
            o = self.off
            self.off += 2 * n
            assert self.off % 4 == 0 and self.off <= ARENA_BYTES, self.off
            return arena[:, o // 4:o // 4 + n // 2].bitcast(BF16)

    def bank(i, n=512):
        return psum[:, 512 * i:512 * i + n]

    def bankbf(i):
        return psum[:, 512 * i:512 * i + 512].bitcast(BF16)

    dump_list = []

    def dump(name, ap, shape, d, keys):
        if name not in dumps:
            return
        t = nc.dram_tensor("dbg_" + name, shape, d, kind="ExternalOutput").ap()
        S.op("sp", lambda e, t=t, ap=ap: e.dma_start(out=t, in_=ap), r=keys, dma=True)
        dump_list.append(name)

    M = Mem(0)
    hT = M.bf(8 * SEQ).rearrange("p (c t) -> p c t", c=8)
    ident = M.bf(128)
    permm = M.bf(128)
    amask = M.bf(1024)
    rmask = M.bf(512)
    gck = M.f32(256)
    ng = M.f32(8)
    gn = M.f32(4)
    gsc = M.f32(2)
    st_ss = M.f32(32)
    st_ms = M.f32(32)
    st_rs = M.f32(32)
    state = M.f32(256)
    state_bf = M.bf(256)
    bnst = M.f32(24)
    bnmv = M.f32(8)
    rstd4 = M.f32(4)
    mhalf = M.f32(4)
    dtile = [M.f32(32) for _ in range(2)]
    PBASE = (M.off + 63) // 64 * 64

    S.op("sp", lambda e: e.dma_start(out=ident, in_=ident_d), w=["ident"], dma=True)
    S.op("sp", lambda e: e.dma_start(out=permm, in_=permm_d), w=["permm"], dma=True)
    S.op("sp", lambda e: e.dma_start(out=amask, in_=amask_d), w=["amask"], dma=True)
    S.op("sp", lambda e: e.dma_start(out=rmask, in_=rmask_d), w=["rmask"], dma=True)
    S.op("sp", lambda e: e.dma_start(out=gck, in_=gck_d), w=["gck"], dma=True)
    S.op("sp", lambda e: e.dma_start(out=ng, in_=ng_d), w=["ng"], dma=True)
    S.op("sp", lambda e: e.dma_start(out=gn, in_=gn_d), w=["gn"], dma=True)
    S.op("sp", lambda e: e.dma_start(out=gsc, in_=gsc_d), w=["gsc"], dma=True)
    S.op("dve", lambda e: e.memset(st_ss, 0.0), w=["st_ss"])
    S.op("dve", lambda e: e.memset(mhalf, -0.5), w=["mhalf"])

    M = Mem(PBASE)
    xs = [M.f32(DM) for _ in range(8)]
    junks = [M.bf(DM) for _ in range(4)]
    xn = [M.bf(DM) for _ in range(4)]
    def pro_a(tt):
        sl = tt % 8
        S.op("sp", lambda e: e.dma_start(out=xs[sl], in_=x_d[128 * tt:128 * tt + 128, :]), w=[("xs", sl)], dma=True)
        S.op("act", lambda e: e.activation(out=junks[tt % 4], in_=xs[sl], func=AF.Square, accum_out=st_ss[:, tt:tt + 1]),
             r=[("xs", sl), "st_ss"], w=[("junk", tt % 4), ("ss", tt)])
        S.op("dve", lambda e: e.tensor_scalar(out=st_ms[:, tt:tt + 1], in0=st_ss[:, tt:tt + 1], scalar1=1.0 / DM, scalar2=1e-6,
                                              op0=ALU.mult, op1=ALU.add), r=[("ss", tt)], w=[("ms", tt)])
        S.op("pool", lambda e: e.tensor_tensor(out=st_rs[:, tt:tt + 1], in0=st_ms[:, tt:tt + 1], in1=mhalf[:, 0:1], op=ALU.pow),
             r=[("ms", tt), "mhalf"], w=[("rs", tt)])

    def pro_b(tt):
        xsl = tt % 8
        sl = tt % 4
        pb = tt % 4
        S.op("act", lambda e: e.activation(out=xn[sl], in_=xs[xsl], func=AF.Copy, scale=st_rs[:, tt:tt + 1]),
             r=[("xs", xsl), ("rs", tt)], w=[("xn", sl)])
        for c in range(8):
            S.op("pe", lambda e, c=c: e.transpose(out=bankbf(pb)[:, 128 * c:128 * c + 128], in_=xn[sl][:, 128 * c:128 * c + 128], identity=ident),
                 r=[("xn", sl), "ident"], w=[("ps", pb)])
        S.op("dve", lambda e: e.tensor_copy(out=hT[:, :, 128 * tt:128 * tt + 128], in_=bankbf(pb).rearrange("p (c t) -> p c t", c=8)),
             w=[("ps", pb), ("hT", tt // 4)])

    for step in range(32 + 2):
        if step < 32:
            pro_a(step)
        if step >= 2:
            pro_b(step - 2)
    dump("hT", hT, [128, 8, SEQ], BF16, [("hT", i) for i in range(8)])
    if stop_after == "prologue":
        S.emit(sems, dsem)
        return nc, dump_list

    S.barrier()

    cnt = {"w": 0, "rope": 0, "pb": 0, "vt": 0, "sg": 0, "pt": 0, "ob": 0}

    def sub(ap, extra):
        return bass.AP(ap.tensor, ap.offset + extra[0], [list(ap.ap[0])] + [list(x) for x in extra[1]])

    def load_w(wst, wbf, col0, ncols, tag, src, scale=True, extra_w=()):
        for j in range(ncols // 128):
            sl = cnt["w"] % len(wst)
            cnt["w"] += 1
            S.op("sp", lambda e, sl=sl, j=j: e.dma_start(out=wst[sl], in_=src[:, :, col0 + 128 * j:col0 + 128 * j + 128]),
                 w=[("wst", sl)] + list(extra_w), dma=True)
            if scale:
                S.op("pool", lambda e, sl=sl, j=j: e.tensor_tensor(out=wbf[:, :, 128 * j:128 * j + 128], in0=wst[sl],
                                                                    in1=ng.unsqueeze(2).to_broadcast([128, 8, 128]), op=ALU.mult),
                     r=[("wst", sl), "ng"], w=[(tag, j)] + list(extra_w))
            else:
                S.op("pool", lambda e, sl=sl, j=j: e.tensor_copy(out=wbf[:, :, 128 * j:128 * j + 128], in_=wst[sl]),
                     r=[("wst", sl)], w=[(tag, j)])

    def rope(pb, dst, dstkey, cos, sin, ckeys, qraw, t1, t2):
        sl = cnt["rope"] % 2
        cnt["rope"] += 1
        rb = 4 + sl
        aeng = "pool" if (cnt["rope"] // 2) % 2 == 0 else "dve"
        S.op("act", lambda e: e.activation(out=qraw[sl], in_=bank(pb), func=AF.Copy), w=[("ps", pb), ("qraw", sl)])
        S.op("dve", lambda e: e.tensor_tensor(out=t1[sl], in0=bank(pb), in1=cos, op=ALU.mult), r=[ckeys[0]], w=[("ps", pb), ("t1", sl)])

        def stage_b():
            S.op("pe", lambda e: e.matmul(bank(rb), lhsT=permm, rhs=qraw[sl], start=True, stop=True), r=["permm", ("qraw", sl)], w=[("ps", rb)])
            S.op("dve", lambda e: e.tensor_tensor(out=t2[sl], in0=bank(rb), in1=sin, op=ALU.mult), r=[ckeys[1]], w=[("ps", rb), ("t2", sl)])
            S.op(aeng, lambda e: e.tensor_tensor(out=dst, in0=t1[sl], in1=t2[sl], op=ALU.add), r=[("t1", sl), ("t2", sl)], w=[dstkey])
        return stage_b

    pend = []

    def flush():
        while pend:
            pend.pop(0)()

    M = Mem(PBASE)
    wst = [M.f32(1024).rearrange("p (c f) -> p c f", c=8) for _ in range(1)]
    wbf = M.bf(8 * 512).rearrange("p (c f) -> p c f", c=8)
    cosT = [M.f32(512) for _ in range(2)]
    sinT = [M.f32(512) for _ in range(2)]
    qraw = [M.bf(512) for _ in range(2)]
    t1 = [M.f32(512) for _ in range(2)]
    t2 = [M.f32(512) for _ in range(2)]
    qT = M.bf(SEQ)
    kT = M.bf(SEQ)
    vT = M.bf(SEQ)
    gTs = [M.bf(SEQ) for _ in range(2)]
    Vtok = [M.bf(32 * 192) for _ in range(2)]
    PT = [M.bf(1024) for _ in range(4)]
    accs = [M.f32(SEQ), M.f32(SEQ)]
    PHASEA_END = M.off

    for vs in range(2):
        S.op("pool", lambda e, vs=vs: e.memset(sub(Vtok[vs], (64, [[192, 32], [1, 64]])), 1.0), w=[("Vtok", vs)])

    def acc_view(acc, b, g):
        if b == 0:
            return acc[:, 512 * g:512 * g + 512], [("acc", g)], None
        if b == 1:
            r4, m0 = g // 2, 4 * (g % 2)
            s = r4 + 512 * m0
            return acc[:, s:s + 2045:4], [("acc", m0 + i) for i in range(4)], None
        return sub(acc, (2 * g, [[1, 2], [16, 256]])), [("acc", i) for i in range(8)], "p (r l) -> p r l"

    load_w(wst, wbf, 0, 512, "wbf", win_v)
    tails = []
    for p in range(4):
        gT = gTs[p % 2]
        gkey = "gT%d" % (p % 2)
        for tt in range(8):
            tsl = tt % 2
            S.op("sp", lambda e, tt=tt, tsl=tsl: e.dma_start(out=cosT[tsl], in_=ropeA_d[0, :, 512 * tt:512 * tt + 512]), w=[("cos", tsl)], dma=True)
            S.op("sp", lambda e, tt=tt, tsl=tsl: e.dma_start(out=sinT[tsl], in_=ropeA_d[1, :, 512 * tt:512 * tt + 512]), w=[("sin", tsl)], dma=True)
            for f in range(4):
                pb = cnt["pb"] % 4
                cnt["pb"] += 1
                for c in range(8):
                    S.op("pe", lambda e, c=c, f=f, tt=tt, pb=pb: e.matmul(bank(pb), lhsT=wbf[:, c, 128 * f:128 * f + 128],
                                                                          rhs=hT[:, c, 512 * tt:512 * tt + 512], start=(c == 0), stop=(c == 7)),
                         r=[("wbf", f), ("hT", tt)], w=[("ps", pb)])
                flush()
                tsel = slice(512 * tt, 512 * tt + 512)
                if f < 2:
                    pend.append(rope(pb, (qT if f == 0 else kT)[:, tsel], ("qT" if f == 0 else "kT", tt), cosT[tsl], sinT[tsl],
                                     [("cos", tsl), ("sin", tsl)], qraw, t1, t2))
                elif f == 2:
                    S.op("dve", lambda e, pb=pb, tsel=tsel: e.tensor_copy(out=vT[:, tsel], in_=bank(pb)), w=[("ps", pb), ("vT", tt)])
                else:
                    S.op("act", lambda e, pb=pb, tsel=tsel, gT=gT: e.activation(out=gT[:, tsel], in_=bank(pb), func=AF.Silu), w=[("ps", pb), (gkey, tt)])
            if tt == 2:
                while tails:
                    tails.pop(0)()
        flush()
        if p == 0:
            dump("qT0", qT, [128, SEQ], BF16, [("qT", i) for i in range(8)])
            dump("kT0", kT, [128, SEQ], BF16, [("kT", i) for i in range(8)])
            dump("vT0", vT, [128, SEQ], BF16, [("vT", i) for i in range(8)])
            dump("gT0", gT, [128, SEQ], BF16, [(gkey, i) for i in range(8)])
            if stop_after == "proj0":
                S.emit(sems, dsem)
                return nc, dump_list
        if p < 3:
            load_w(wst, wbf, 512 * (p + 1), 512, "wbf", win_v)
        else:
            load_w(wst, wbf, 2048, 512, "wbf", win_v)

        def vtrans_ops(b, vs):
            ops = []
            for g8 in range(4):
                tb = 7
                grp = []
                for j in range(8):
                    n = 8 * g8 + j
                    grp.append(("pe", lambda e, j=j, n=n, tb=tb, b=b: e.transpose(out=bankbf(tb)[:, 128 * j:128 * j + 128], in_=_tok(vT, b, n), identity=ident),
                                _tok_keys("vT", b, n) + ["ident"], [("ps", tb)]))
                grp.append(("dve", lambda e, g8=g8, tb=tb, vs=vs: e.tensor_copy(out=sub(Vtok[vs], (192 * 8 * g8, [[192, 8], [128, 2], [1, 64]])),
                                                                          in_=bankbf(tb).rearrange("p (n h d) -> p n h d", n=8, h=2)),
                            [], [("ps", tb), ("Vtok", vs)]))
                ops.append(grp)
            return ops

        def emit_ops(lst):
            for (eng, fn, r, w) in lst:
                S.op(eng, fn, r=r, w=w)

        groups = []
        for b in range(3):
            vs = b % 2
            for h in range(2):
                for g in range(8):
                    groups.append((b, h, g, vs))
        vt_pending = {}
        emit_ops([o for grp in vtrans_ops(0, 0) for o in grp])

        def s_ops(i):
            b, h, g, vs = groups[i]
            r0 = 64 * h
            sg = i % 3
            Sps = psum[:, 1024 * sg:1024 * sg + 1024]
            for j in range(4):
                kb = 4 * g + j
                last = (kb % BPC[b] == BPC[b] - 1)
                sb_ = ("ps", 2 * sg + j // 2)
                if not last:
                    S.op("pe", lambda e, j=j, kb=kb, Sps=Sps, b=b, r0=r0: e.matmul(Sps[:, 256 * j:256 * j + 256], lhsT=_tok(kT[r0:r0 + 64], b, kb),
                                                                            rhs=_tok2(qT[r0:r0 + 64], b, kb, 2), start=True, stop=True),
                         r=_tok_keys("kT", b, kb) + _tok_keys("qT", b, kb) + _tok_keys("qT", b, kb + 1), w=[sb_])
                else:
                    S.op("pe", lambda e, j=j, kb=kb, Sps=Sps, b=b, r0=r0: e.matmul(Sps[:, 256 * j:256 * j + 128], lhsT=_tok(kT[r0:r0 + 64], b, kb),
                                                                            rhs=_tok(qT[r0:r0 + 64], b, kb), start=True, stop=True),
                         r=_tok_keys("kT", b, kb) + _tok_keys("qT", b, kb), w=[sb_])

        def ew_ops(i):
            b, h, g, vs = groups[i]
            sg = i % 3
            pt = i % 4
            Sps = psum[:, 1024 * sg:1024 * sg + 1024]
            S.op("act", lambda e, Sps=Sps, pt=pt: e.activation(out=PT[pt], in_=Sps, func=AF.Exp, scale=0.125),
                 w=[("ps", 2 * sg), ("ps", 2 * sg + 1), ("PT", pt)])
            meng = "dve"
            S.op(meng, lambda e, pt=pt: e.tensor_tensor(out=PT[pt], in0=PT[pt], in1=amask, op=ALU.mult), r=["amask", ("PT", pt)], w=[("PT", pt)])

        def pv_ops(i):
            b, h, g, vs = groups[i]
            vc0 = 0 if h == 0 else 64
            pt = i % 4
            ob = 6
            acc = accs[h]
            n0 = 4 * g
            started = False

            def vaug(n):
                return Vtok[vs][:, 192 * n + vc0:192 * n + vc0 + 128]

            def mm(o0, ncols, n, rhs, keys, stop):
                nonlocal started
                st = not started
                started = True
                S.op("pe", lambda e: e.matmul(bank(ob)[:, o0:o0 + ncols], lhsT=vaug(n), rhs=rhs, start=st, stop=stop, skip_group_check=True),
                     r=keys + [("Vtok", vs)], w=[("ps", ob)])

            if n0 % BPC[b] != 0:
                ptp = (i - 1) % 4
                mm(0, 128, n0 - 1, PT[ptp][:, 896:1024], [("PT", ptp)], False)
            for j in range(4):
                kb = n0 + j
                last = (kb % BPC[b] == BPC[b] - 1)
                if (not last) and j < 3:
                    mm(128 * j, 256, kb, PT[pt][:, 256 * j:256 * j + 256], [("PT", pt)], False)
                else:
                    mm(128 * j, 128, kb, PT[pt][:, 256 * j:256 * j + 128], [("PT", pt)], j == 3)
            view, akeys, rr = acc_view(acc, b, g)
            akeys = [(k[0] + str(h), k[1]) for k in akeys]
            src = bank(ob) if rr is None else bank(ob).rearrange(rr, r=2)
            if b == 0:
                S.op("act", lambda e, view=view, src=src: e.activation(out=view, in_=src, func=AF.Copy), w=[("ps", ob)] + akeys)
            else:
                S.op("dve", lambda e, view=view, src=src: e.tensor_tensor(out=view, in0=src, in1=view, op=ALU.add),
                     r=akeys, w=[("ps", ob)] + akeys)

        def den_ops(h):
            drow = 64 if h == 0 else 0
            ak = [("acc" + str(h), i) for i in range(8)]
            S.op("pool", lambda e, h=h, drow=drow: e.dma_start(out=dn_d[h:h + 1, :], in_=accs[h][drow:drow + 1, :]), r=ak, w=[("dn", h)], dma=True)
            S.op("pool", lambda e, h=h: e.dma_start(out=dtile[h], in_=dn_d[h].rearrange("(p j) -> p j", j=32)), r=[("dn", h)], w=[("dt", h)], dma=True)
            S.op("dve", lambda e, h=h: e.reciprocal(out=dtile[h], in_=dtile[h]), r=[("dt", h)], w=[("dt", h)])
            S.op("pool", lambda e, h=h: e.dma_start(out=rn_d[h].rearrange("(p j) -> p j", j=32), in_=dtile[h]), r=[("dt", h)], w=[("rn", h)], dma=True)

        NG = len(groups)
        LOOK = 3
        for i in range(min(LOOK, NG)):
            s_ops(i)
        for i in range(min(LOOK - 1, NG)):
            ew_ops(i)
        for i in range(NG):
            b, h, g, vs = groups[i]
            if h == 1 and b < 2 and g % 2 == 0:
                if g == 0:
                    vt_pending[b + 1] = vtrans_ops(b + 1, (b + 1) % 2)
                emit_ops(vt_pending[b + 1][g // 2])
            pv_ops(i)
            if i + LOOK < NG:
                s_ops(i + LOOK)
            if i + LOOK - 1 < NG:
                ew_ops(i + LOOK - 1)
            if b == 2 and g == 7:
                den_ops(h)
        if p == 0:
            dump("acc0", accs[0], [128, SEQ], F32, [("acc0", i) for i in range(8)])
            dump("acc1", accs[1], [128, SEQ], F32, [("acc1", i) for i in range(8)])
        for h in range(2):
            o = 1 - h
            rows = slice(0, 64) if h == 0 else slice(64, 128)
            ok = [("acc" + str(o), i) for i in range(8)]
            S.op("pool", lambda e, h=h, o=o, rows=rows: e.dma_start(out=accs[o][rows, :], in_=rn_d[h:h + 1, :].to_broadcast([64, SEQ])),
                 r=[("rn", h)], w=ok, dma=True)
        def make_tail(p, gT, gkey):
            def tail():
                for half in range(2):
                    cs = slice(2048 * half, 2048 * half + 2048)
                    k0 = [("acc0", 4 * half + i) for i in range(4)]
                    k1 = [("acc1", 4 * half + i) for i in range(4)]
                    gk = [(gkey, 4 * half + i) for i in range(4)]
                    S.op("dve", lambda e, cs=cs: e.tensor_tensor(out=accs[0][:, cs], in0=accs[0][:, cs], in1=accs[1][:, cs], op=ALU.mult), r=k0 + k1, w=k0)
                    S.op("dve", lambda e, cs=cs: e.tensor_tensor(out=gT[:, cs], in0=accs[0][:, cs], in1=gT[:, cs], op=ALU.mult), r=k0 + gk, w=gk)
                    S.op("sp", lambda e, cs=cs: e.dma_start(out=mix_d[p][:, cs], in_=gT[:, cs]), r=gk, w=[("mixd", p)], dma=True)
            return tail

        tails.append(make_tail(p, gT, gkey))
        if p == 3 or stop_after == "attn0":
            while tails:
                tails.pop(0)()
        if p == 0:
            dump("mixT0", gT, [128, SEQ], BF16, [(gkey, i) for i in range(8)])
            if stop_after == "attn0":
                S.emit(sems, dsem)
                return nc, dump_list
    S.barrier()
    for pp in range(4):
        dump("mixd%d" % pp, mix_d[pp], [128, SEQ], BF16, [("mixd", pp)])
    if stop_after == "phaseA":
        S.emit(sems, dsem)
        return nc, dump_list
    M = Mem(PBASE)
    wstB = [M.f32(1024).rearrange("p (c f) -> p c f", c=8) for _ in range(2)]
    wbfB = M.bf(8 * 512).rearrange("p (c f) -> p c f", c=8)
    TB = M.off
    tabs = [[M.f32(512) for _ in range(2)] for _ in range(4)]
    qrawB = [M.bf(512) for _ in range(2)]
    t1B = [M.f32(512) for _ in range(2)]
    t2B = [M.f32(512) for _ in range(2)]
    grt = [M.bf(4 * 512).rearrange("p (f t) -> p f t", f=4) for _ in range(2)]
    TB_END = M.off
    qrT = [M.bf(SEQ) for _ in range(2)]
    krT = [M.bf(SEQ) for _ in range(2)]
    Vr = M.bf(32 * 512).rearrange("p (n v) -> p n v", n=32)
    WO_OFF = M.off
    wo = M.bf(8 * DM).rearrange("p (c f) -> p c f", c=8)

    for tt in range(8):
        for ft in range(2):
            tsl = (2 * tt + ft) % 2
            for k4 in range(4):
                S.op("sp", lambda e, tt=tt, ft=ft, k4=k4, tsl=tsl: e.dma_start(out=tabs[k4][tsl], in_=ropeR_d[4 * ft + k4, :, 512 * tt:512 * tt + 512]),
                     w=[("tab", k4, tsl)], dma=True)
            for qk in range(2):
                f = 2 * qk + ft
                pb = cnt["pb"] % 4
                cnt["pb"] += 1
                for c in range(8):
                    S.op("pe", lambda e, c=c, f=f, tt=tt, pb=pb: e.matmul(bank(pb), lhsT=wbf[:, c, 128 * f:128 * f + 128],
                                                                          rhs=hT[:, c, 512 * tt:512 * tt + 512], start=(c == 0), stop=(c == 7)),
                         r=[("wbf", f), ("hT", tt)], w=[("ps", pb)])
                flush()
                dst = (qrT if qk == 0 else krT)[ft][:, 512 * tt:512 * tt + 512]
                pend.append(rope(pb, dst, ("qrT" if qk == 0 else "krT", ft, tt), tabs[2 * qk][tsl], tabs[2 * qk + 1][tsl],
                                 [("tab", 2 * qk, tsl), ("tab", 2 * qk + 1, tsl)], qrawB, t1B, t2B))
    flush()
    dump("qrT0", qrT[0], [128, SEQ], BF16, [("qrT", 0, i) for i in range(8)])
    dump("krT0", krT[0], [128, SEQ], BF16, [("krT", 0, i) for i in range(8)])
    if stop_after == "B1":
        S.emit(sems, dsem)
        return nc, dump_list
    load_w(wstB, wbfB, 2560, 512, "wbfB", win_v, extra_w=[("wbf", f) for f in range(4)])
    for n in range(32):
        pb = cnt["pb"] % 4
        cnt["pb"] += 1
        for c in range(8):
            S.op("pe", lambda e, c=c, n=n, pb=pb: e.matmul(bank(pb), lhsT=hT[:, c, 128 * n:128 * n + 128], rhs=wbfB[:, c, :], start=(c == 0), stop=(c == 7)),
                 r=[("wbfB", i) for i in range(4)] + [("hT", n // 4)], w=[("ps", pb)])
        if n % 2 == 0:
            S.op("dve", lambda e, n=n, pb=pb: e.tensor_copy(out=Vr[:, n, :], in_=bank(pb)), w=[("ps", pb), ("Vr", n)])
        else:
            S.op("act", lambda e, n=n, pb=pb: e.activation(out=Vr[:, n, :], in_=bank(pb), func=AF.Copy), w=[("ps", pb), ("Vr", n)])
    load_w(wstB, wbfB, 3072, 512, "wbfB", win_v)
    for tt in range(8):
        gs = tt % 2
        for ft in range(4):
            pb = cnt["pb"] % 4
            cnt["pb"] += 1
            for c in range(8):
                S.op("pe", lambda e, c=c, ft=ft, tt=tt, pb=pb: e.matmul(bank(pb), lhsT=wbfB[:, c, 128 * ft:128 * ft + 128],
                                                                        rhs=hT[:, c, 512 * tt:512 * tt + 512], start=(c == 0), stop=(c == 7)),
                     r=[("wbfB", ft), ("hT", tt)], w=[("ps", pb)])
            S.op("act", lambda e, ft=ft, pb=pb, gs=gs: e.activation(out=grt[gs][:, ft, :], in_=bank(pb), func=AF.Silu), w=[("ps", pb), ("grt", gs, ft)])
            S.op("dve", lambda e, ft=ft, gs=gs: e.tensor_scalar(out=grt[gs][:, ft, :], in0=grt[gs][:, ft, :], scalar1=gn[:, ft:ft + 1], scalar2=None, op0=ALU.mult),
                 r=[("grt", gs, ft), "gn"], w=[("grt", gs, ft)])
        S.op("sp", lambda e, tt=tt, gs=gs: e.dma_start(out=grs_d[:, :, 512 * tt:512 * tt + 512].rearrange("f p t -> p f t"), in_=grt[gs]),
             r=[("grt", gs, ft) for ft in range(4)], w=[("grs", tt)], dma=True)
    if stop_after == "B3":
        S.emit(sems, dsem)
        return nc, dump_list
    S.barrier()

    M = Mem(TB)
    Ktok = [M.bf(256) for _ in range(2)]
    sT = [M.bf(512) for _ in range(2)]
    onb = [M.bf(512) for _ in range(2)]
    grl = [M.bf(4 * 512).rearrange("p (f t) -> p f t", f=4) for _ in range(2)]
    mixst = [M.bf(4 * 512).rearrange("p (f t) -> p f t", f=4) for _ in range(2)]
    assert M.off <= TB_END - 256
    mixA = Mem(0).bf(8 * SEQ).rearrange("p (c t) -> p c t", c=8)
    def mixa_load(c):
        S.op("sp", lambda e: e.dma_start(out=mixA[:, c, :], in_=mix_d[c]), r=[("mixd", c)], w=[("mixA", c)] + [("hT", i) for i in range(8)], dma=True)
    dump("mixa0e", mixA[:, 0:4, 0:512], [128, 4, 512], BF16, [("mixA", h) for h in range(4)])
    for pp in range(4):
        dump("mixe%d" % pp, mix_d[pp], [128, SEQ], BF16, [("mixd", pp)])
    def wo_chunk(j):
        sl = cnt["w"] % len(wstB)
        cnt["w"] += 1
        S.op("sp", lambda e: e.dma_start(out=wstB[sl], in_=wout_v[:, :, 128 * j:128 * j + 128]), w=[("wst", sl)], dma=True)
        S.op("pool", lambda e: e.tensor_copy(out=wo[:, :, 128 * j:128 * j + 128], in_=wstB[sl]), r=[("wst", sl)], w=[("wo", j)])
    S.op("dve", lambda e: e.memset(state, 0.0), w=["state"])
    nbias = bnst[:, 0:4]
    bnst2 = [bnst, Mem(TB_END - 256).f32(24)]
    bnmv2 = [bnmv, Mem(TB_END - 128).f32(8)]
    rstd2 = [rstd4, Mem(TB_END - 64).f32(4)]
    nb2 = [Mem(TB_END - 32).f32(4), Mem(TB_END - 16).f32(4)]

    OB = (2, 3, 6)

    def r_1(n):
        tsel = slice(128 * n, 128 * n + 128)
        ks = n % 2
        grp, sub4 = n // 4, n % 4
        gl = grp % 2
        if sub4 == 0:
            S.op("sp", lambda e: e.dma_start(out=grl[gl], in_=grs_d[:, :, 512 * grp:512 * grp + 512].rearrange("f p t -> p f t")),
                 r=[("grs", grp)], w=[("grl", gl)], dma=True)
        for ft in range(2):
            S.op("pe", lambda e, ft=ft: e.transpose(out=bankbf(0)[:, 128 * ft:128 * ft + 128], in_=krT[ft][:, tsel], identity=ident),
                 r=[("krT", ft, n // 4), "ident"], w=[("ps", 0)])
        S.op("act", lambda e: e.activation(out=Ktok[ks], in_=bankbf(0)[:, 0:256], func=AF.Copy), w=[("ps", 0), ("Ktok", ks)])
        S.op("pool", lambda e: e.tensor_tensor(out=Ktok[ks], in0=Ktok[ks], in1=gck, op=ALU.mult), r=["gck", ("Ktok", ks)], w=[("Ktok", ks)])
        for h in (0, 2, 1, 3):
            r0 = 64 * (h % 2)
            sbk = 1 if h % 2 == 0 else 7
            S.op("pe", lambda e, h=h, r0=r0, sbk=sbk: e.matmul(bank(sbk)[:, 128 * (h // 2):128 * (h // 2) + 128], lhsT=krT[h // 2][r0:r0 + 64, tsel],
                                                          rhs=qrT[h // 2][r0:r0 + 64, tsel], start=True, stop=True),
                 r=[("krT", h // 2, n // 4), ("qrT", h // 2, n // 4)], w=[("ps", sbk)])
        for hp, sbk in ((0, 1), (1, 7)):
            S.op("dve", lambda e, hp=hp, sbk=sbk: e.tensor_tensor(out=sT[ks][:, 256 * hp:256 * hp + 256], in0=bank(sbk)[:, 0:256], in1=rmask[:, 0:256], op=ALU.mult),
                 r=["rmask"], w=[("ps", sbk), ("sT", ks)])

    def r_2(n):
        tsel = slice(128 * n, 128 * n + 128)
        ks = n % 2
        ob = OB[n % 3]
        for h in range(4):
            r0 = 64 * (h % 2)
            S.op("pe", lambda e, h=h, r0=r0: e.matmul(bank(4)[r0:r0 + 64, 128 * (h // 2):128 * (h // 2) + 128], lhsT=Ktok[ks][:, 64 * h:64 * h + 64],
                                                 rhs=Vr[:, n, 128 * h:128 * h + 128], start=True, stop=True),
                 r=[("Ktok", ks), ("Vr", n)], w=[("ps", 4)])
        for h in range(4):
            r0 = 64 * (h % 2)
            sblk = 2 * (h % 2) + h // 2
            S.op("pe", lambda e, h=h, sblk=sblk: e.matmul(bank(ob)[:, 128 * h:128 * h + 128], lhsT=sT[ks][:, 128 * sblk:128 * sblk + 128],
                                                     rhs=Vr[:, n, 128 * h:128 * h + 128], start=True, stop=(n == 0)),
                 r=[("sT", ks), ("Vr", n)], w=[("ps", ob)])
            if n > 0:
                S.op("pe", lambda e, h=h, r0=r0: e.matmul(bank(ob)[:, 128 * h:128 * h + 128], lhsT=qrT[h // 2][r0:r0 + 64, tsel],
                                                     rhs=state_bf[r0:r0 + 64, 128 * (h // 2):128 * (h // 2) + 128], start=False, stop=True),
                     r=[("qrT", h // 2, n // 4), "state_bf"], w=[("ps", ob)])
        for cb in range(2):
            S.op("dve", lambda e, cb=cb: e.scalar_tensor_tensor(out=state[:, 128 * cb:128 * cb + 128], in0=state[:, 128 * cb:128 * cb + 128],
                                                                scalar=gsc[:, cb:cb + 1], in1=bank(4)[:, 128 * cb:128 * cb + 128], op0=ALU.mult, op1=ALU.add),
                 r=["state", "gsc"], w=["state", ("ps", 4)])
        S.op("act", lambda e: e.activation(out=state_bf, in_=state, func=AF.Copy), r=["state"], w=["state_bf"])

    def r_3(n):
        ks = n % 2
        ob = OB[n % 3]
        for h in range(4):
            S.op("dve", lambda e, h=h: e.bn_stats(out=bnst2[ks][:, 6 * h:6 * h + 6], in_=bank(ob)[:, 128 * h:128 * h + 128]), w=[("ps", ob), ("bnst", ks, h)])
        for h in range(4):
            S.op("dve", lambda e, h=h: e.bn_aggr(out=bnmv2[ks][:, 2 * h:2 * h + 2], in_=bnst2[ks][:, 6 * h:6 * h + 6]), r=[("bnst", ks, h)], w=[("bnmv", ks, h)])
        S.op("pool", lambda e: e.tensor_scalar(out=rstd2[ks], in0=bnmv2[ks][:, 1:8:2], scalar1=1e-5, scalar2=None, op0=ALU.add), r=[("bnmv", ks, h) for h in range(4)], w=[("rstd", ks)])
        S.op("pool", lambda e: e.tensor_tensor(out=rstd2[ks], in0=rstd2[ks], in1=mhalf, op=ALU.pow), r=[("rstd", ks), "mhalf"], w=[("rstd", ks)])
        S.op("pool", lambda e: e.tensor_tensor(out=nb2[ks], in0=bnmv2[ks][:, 0:8:2], in1=rstd2[ks], op=ALU.mult), r=[("bnmv", ks, h) for h in range(4)] + [("rstd", ks)], w=[("nb", ks)])
        S.op("pool", lambda e: e.tensor_scalar(out=nb2[ks], in0=nb2[ks], scalar1=-1.0, scalar2=None, op0=ALU.mult), r=[("nb", ks)], w=[("nb", ks)])

    def r_4(n):
        ks = n % 2
        ob = OB[n % 3]
        for h in range(4):
            S.op("act", lambda e, h=h: e.activation(out=onb[ks][:, 128 * h:128 * h + 128], in_=bank(ob)[:, 128 * h:128 * h + 128], func=AF.Identity,
                                                    bias=nb2[ks][:, h:h + 1], scale=rstd2[ks][:, h:h + 1]),
                 r=[("nb", ks), ("rstd", ks)], w=[("ps", ob), ("onb", ks, h)])

    def r_5(n):
        ks = n % 2
        grp, sub4 = n // 4, n % 4
        gl = grp % 2
        for h in range(4):
            S.op("pe", lambda e, h=h: e.transpose(out=bankbf(5)[:, 128 * h:128 * h + 128], in_=onb[ks][:, 128 * h:128 * h + 128], identity=ident),
                 r=[("onb", ks, h), "ident"], w=[("ps", 5)])
        S.op("dve", lambda e: e.tensor_tensor(out=mixA[:, 4:8, 128 * n:128 * n + 128], in0=bankbf(5)[:, 0:512].rearrange("p (h t) -> p h t", h=4),
                                              in1=grl[gl][:, :, 128 * sub4:128 * sub4 + 128], op=ALU.mult),
             r=[("grl", gl)], w=[("ps", 5)] + [("mixA", 4 + h) for h in range(4)])

    stages = ((r_1, 0), (r_2, 1), (r_3, 2), (r_4, 3), (r_5, 4))
    for step in range(32 + 4):
        for fn, sk in stages:
            if 0 <= step - sk < 32:
                fn(step - sk)
        if step % 4 == 2 and step // 4 < 8:
            wo_chunk(step // 4)
        if step % 4 == 1 and step // 4 < 4:
            mixa_load(step // 4)
    dump("mixr0", mixA[:, 4:8, 0:512], [128, 4, 512], BF16, [("mixA", 4 + h) for h in range(4)])
    dump("mixa0", mixA[:, 0:4, 0:512], [128, 4, 512], BF16, [("mixA", h) for h in range(4)])
    dump("wo", wo, [128, 8, DM], BF16, [("wo", j) for j in range(8)])
    if stop_after == "ret":
        S.emit(sems, dsem)
        return nc, dump_list
    S.barrier()

    M = Mem(PBASE + 16384)
    xr = [M.f32(DM) for _ in range(8)]
    yb = [M.f32(DM) for _ in range(4)]
    junk2s = [M.bf(DM) for _ in range(4)]
    fg = M.f32(DM)
    assert M.off <= WO_OFF
    S.op("sp", lambda e: e.dma_start(out=fg, in_=fg_d), w=["fg"], dma=True)
    S.op("dve", lambda e: e.memset(st_ss, 0.0), w=["st_ss"])
    def c_a(tt):
        sl = tt % 4
        xsl = tt % 8
        S.op("sp", lambda e: e.dma_start(out=xr[xsl], in_=x_d[128 * tt:128 * tt + 128, :]), w=[("xr", xsl)], dma=True)
        for nh in range(2):
            pb = (2 * tt + nh) % 8
            for c in range(8):
                S.op("pe", lambda e, c=c, nh=nh, pb=pb: e.matmul(bank(pb), lhsT=mixA[:, c, 128 * tt:128 * tt + 128], rhs=wo[:, c, 512 * nh:512 * nh + 512],
                                                                 start=(c == 0), stop=(c == 7)),
                     r=[("mixA", c)] + [("wo", 4 * nh + i) for i in range(4)], w=[("ps", pb)])
            S.op("dve", lambda e, nh=nh, pb=pb: e.tensor_tensor(out=yb[sl][:, 512 * nh:512 * nh + 512], in0=bank(pb), in1=xr[xsl][:, 512 * nh:512 * nh + 512], op=ALU.add),
                 r=[("xr", xsl)], w=[("ps", pb), ("yb", sl, nh)])
        S.op("act", lambda e: e.activation(out=junk2s[tt % 4], in_=yb[sl], func=AF.Square, accum_out=st_ss[:, tt:tt + 1]),
             r=[("yb", sl, 0), ("yb", sl, 1), "st_ss"], w=[("junk2", tt % 4), ("ss", tt)])

    def c_b(tt):
        S.op("dve", lambda e: e.tensor_scalar(out=st_ms[:, tt:tt + 1], in0=st_ss[:, tt:tt + 1], scalar1=1.0 / DM, scalar2=1e-6,
                                              op0=ALU.mult, op1=ALU.add), r=[("ss", tt)], w=[("ms", tt)])
        S.op("pool", lambda e: e.tensor_tensor(out=st_rs[:, tt:tt + 1], in0=st_ms[:, tt:tt + 1], in1=mhalf[:, 0:1], op=ALU.pow),
             r=[("ms", tt), "mhalf"], w=[("rs", tt)])

    def c_c(tt):
        sl = tt % 4
        S.op("dve", lambda e: e.scalar_tensor_tensor(out=yb[sl], in0=yb[sl], scalar=st_rs[:, tt:tt + 1], in1=fg, op0=ALU.mult, op1=ALU.mult),
             r=[("yb", sl, 0), ("yb", sl, 1), ("rs", tt), "fg"], w=[("yb", sl, 0), ("yb", sl, 1)])
        S.op("pool", lambda e: e.dma_start(out=out_d[128 * tt:128 * tt + 128, :], in_=yb[sl]), r=[("yb", sl, 0), ("yb", sl, 1)], w=[("out", tt)], dma=True)

    for step in range(32 + 2):
        if step < 32:
            c_a(step)
        if 1 <= step < 33:
            c_b(step - 1)
        if step >= 2:
            c_c(step - 2)
    S.emit(sems, dsem)
    return nc, dump_list


_CACHE = {}


def _prep_inputs(x, norm_gain, w_in, ret_gn_gain, w_out, final_gain):
    c = _host_constants()
    w_in_p = np.ascontiguousarray(np.asarray(w_in, np.float32)[0][:, _win_cols()])
    shared = dict(c)
    shared["w_in"] = w_in_p
    shared["w_out"] = np.ascontiguousarray(np.asarray(w_out, np.float32)[0])
    shared["ng"] = np.ascontiguousarray(np.asarray(norm_gain, np.float32)[0].reshape(8, 128).T)
    shared["gn"] = np.ascontiguousarray(np.asarray(ret_gn_gain, np.float32)[0].reshape(4, 128).T)
    shared["fg"] = np.ascontiguousarray(np.broadcast_to(np.asarray(final_gain, np.float32)[None, :], (128, DM)))
    x = np.asarray(x, np.float32)
    return [dict(shared, x=np.ascontiguousarray(x[b])) for b in range(x.shape[0])]


def kernel(x, norm_gain, w_in, ret_gn_gain, w_out, final_gain):
    in_maps = _prep_inputs(x, norm_gain, w_in, ret_gn_gain, w_out, final_gain)
    nc, _ = build()
    res = run_bass_kernel_spmd(nc, in_maps, core_ids=list(range(NCORES)))
    return np.stack([np.asarray(r["out"], np.float32) for r in res.results], axis=0)
```

```python
import numpy as np
import ml_dtypes
import concourse.bass as bass
import concourse.mybir as mybir
from concourse.bass_utils import run_bass_kernel_spmd

dt = mybir.dt
F32 = dt.float32
BF16 = dt.bfloat16
AF = mybir.ActivationFunctionType
ALU = mybir.AluOpType
AX = mybir.AxisListType


class Sched:
    COMPUTE = ("pe", "act", "dve", "pool")
    EPOCH = 400
    STRICT = True

    def __init__(self, nc, dma_ring=None):
        self.nc = nc
        self.ops = []
        self.lastw = {}
        self.readers = {}
        self.last_on = {}
        self.extra = {}
        self.dma_live = []
        self.dma_ring = dma_ring or {"sp": 24, "pool": 12, "act": 6}

    def op(self, eng, fn, r=(), w=(), dma=False):
        i = len(self.ops)
        deps = set(self.extra.pop(eng, ()))
        for k in r:
            s = self.lastw.get(k)
            if s is not None:
                deps.add((s, 0))
        for k in w:
            s = self.lastw.get(k)
            if s is not None:
                deps.add((s, 1))
            for s in self.readers.get(k, ()):
                deps.add((s, 2))
        keep = set()
        for (s, kind) in deps:
            so = self.ops[s]
            if so["dma"] or dma:
                keep.add(s)
            elif so["eng"] == eng:
                if eng != "pe" and (kind == 0 or self.STRICT):
                    keep.add(s)
            else:
                keep.add(s)
        keep.discard(i)
        self.ops.append(dict(eng=eng, fn=fn, deps=sorted(keep), dma=dma, signal=dma))
        for s in keep:
            self.ops[s]["signal"] = True
        for k in w:
            self.lastw[k] = i
            self.readers[k] = []
        for k in r:
            lst = self.readers.setdefault(k, [])
            if not dma:
                lst[:] = [s for s in lst if self.ops[s]["dma"] or self.ops[s]["eng"] != eng]
            lst.append(i)
        self.last_on[eng] = i
        if dma:
            self.dma_live.append(i)
        return i

    def barrier(self):
        srcs = set(self.last_on.values()) | set(self.dma_live)
        self.dma_live = []
        for e in ("pe", "act", "dve", "pool", "sp"):
            self.extra.setdefault(e, set())
            for s in srcs:
                so = self.ops[s]
                if so["eng"] == e and not so["dma"]:
                    continue
                self.extra[e].add((s, 0))
                so["signal"] = True

    def emit(self, sems, dma_sems):
        nc = self.nc
        cnt = {e: 0 for e in self.COMPUTE}
        dcnt = {q: 0 for q in dma_sems}
        dval = {q: [0] * len(dma_sems[q]) for q in dma_sems}
        for o in self.ops:
            if o["dma"]:
                q = o["eng"]
                j = dcnt[q] % len(dma_sems[q])
                dcnt[q] += 1
                o["ring_prev"] = (dma_sems[q][j], dval[q][j]) if dval[q][j] > 0 else None
                dval[q][j] += 16
                o["done"] = (dma_sems[q][j], dval[q][j])
            elif o["signal"]:
                k = cnt[o["eng"]]
                cnt[o["eng"]] += 1
                o["done"] = (sems[o["eng"]][k // self.EPOCH], k % self.EPOCH + 1)
        by_eng = {}
        for o in self.ops:
            by_eng.setdefault(o["eng"], []).append(o)
        final_waits = []
        for q in dma_sems:
            for j, s in enumerate(dma_sems[q]):
                if dval[q][j] > 0:
                    final_waits.append((s, dval[q][j]))
        handles = {"pe": nc.tensor, "act": nc.scalar, "dve": nc.vector, "pool": nc.gpsimd, "sp": nc.sync}
        ops_all = self.ops

        def run(eng, e):
            waited = {}

            def wait(sem, val):
                key = id(sem)
                if waited.get(key, 0) < val:
                    e.wait_ge(sem, val)
                    waited[key] = val

            for o in by_eng.get(eng, []):
                for s in o["deps"]:
                    sem, val = ops_all[s]["done"]
                    wait(sem, val)
                if o["dma"] and o["ring_prev"] is not None:
                    wait(*o["ring_prev"])
                ins = o["fn"](e)
                if o["signal"]:
                    ins.then_inc(o["done"][0], 16 if o["dma"] else 1)
            if eng == "sp":
                for (s, v) in final_waits:
                    wait(s, v)

        with nc.Block() as block:
            @block.sync
            def _(e):
                run("sp", e)

            @block.tensor
            def _(e):
                run("pe", e)

            @block.scalar
            def _(e):
                run("act", e)

            @block.vector
            def _(e):
                run("dve", e)

            @block.gpsimd
            def _(e):
                run("pool", e)


SEQ = 4096
DM = 1024
NCORES = 8
THETA = 10000.0
_bf = ml_dtypes.bfloat16


def _host_constants():
    c = {}
    c["ident"] = np.eye(128, dtype=np.float32).astype(_bf)
    permm = np.zeros((128, 128), np.float32)
    for m in range(128):
        k = m + 32 if (m % 64) < 32 else m - 32
        permm[k, m] = 1.0
    c["permm"] = permm.astype(_bf)
    kk = np.arange(128)[:, None]
    qq = np.arange(128)[None, :]
    prev = (kk >= qq).astype(np.float32)
    own = (qq >= kk).astype(np.float32)
    c["amask"] = np.tile(np.concatenate([own, prev], 1), (1, 4)).astype(_bf)
    c["rmask"] = np.tile(own, (1, 4)).astype(_bf)
    pos = np.arange(SEQ, dtype=np.float64)
    j = np.arange(128) % 32
    sign = np.where((np.arange(128) % 64) >= 32, 1.0, -1.0)[:, None]
    inv = THETA ** (-(np.arange(32, dtype=np.float64)) / 32.0)
    ang = pos[None, :] * inv[j][:, None]
    c["ropeA"] = np.stack([np.cos(ang), np.sin(ang) * sign]).astype(np.float32)
    invr = 1.0 / (THETA ** np.linspace(0.0, 1.0, 32, dtype=np.float64))
    angr = pos[None, :] * invr[j][:, None]
    cosR = np.cos(angr)
    sinR = np.sin(angr) * sign
    lg = np.log1p(-(2.0 ** (-5.0 - np.arange(4, dtype=np.float64))))
    i_in = (np.arange(SEQ) % 128).astype(np.float64)
    ropeR = np.zeros((8, 128, SEQ), np.float32)
    for ft in range(2):
        for hp in range(2):
            h = 2 * ft + hp
            rows = slice(64 * hp, 64 * hp + 64)
            dq = np.exp((i_in + 1.0) * lg[h])[None, :]
            dk = (np.exp(-(i_in + 1.0) * lg[h]) / 8.0)[None, :]
            ropeR[ft * 4 + 0, rows] = cosR[rows] * dq
            ropeR[ft * 4 + 1, rows] = sinR[rows] * dq
            ropeR[ft * 4 + 2, rows] = cosR[rows] * dk
            ropeR[ft * 4 + 3, rows] = sinR[rows] * dk
    c["ropeR"] = ropeR
    gC = np.exp(128.0 * lg)
    gck = np.zeros((128, 256), np.float32)
    for h in range(4):
        gck[:, h * 64:(h + 1) * 64] = gC[h]
    c["gck"] = gck
    gsc = np.zeros((128, 2), np.float32)
    for cb in range(2):
        gsc[:64, cb] = gC[2 * cb]
        gsc[64:, cb] = gC[2 * cb + 1]
    c["gsc"] = gsc
    return c


def _win_cols():
    cols = []
    for p in range(4):
        for base in (0, 512, 1024, 1536):
            cols += list(range(base + 128 * p, base + 128 * p + 128))
    permh = [2 * i for i in range(32)] + [2 * i + 1 for i in range(32)]
    for base in (2048, 2304):
        for h in range(4):
            cols += [base + 64 * h + cc for cc in permh]
    cols += list(range(2560, 3584))
    return np.asarray(cols)


def _tok(buf, b, n):
    if b == 0:
        return buf[:, 128 * n:128 * n + 128]
    if b == 1:
        r, m = n // 8, n % 8
        s = r + 512 * m
        return buf[:, s:s + 509:4]
    r, m = n // 2, n % 2
    s = r + 2048 * m
    return buf[:, s:s + 2033:16]


def _tok2(buf, b, n, nblk):
    cntk = 128 * nblk
    if b == 0:
        return buf[:, 128 * n:128 * n + cntk]
    if b == 1:
        r, m = n // 8, n % 8
        s = r + 512 * m
        return buf[:, s:s + 4 * (cntk - 1) + 1:4]
    r, m = n // 2, n % 2
    s = r + 2048 * m
    return buf[:, s:s + 16 * (cntk - 1) + 1:16]


def _tok_keys(name, b, n):
    if b == 0:
        return [(name, n // 4)]
    if b == 1:
        return [(name, n % 8)]
    return [(name, 4 * (n % 2) + i) for i in range(4)]


BPC = (32, 8, 2)


def build(stop_after=None, dumps=()):
    nc = bass.Bass("TRN2", target_bir_lowering=False)
    from contextlib import ExitStack
    es = ExitStack()

    def din(name, shape, d=F32):
        return nc.dram_tensor(name, shape, d, kind="ExternalInput").ap()

    x_d = din("x", [SEQ, DM])
    win_d = din("w_in", [DM, 3584])
    wout_d = din("w_out", [DM, DM])
    ng_d = din("ng", [128, 8])
    gn_d = din("gn", [128, 4])
    fg_d = din("fg", [128, DM])
    ident_d = din("ident", [128, 128], BF16)
    permm_d = din("permm", [128, 128], BF16)
    amask_d = din("amask", [128, 1024], BF16)
    rmask_d = din("rmask", [128, 512], BF16)
    ropeA_d = din("ropeA", [2, 128, SEQ])
    ropeR_d = din("ropeR", [8, 128, SEQ])
    gck_d = din("gck", [128, 256])
    gsc_d = din("gsc", [128, 2])
    out_d = nc.dram_tensor("out", [SEQ, DM], F32, kind="ExternalOutput").ap()
    mix_d = nc.dram_tensor("mixs", [8, 128, SEQ], BF16, kind="Internal").ap()
    grs_d = nc.dram_tensor("grs", [4, 128, SEQ], BF16, kind="Internal").ap()
    dn_d = nc.dram_tensor("dns", [2, SEQ], F32, kind="Internal").ap()
    rn_d = nc.dram_tensor("rns", [2, SEQ], F32, kind="Internal").ap()
    win_v = win_d.rearrange("(c p) f -> p c f", p=128)
    wout_v = wout_d.rearrange("(c p) f -> p c f", p=128)

    ARENA_BYTES = 211968
    arena = es.enter_context(nc.sbuf_tensor("arena", [128, ARENA_BYTES // 4], F32))
    psum = es.enter_context(nc.psum_tensor("psum", [128, 4096], F32))
    sems = {e: [es.enter_context(nc.semaphore(f"s_{e}{i}")) for i in range(n)]
            for e, n in (("pe", 12), ("act", 6), ("dve", 12), ("pool", 4))}
    dsem = {q: [es.enter_context(nc.semaphore(f"d_{q}{i}")) for i in range(n)] for q, n in (("sp", 28), ("pool", 8))}
    S = Sched(nc)

    class Mem:
        def __init__(self, base):
            self.off = base

        def f32(self, n):
            o = self.off
            self.off += 4 * n
            assert self.off <= ARENA_BYTES, self.off
            return arena[:, o // 4:o // 4 + n]

        def bf(self, n):
            o = self.off
            self.off += 2 * n
            assert self.off % 4 == 0 and self.off <= ARENA_BYTES, self.off
            return arena[:, o // 4:o // 4 + n // 2].bitcast(BF16)

    def bank(i, n=512):
        return psum[:, 512 * i:512 * i + n]

    def bankbf(i):
        return psum[:, 512 * i:512 * i + 512].bitcast(BF16)

    dump_list = []

    def dump(name, ap, shape, d, keys):
        if name not in dumps:
            return
        t = nc.dram_tensor("dbg_" + name, shape, d, kind="ExternalOutput").ap()
        S.op("sp", lambda e, t=t, ap=ap: e.dma_start(out=t, in_=ap), r=keys, dma=True)
        dump_list.append(name)

    M = Mem(0)
    hT = M.bf(8 * SEQ).rearrange("p (c t) -> p c t", c=8)
    ident = M.bf(128)
    permm = M.bf(128)
    amask = M.bf(1024)
    rmask = M.bf(512)
    gck = M.f32(256)
    ng = M.f32(8)
    gn = M.f32(4)
    gsc = M.f32(2)
    st_ss = M.f32(32)
    st_ms = M.f32(32)
    st_rs = M.f32(32)
    state = M.f32(256)
    state_bf = M.bf(256)
    bnst = M.f32(24)
    bnmv = M.f32(8)
    rstd4 = M.f32(4)
    mhalf = M.f32(4)
    dtile = [M.f32(32) for _ in range(2)]
    PBASE = (M.off + 63) // 64 * 64

    S.op("sp", lambda e: e.dma_start(out=ident, in_=ident_d), w=["ident"], dma=True)
    S.op("sp", lambda e: e.dma_start(out=permm, in_=permm_d), w=["permm"], dma=True)
    S.op("sp", lambda e: e.dma_start(out=amask, in_=amask_d), w=["amask"], dma=True)
    S.op("sp", lambda e: e.dma_start(out=rmask, in_=rmask_d), w=["rmask"], dma=True)
    S.op("sp", lambda e: e.dma_start(out=gck, in_=gck_d), w=["gck"], dma=True)
    S.op("sp", lambda e: e.dma_start(out=ng, in_=ng_d), w=["ng"], dma=True)
    S.op("sp", lambda e: e.dma_start(out=gn, in_=gn_d), w=["gn"], dma=True)
    S.op("sp", lambda e: e.dma_start(out=gsc, in_=gsc_d), w=["gsc"], dma=True)
    S.op("dve", lambda e: e.memset(st_ss, 0.0), w=["st_ss"])
    S.op("dve", lambda e: e.memset(mhalf, -0.5), w=["mhalf"])

    M = Mem(PBASE)
    xs = [M.f32(DM) for _ in range(8)]
    junks = [M.bf(DM) for _ in range(4)]
    xn = [M.bf(DM) for _ in range(4)]
    def pro_a(tt):
        sl = tt % 8
        S.op("sp", lambda e: e.dma_start(out=xs[sl], in_=x_d[128 * tt:128 * tt + 128, :]), w=[("xs", sl)], dma=True)
        S.op("act", lambda e: e.activation(out=junks[tt % 4], in_=xs[sl], func=AF.Square, accum_out=st_ss[:, tt:tt + 1]),
             r=[("xs", sl), "st_ss"], w=[("junk", tt % 4), ("ss", tt)])
        S.op("dve", lambda e: e.tensor_scalar(out=st_ms[:, tt:tt + 1], in0=st_ss[:, tt:tt + 1], scalar1=1.0 / DM, scalar2=1e-6,
                                              op0=ALU.mult, op1=ALU.add), r=[("ss", tt)], w=[("ms", tt)])
        S.op("pool", lambda e: e.tensor_tensor(out=st_rs[:, tt:tt + 1], in0=st_ms[:, tt:tt + 1], in1=mhalf[:, 0:1], op=ALU.pow),
             r=[("ms", tt), "mhalf"], w=[("rs", tt)])

    def pro_b(tt):
        xsl = tt % 8
        sl = tt % 4
        pb = tt % 4
        S.op("act", lambda e: e.activation(out=xn[sl], in_=xs[xsl], func=AF.Copy, scale=st_rs[:, tt:tt + 1]),
             r=[("xs", xsl), ("rs", tt)], w=[("xn", sl)])
        for c in range(8):
            S.op("pe", lambda e, c=c: e.transpose(out=bankbf(pb)[:, 128 * c:128 * c + 128], in_=xn[sl][:, 128 * c:128 * c + 128], identity=ident),
                 r=[("xn", sl), "ident"], w=[("ps", pb)])
        S.op("dve", lambda e: e.tensor_copy(out=hT[:, :, 128 * tt:128 * tt + 128], in_=bankbf(pb).rearrange("p (c t) -> p c t", c=8)),
             w=[("ps", pb), ("hT", tt // 4)])

    for step in range(32 + 2):
        if step < 32:
            pro_a(step)
        if step >= 2:
            pro_b(step - 2)
    dump("hT", hT, [128, 8, SEQ], BF16, [("hT", i) for i in range(8)])
    if stop_after == "prologue":
        S.emit(sems, dsem)
        return nc, dump_list

    S.barrier()

    cnt = {"w": 0, "rope": 0, "pb": 0, "vt": 0, "sg": 0, "pt": 0, "ob": 0}

    def sub(ap, extra):
        return bass.AP(ap.tensor, ap.offset + extra[0], [list(ap.ap[0])] + [list(x) for x in extra[1]])

    def load_w(wst, wbf, col0, ncols, tag, src, scale=True, extra_w=()):
        for j in range(ncols // 128):
            sl = cnt["w"] % len(wst)
            cnt["w"] += 1
            S.op("sp", lambda e, sl=sl, j=j: e.dma_start(out=wst[sl], in_=src[:, :, col0 + 128 * j:col0 + 128 * j + 128]),
                 w=[("wst", sl)] + list(extra_w), dma=True)
            if scale:
                S.op("pool", lambda e, sl=sl, j=j: e.tensor_tensor(out=wbf[:, :, 128 * j:128 * j + 128], in0=wst[sl],
                                                                    in1=ng.unsqueeze(2).to_broadcast([128, 8, 128]), op=ALU.mult),
                     r=[("wst", sl), "ng"], w=[(tag, j)] + list(extra_w))
            else:
                S.op("pool", lambda e, sl=sl, j=j: e.tensor_copy(out=wbf[:, :, 128 * j:128 * j + 128], in_=wst[sl]),
                     r=[("wst", sl)], w=[(tag, j)])

    def rope(pb, dst, dstkey, cos, sin, ckeys, qraw, t1, t2):
        sl = cnt["rope"] % 2
        cnt["rope"] += 1
        rb = 4 + sl
        aeng = "pool" if (cnt["rope"] // 2) % 2 == 0 else "dve"
        S.op("act", lambda e: e.activation(out=qraw[sl], in_=bank(pb), func=AF.Copy), w=[("ps", pb), ("qraw", sl)])
        S.op("dve", lambda e: e.tensor_tensor(out=t1[sl], in0=bank(pb), in1=cos, op=ALU.mult), r=[ckeys[0]], w=[("ps", pb), ("t1", sl)])

        def stage_b():
            S.op("pe", lambda e: e.matmul(bank(rb), lhsT=permm, rhs=qraw[sl], start=True, stop=True), r=["permm", ("qraw", sl)], w=[("ps", rb)])
            S.op("dve", lambda e: e.tensor_tensor(out=t2[sl], in0=bank(rb), in1=sin, op=ALU.mult), r=[ckeys[1]], w=[("ps", rb), ("t2", sl)])
            S.op(aeng, lambda e: e.tensor_tensor(out=dst, in0=t1[sl], in1=t2[sl], op=ALU.add), r=[("t1", sl), ("t2", sl)], w=[dstkey])
        return stage_b

    pend = []

    def flush():
        while pend:
            pend.pop(0)()

    M = Mem(PBASE)
    wst = [M.f32(1024).rearrange("p (c f) -> p c f", c=8) for _ in range(1)]
    wbf = M.bf(8 * 512).rearrange("p (c f) -> p c f", c=8)
    cosT = [M.f32(512) for _ in range(2)]
    sinT = [M.f32(512) for _ in range(2)]
    qraw = [M.bf(512) for _ in range(2)]
    t1 = [M.f32(512) for _ in range(2)]
    t2 = [M.f32(512) for _ in range(2)]
    qT = M.bf(SEQ)
    kT = M.bf(SEQ)
    vT = M.bf(SEQ)
    gTs = [M.bf(SEQ) for _ in range(2)]
    Vtok = [M.bf(32 * 192) for _ in range(2)]
    PT = [M.bf(1024) for _ in range(4)]
    accs = [M.f32(SEQ), M.f32(SEQ)]
    PHASEA_END = M.off

    for vs in range(2):
        S.op("pool", lambda e, vs=vs: e.memset(sub(Vtok[vs], (64, [[192, 32], [1, 64]])), 1.0), w=[("Vtok", vs)])

    def acc_view(acc, b, g):
        if b == 0:
            return acc[:, 512 * g:512 * g + 512], [("acc", g)], None
        if b == 1:
            r4, m0 = g // 2, 4 * (g % 2)
            s = r4 + 512 * m0
            return acc[:, s:s + 2045:4], [("acc", m0 + i) for i in range(4)], None
        return sub(acc, (2 * g, [[1, 2], [16, 256]])), [("acc", i) for i in range(8)], "p (r l) -> p r l"

    load_w(wst, wbf, 0, 512, "wbf", win_v)
    tails = []
    for p in range(4):
        gT = gTs[p % 2]
        gkey = "gT%d" % (p % 2)
        for tt in range(8):
            tsl = tt % 2
            S.op("sp", lambda e, tt=tt, tsl=tsl: e.dma_start(out=cosT[tsl], in_=ropeA_d[0, :, 512 * tt:512 * tt + 512]), w=[("cos", tsl)], dma=True)
            S.op("sp", lambda e, tt=tt, tsl=tsl: e.dma_start(out=sinT[tsl], in_=ropeA_d[1, :, 512 * tt:512 * tt + 512]), w=[("sin", tsl)], dma=True)
            for f in range(4):
                pb = cnt["pb"] % 4
                cnt["pb"] += 1
                for c in range(8):
                    S.op("pe", lambda e, c=c, f=f, tt=tt, pb=pb: e.matmul(bank(pb), lhsT=wbf[:, c, 128 * f:128 * f + 128],
                                                                          rhs=hT[:, c, 512 * tt:512 * tt + 512], start=(c == 0), stop=(c == 7)),
                         r=[("wbf", f), ("hT", tt)], w=[("ps", pb)])
                flush()
                tsel = slice(512 * tt, 512 * tt + 512)
                if f < 2:
                    pend.append(rope(pb, (qT if f == 0 else kT)[:, tsel], ("qT" if f == 0 else "kT", tt), cosT[tsl], sinT[tsl],
                                     [("cos", tsl), ("sin", tsl)], qraw, t1, t2))
                elif f == 2:
                    S.op("dve", lambda e, pb=pb, tsel=tsel: e.tensor_copy(out=vT[:, tsel], in_=bank(pb)), w=[("ps", pb), ("vT", tt)])
                else:
                    S.op("act", lambda e, pb=pb, tsel=tsel, gT=gT: e.activation(out=gT[:, tsel], in_=bank(pb), func=AF.Silu), w=[("ps", pb), (gkey, tt)])
            if tt == 2:
                while tails:
                    tails.pop(0)()
        flush()
        if p == 0:
            dump("qT0", qT, [128, SEQ], BF16, [("qT", i) for i in range(8)])
            dump("kT0", kT, [128, SEQ], BF16, [("kT", i) for i in range(8)])
            dump("vT0", vT, [128, SEQ], BF16, [("vT", i) for i in range(8)])
            dump("gT0", gT, [128, SEQ], BF16, [(gkey, i) for i in range(8)])
            if stop_after == "proj0":
                S.emit(sems, dsem)
                return nc, dump_list
        if p < 3:
            load_w(wst, wbf, 512 * (p + 1), 512, "wbf", win_v)
        else:
            load_w(wst, wbf, 2048, 512, "wbf", win_v)

        def vtrans_ops(b, vs, tbs):
            ops = []
            for g8 in range(4):
                tb = tbs[g8]
                grp = []
                for j in range(8):
                    n = 8 * g8 + j
                    grp.append(("pe", lambda e, j=j, n=n, tb=tb, b=b: e.transpose(out=bankbf(tb)[:, 128 * j:128 * j + 128], in_=_tok(vT, b, n), identity=ident),
                                _tok_keys("vT", b, n) + ["ident"], [("ps", tb)]))
                grp.append(("dve", lambda e, g8=g8, tb=tb, vs=vs: e.tensor_copy(out=sub(Vtok[vs], (192 * 8 * g8, [[192, 8], [128, 2], [1, 64]])),
                                                                          in_=bankbf(tb).rearrange("p (n h d) -> p n h d", n=8, h=2)),
                            [], [("ps", tb), ("Vtok", vs)]))
                ops.append(grp)
            return ops

        def emit_ops(lst):
            for (eng, fn, r, w) in lst:
                S.op(eng, fn, r=r, w=w)

        groups = []
        for b in range(3):
            vs = b % 2
            for h in range(2):
                for g in range(8):
                    groups.append((b, h, g, vs))
        vt_pending = {}
        emit_ops([o for grp in vtrans_ops(0, 0, (1, 3, 5, 1)) for o in grp])

        def s_ops(i):
            b, h, g, vs = groups[i]
            r0 = 64 * h
            sg = i % 3
            Sps = psum[:, 1024 * sg:1024 * sg + 1024]
            for j in range(4):
                kb = 4 * g + j
                last = (kb % BPC[b] == BPC[b] - 1)
                sb_ = ("ps", 2 * sg + j // 2)
                if not last:
                    S.op("pe", lambda e, j=j, kb=kb, Sps=Sps, b=b, r0=r0: e.matmul(Sps[:, 256 * j:256 * j + 256], lhsT=_tok(kT[r0:r0 + 64], b, kb),
                                                                            rhs=_tok2(qT[r0:r0 + 64], b, kb, 2), start=True, stop=True),
                         r=_tok_keys("kT", b, kb) + _tok_keys("qT", b, kb) + _tok_keys("qT", b, kb + 1), w=[sb_])
                else:
                    S.op("pe", lambda e, j=j, kb=kb, Sps=Sps, b=b, r0=r0: e.matmul(Sps[:, 256 * j:256 * j + 128], lhsT=_tok(kT[r0:r0 + 64], b, kb),
                                                                            rhs=_tok(qT[r0:r0 + 64], b, kb), start=True, stop=True),
                         r=_tok_keys("kT", b, kb) + _tok_keys("qT", b, kb), w=[sb_])

        def ew_ops(i):
            b, h, g, vs = groups[i]
            sg = i % 3
            pt = i % 4
            Sps = psum[:, 1024 * sg:1024 * sg + 1024]
            S.op("act", lambda e, Sps=Sps, pt=pt: e.activation(out=PT[pt], in_=Sps, func=AF.Exp, scale=0.125),
                 w=[("ps", 2 * sg), ("ps", 2 * sg + 1), ("PT", pt)])
            meng = "dve"
            S.op(meng, lambda e, pt=pt: e.tensor_tensor(out=PT[pt], in0=PT[pt], in1=amask, op=ALU.mult), r=["amask", ("PT", pt)], w=[("PT", pt)])

        def pv_ops(i):
            b, h, g, vs = groups[i]
            vc0 = 0 if h == 0 else 64
            pt = i % 4
            ob = 6 + i % 2
            acc = accs[h]
            n0 = 4 * g
            started = False

            def vaug(n):
                return Vtok[vs][:, 192 * n + vc0:192 * n + vc0 + 128]

            def mm(o0, ncols, n, rhs, keys, stop):
                nonlocal started
                st = not started
                started = True
                S.op("pe", lambda e: e.matmul(bank(ob)[:, o0:o0 + ncols], lhsT=vaug(n), rhs=rhs, start=st, stop=stop, skip_group_check=True),
                     r=keys + [("Vtok", vs)], w=[("ps", ob)])

            if n0 % BPC[b] != 0:
                ptp = (i - 1) % 4
                mm(0, 128, n0 - 1, PT[ptp][:, 896:1024], [("PT", ptp)], False)
            for j in range(4):
                kb = n0 + j
                last = (kb % BPC[b] == BPC[b] - 1)
                if (not last) and j < 3:
                    mm(128 * j, 256, kb, PT[pt][:, 256 * j:256 * j + 256], [("PT", pt)], False)
                else:
                    mm(128 * j, 128, kb, PT[pt][:, 256 * j:256 * j + 128], [("PT", pt)], j == 3)
            view, akeys, rr = acc_view(acc, b, g)
            akeys = [(k[0] + str(h), k[1]) for k in akeys]
            src = bank(ob) if rr is None else bank(ob).rearrange(rr, r=2)
            if b == 0:
                S.op("act", lambda e, view=view, src=src: e.activation(out=view, in_=src, func=AF.Copy), w=[("ps", ob)] + akeys)
            else:
                S.op("dve", lambda e, view=view, src=src: e.tensor_tensor(out=view, in0=src, in1=view, op=ALU.add),
                     r=akeys, w=[("ps", ob)] + akeys)

        def den_ops(h):
            drow = 64 if h == 0 else 0
            ak = [("acc" + str(h), i) for i in range(8)]
            S.op("pool", lambda e, h=h, drow=drow: e.dma_start(out=dn_d[h:h + 1, :], in_=accs[h][drow:drow + 1, :]), r=ak, w=[("dn", h)], dma=True)
            S.op("pool", lambda e, h=h: e.dma_start(out=dtile[h], in_=dn_d[h].rearrange("(p j) -> p j", j=32)), r=[("dn", h)], w=[("dt", h)], dma=True)
            S.op("dve", lambda e, h=h: e.reciprocal(out=dtile[h], in_=dtile[h]), r=[("dt", h)], w=[("dt", h)])
            S.op("pool", lambda e, h=h: e.dma_start(out=rn_d[h].rearrange("(p j) -> p j", j=32), in_=dtile[h]), r=[("dt", h)], w=[("rn", h)], dma=True)

        NG = len(groups)
        LOOK = 3
        for i in range(min(LOOK, NG)):
            s_ops(i)
        for i in range(min(LOOK - 1, NG)):
            ew_ops(i)
        for i in range(NG):
            b, h, g, vs = groups[i]
            pv_ops(i)
            if h == 1 and b < 2 and g % 2 == 0:
                if g == 0:
                    vt_pending[b + 1] = vtrans_ops(b + 1, (b + 1) % 2, tuple(2 * ((i + 2 * k) % 3) + 1 for k in range(4)))
                emit_ops(vt_pending[b + 1][g // 2])
            if i + LOOK < NG:
                s_ops(i + LOOK)
            if i + LOOK - 1 < NG:
                ew_ops(i + LOOK - 1)
            if b == 2 and g == 7:
                den_ops(h)
        if p == 0:
            dump("acc0", accs[0], [128, SEQ], F32, [("acc0", i) for i in range(8)])
            dump("acc1", accs[1], [128, SEQ], F32, [("acc1", i) for i in range(8)])
        for h in range(2):
            o = 1 - h
            rows = slice(0, 64) if h == 0 else slice(64, 128)
            ok = [("acc" + str(o), i) for i in range(8)]
            S.op("pool", lambda e, h=h, o=o, rows=rows: e.dma_start(out=accs[o][rows, :], in_=rn_d[h:h + 1, :].to_broadcast([64, SEQ])),
                 r=[("rn", h)], w=ok, dma=True)
        def make_tail(p, gT, gkey):
            def tail():
                for half in range(2):
                    cs = slice(2048 * half, 2048 * half + 2048)
                    k0 = [("acc0", 4 * half + i) for i in range(4)]
                    k1 = [("acc1", 4 * half + i) for i in range(4)]
                    gk = [(gkey, 4 * half + i) for i in range(4)]
                    S.op("dve", lambda e, cs=cs: e.tensor_tensor(out=accs[0][:, cs], in0=accs[0][:, cs], in1=accs[1][:, cs], op=ALU.mult), r=k0 + k1, w=k0)
                    S.op("dve", lambda e, cs=cs: e.tensor_tensor(out=gT[:, cs], in0=accs[0][:, cs], in1=gT[:, cs], op=ALU.mult), r=k0 + gk, w=gk)
                    S.op("sp", lambda e, cs=cs: e.dma_start(out=mix_d[p][:, cs], in_=gT[:, cs]), r=gk, w=[("mixd", p)], dma=True)
            return tail

        tails.append(make_tail(p, gT, gkey))
        if p == 3 or stop_after == "attn0":
            while tails:
                tails.pop(0)()
        if p == 0:
            dump("mixT0", gT, [128, SEQ], BF16, [(gkey, i) for i in range(8)])
            if stop_after == "attn0":
                S.emit(sems, dsem)
                return nc, dump_list
    S.barrier()
    for pp in range(4):
        dump("mixd%d" % pp, mix_d[pp], [128, SEQ], BF16, [("mixd", pp)])
    if stop_after == "phaseA":
        S.emit(sems, dsem)
        return nc, dump_list
    M = Mem(PBASE)
    wstB = [M.f32(1024).rearrange("p (c f) -> p c f", c=8) for _ in range(2)]
    wbfB = M.bf(8 * 512).rearrange("p (c f) -> p c f", c=8)
    TB = M.off
    tabs = [[M.f32(512) for _ in range(2)] for _ in range(4)]
    qrawB = [M.bf(512) for _ in range(2)]
    t1B = [M.f32(512) for _ in range(2)]
    t2B = [M.f32(512) for _ in range(2)]
    grt = [M.bf(4 * 512).rearrange("p (f t) -> p f t", f=4) for _ in range(2)]
    TB_END = M.off
    qrT = [M.bf(SEQ) for _ in range(2)]
    krT = [M.bf(SEQ) for _ in range(2)]
    Vr = M.bf(32 * 512).rearrange("p (n v) -> p n v", n=32)
    WO_OFF = M.off
    wo = M.bf(8 * DM).rearrange("p (c f) -> p c f", c=8)

    for tt in range(8):
        for ft in range(2):
            tsl = (2 * tt + ft) % 2
            for k4 in range(4):
                S.op("sp", lambda e, tt=tt, ft=ft, k4=k4, tsl=tsl: e.dma_start(out=tabs[k4][tsl], in_=ropeR_d[4 * ft + k4, :, 512 * tt:512 * tt + 512]),
                     w=[("tab", k4, tsl)], dma=True)
            for qk in range(2):
                f = 2 * qk + ft
                pb = cnt["pb"] % 4
                cnt["pb"] += 1
                for c in range(8):
                    S.op("pe", lambda e, c=c, f=f, tt=tt, pb=pb: e.matmul(bank(pb), lhsT=wbf[:, c, 128 * f:128 * f + 128],
                                                                          rhs=hT[:, c, 512 * tt:512 * tt + 512], start=(c == 0), stop=(c == 7)),
                         r=[("wbf", f), ("hT", tt)], w=[("ps", pb)])
                flush()
                dst = (qrT if qk == 0 else krT)[ft][:, 512 * tt:512 * tt + 512]
                pend.append(rope(pb, dst, ("qrT" if qk == 0 else "krT", ft, tt), tabs[2 * qk][tsl], tabs[2 * qk + 1][tsl],
                                 [("tab", 2 * qk, tsl), ("tab", 2 * qk + 1, tsl)], qrawB, t1B, t2B))
    flush()
    dump("qrT0", qrT[0], [128, SEQ], BF16, [("qrT", 0, i) for i in range(8)])
    dump("krT0", krT[0], [128, SEQ], BF16, [("krT", 0, i) for i in range(8)])
    if stop_after == "B1":
        S.emit(sems, dsem)
        return nc, dump_list
    load_w(wstB, wbfB, 2560, 512, "wbfB", win_v, extra_w=[("wbf", f) for f in range(4)])
    for n in range(32):
        pb = cnt["pb"] % 4
        cnt["pb"] += 1
        for c in range(8):
            S.op("pe", lambda e, c=c, n=n, pb=pb: e.matmul(bank(pb), lhsT=hT[:, c, 128 * n:128 * n + 128], rhs=wbfB[:, c, :], start=(c == 0), stop=(c == 7)),
                 r=[("wbfB", i) for i in range(4)] + [("hT", n // 4)], w=[("ps", pb)])
        if n % 2 == 0:
            S.op("dve", lambda e, n=n, pb=pb: e.tensor_copy(out=Vr[:, n, :], in_=bank(pb)), w=[("ps", pb), ("Vr", n)])
        else:
            S.op("act", lambda e, n=n, pb=pb: e.activation(out=Vr[:, n, :], in_=bank(pb), func=AF.Copy), w=[("ps", pb), ("Vr", n)])
    load_w(wstB, wbfB, 3072, 512, "wbfB", win_v)
    for tt in range(8):
        gs = tt % 2
        for ft in range(4):
            pb = cnt["pb"] % 4
            cnt["pb"] += 1
            for c in range(8):
                S.op("pe", lambda e, c=c, ft=ft, tt=tt, pb=pb: e.matmul(bank(pb), lhsT=wbfB[:, c, 128 * ft:128 * ft + 128],
                                                                        rhs=hT[:, c, 512 * tt:512 * tt + 512], start=(c == 0), stop=(c == 7)),
                     r=[("wbfB", ft), ("hT", tt)], w=[("ps", pb)])
            S.op("act", lambda e, ft=ft, pb=pb, gs=gs: e.activation(out=grt[gs][:, ft, :], in_=bank(pb), func=AF.Silu), w=[("ps", pb), ("grt", gs, ft)])
            S.op("dve", lambda e, ft=ft, gs=gs: e.tensor_scalar(out=grt[gs][:, ft, :], in0=grt[gs][:, ft, :], scalar1=gn[:, ft:ft + 1], scalar2=None, op0=ALU.mult),
                 r=[("grt", gs, ft), "gn"], w=[("grt", gs, ft)])
        S.op("sp", lambda e, tt=tt, gs=gs: e.dma_start(out=grs_d[:, :, 512 * tt:512 * tt + 512].rearrange("f p t -> p f t"), in_=grt[gs]),
             r=[("grt", gs, ft) for ft in range(4)], w=[("grs", tt)], dma=True)
    if stop_after == "B3":
        S.emit(sems, dsem)
        return nc, dump_list
    S.barrier()

    M = Mem(TB)
    Ktok = [M.bf(256) for _ in range(2)]
    sT = [M.bf(512) for _ in range(2)]
    onb = [M.bf(512) for _ in range(2)]
    grl = [M.bf(4 * 512).rearrange("p (f t) -> p f t", f=4) for _ in range(2)]
    mixst = [M.bf(4 * 512).rearrange("p (f t) -> p f t", f=4) for _ in range(2)]
    assert M.off <= TB_END - 256
    mixA = Mem(0).bf(8 * SEQ).rearrange("p (c t) -> p c t", c=8)
    def mixa_load(c):
        S.op("sp", lambda e: e.dma_start(out=mixA[:, c, :], in_=mix_d[c]), r=[("mixd", c)], w=[("mixA", c)] + [("hT", i) for i in range(8)], dma=True)
    dump("mixa0e", mixA[:, 0:4, 0:512], [128, 4, 512], BF16, [("mixA", h) for h in range(4)])
    for pp in range(4):
        dump("mixe%d" % pp, mix_d[pp], [128, SEQ], BF16, [("mixd", pp)])
    def wo_chunk(j):
        sl = cnt["w"] % len(wstB)
        cnt["w"] += 1
        S.op("sp", lambda e: e.dma_start(out=wstB[sl], in_=wout_v[:, :, 128 * j:128 * j + 128]), w=[("wst", sl)], dma=True)
        S.op("pool", lambda e: e.tensor_copy(out=wo[:, :, 128 * j:128 * j + 128], in_=wstB[sl]), r=[("wst", sl)], w=[("wo", j)])
    S.op("dve", lambda e: e.memset(state, 0.0), w=["state"])
    nbias = bnst[:, 0:4]
    bnst2 = [bnst, Mem(TB_END - 256).f32(24)]
    bnmv2 = [bnmv, Mem(TB_END - 128).f32(8)]
    rstd2 = [rstd4, Mem(TB_END - 64).f32(4)]
    nb2 = [Mem(TB_END - 32).f32(4), Mem(TB_END - 16).f32(4)]

    OB = (2, 3, 6)

    def r_1(n):
        tsel = slice(128 * n, 128 * n + 128)
        ks = n % 2
        grp, sub4 = n // 4, n % 4
        gl = grp % 2
        if sub4 == 0:
            S.op("sp", lambda e: e.dma_start(out=grl[gl], in_=grs_d[:, :, 512 * grp:512 * grp + 512].rearrange("f p t -> p f t")),
                 r=[("grs", grp)], w=[("grl", gl)], dma=True)
        for ft in range(2):
            S.op("pe", lambda e, ft=ft: e.transpose(out=bankbf(0)[:, 128 * ft:128 * ft + 128], in_=krT[ft][:, tsel], identity=ident),
                 r=[("krT", ft, n // 4), "ident"], w=[("ps", 0)])
        S.op("act", lambda e: e.activation(out=Ktok[ks], in_=bankbf(0)[:, 0:256], func=AF.Copy), w=[("ps", 0), ("Ktok", ks)])
        S.op("pool", lambda e: e.tensor_tensor(out=Ktok[ks], in0=Ktok[ks], in1=gck, op=ALU.mult), r=["gck", ("Ktok", ks)], w=[("Ktok", ks)])
        for h in (0, 2, 1, 3):
            r0 = 64 * (h % 2)
            sbk = 1 if h % 2 == 0 else 7
            S.op("pe", lambda e, h=h, r0=r0, sbk=sbk: e.matmul(bank(sbk)[:, 128 * (h // 2):128 * (h // 2) + 128], lhsT=krT[h // 2][r0:r0 + 64, tsel],
                                                          rhs=qrT[h // 2][r0:r0 + 64, tsel], start=True, stop=True),
                 r=[("krT", h // 2, n // 4), ("qrT", h // 2, n // 4)], w=[("ps", sbk)])
        for hp, sbk in ((0, 1), (1, 7)):
            S.op("dve", lambda e, hp=hp, sbk=sbk: e.tensor_tensor(out=sT[ks][:, 256 * hp:256 * hp + 256], in0=bank(sbk)[:, 0:256], in1=rmask[:, 0:256], op=ALU.mult),
                 r=["rmask"], w=[("ps", sbk), ("sT", ks)])

    def r_2(n):
        tsel = slice(128 * n, 128 * n + 128)
        ks = n % 2
        ob = OB[n % 3]
        for h in range(4):
            r0 = 64 * (h % 2)
            S.op("pe", lambda e, h=h, r0=r0: e.matmul(bank(4)[r0:r0 + 64, 128 * (h // 2):128 * (h // 2) + 128], lhsT=Ktok[ks][:, 64 * h:64 * h + 64],
                                                 rhs=Vr[:, n, 128 * h:128 * h + 128], start=True, stop=True),
                 r=[("Ktok", ks), ("Vr", n)], w=[("ps", 4)])
        for h in range(4):
            r0 = 64 * (h % 2)
            sblk = 2 * (h % 2) + h // 2
            S.op("pe", lambda e, h=h, sblk=sblk: e.matmul(bank(ob)[:, 128 * h:128 * h + 128], lhsT=sT[ks][:, 128 * sblk:128 * sblk + 128],
                                                     rhs=Vr[:, n, 128 * h:128 * h + 128], start=True, stop=(n == 0)),
                 r=[("sT", ks), ("Vr", n)], w=[("ps", ob)])
            if n > 0:
                S.op("pe", lambda e, h=h, r0=r0: e.matmul(bank(ob)[:, 128 * h:128 * h + 128], lhsT=qrT[h // 2][r0:r0 + 64, tsel],
                                                     rhs=state_bf[r0:r0 + 64, 128 * (h // 2):128 * (h // 2) + 128], start=False, stop=True),
                     r=[("qrT", h // 2, n // 4), "state_bf"], w=[("ps", ob)])
        for cb in range(2):
            S.op("dve", lambda e, cb=cb: e.scalar_tensor_tensor(out=state[:, 128 * cb:128 * cb + 128], in0=state[:, 128 * cb:128 * cb + 128],
                                                                scalar=gsc[:, cb:cb + 1], in1=bank(4)[:, 128 * cb:128 * cb + 128], op0=ALU.mult, op1=ALU.add),
                 r=["state", "gsc"], w=["state", ("ps", 4)])
        S.op("act", lambda e: e.activation(out=state_bf, in_=state, func=AF.Copy), r=["state"], w=["state_bf"])

    def r_3(n):
        ks = n % 2
        ob = OB[n % 3]
        for h in range(4):
            S.op("dve", lambda e, h=h: e.bn_stats(out=bnst2[ks][:, 6 * h:6 * h + 6], in_=bank(ob)[:, 128 * h:128 * h + 128]), w=[("ps", ob), ("bnst", ks, h)])
        for h in range(4):
            S.op("dve", lambda e, h=h: e.bn_aggr(out=bnmv2[ks][:, 2 * h:2 * h + 2], in_=bnst2[ks][:, 6 * h:6 * h + 6]), r=[("bnst", ks, h)], w=[("bnmv", ks, h)])
        S.op("pool", lambda e: e.tensor_scalar(out=rstd2[ks], in0=bnmv2[ks][:, 1:8:2], scalar1=1e-5, scalar2=None, op0=ALU.add), r=[("bnmv", ks, h) for h in range(4)], w=[("rstd", ks)])
        S.op("pool", lambda e: e.tensor_tensor(out=rstd2[ks], in0=rstd2[ks], in1=mhalf, op=ALU.pow), r=[("rstd", ks), "mhalf"], w=[("rstd", ks)])
        S.op("pool", lambda e: e.tensor_tensor(out=nb2[ks], in0=bnmv2[ks][:, 0:8:2], in1=rstd2[ks], op=ALU.mult), r=[("bnmv", ks, h) for h in range(4)] + [("rstd", ks)], w=[("nb", ks)])
        S.op("pool", lambda e: e.tensor_scalar(out=nb2[ks], in0=nb2[ks], scalar1=-1.0, scalar2=None, op0=ALU.mult), r=[("nb", ks)], w=[("nb", ks)])

    def r_4(n):
        ks = n % 2
        ob = OB[n % 3]
        for h in range(4):
            S.op("act", lambda e, h=h: e.activation(out=onb[ks][:, 128 * h:128 * h + 128], in_=bank(ob)[:, 128 * h:128 * h + 128], func=AF.Identity,
                                                    bias=nb2[ks][:, h:h + 1], scale=rstd2[ks][:, h:h + 1]),
                 r=[("nb", ks), ("rstd", ks)], w=[("ps", ob), ("onb", ks, h)])

    def r_5(n):
        ks = n % 2
        grp, sub4 = n // 4, n % 4
        gl = grp % 2
        for h in range(4):
            S.op("pe", lambda e, h=h: e.transpose(out=bankbf(5)[:, 128 * h:128 * h + 128], in_=onb[ks][:, 128 * h:128 * h + 128], identity=ident),
                 r=[("onb", ks, h), "ident"], w=[("ps", 5)])
        S.op("dve", lambda e: e.tensor_tensor(out=mixA[:, 4:8, 128 * n:128 * n + 128], in0=bankbf(5)[:, 0:512].rearrange("p (h t) -> p h t", h=4),
                                              in1=grl[gl][:, :, 128 * sub4:128 * sub4 + 128], op=ALU.mult),
             r=[("grl", gl)], w=[("ps", 5)] + [("mixA", 4 + h) for h in range(4)])

    stages = ((r_1, 0), (r_2, 1), (r_3, 2), (r_4, 3), (r_5, 4))
    for step in range(32 + 4):
        for fn, sk in stages:
            if 0 <= step - sk < 32:
                fn(step - sk)
        if step % 4 == 2 and step // 4 < 8:
            wo_chunk(step // 4)
        if step % 4 == 1 and step // 4 < 4:
            mixa_load(step // 4)
    dump("mixr0", mixA[:, 4:8, 0:512], [128, 4, 512], BF16, [("mixA", 4 + h) for h in range(4)])
    dump("mixa0", mixA[:, 0:4, 0:512], [128, 4, 512], BF16, [("mixA", h) for h in range(4)])
    dump("wo", wo, [128, 8, DM], BF16, [("wo", j) for j in range(8)])
    if stop_after == "ret":
        S.emit(sems, dsem)
        return nc, dump_list
    S.barrier()

    M = Mem(PBASE + 16384)
    xr = [M.f32(DM) for _ in range(8)]
    yb = [M.f32(DM) for _ in range(4)]
    junk2s = [M.bf(DM) for _ in range(4)]
    fg = M.f32(DM)
    assert M.off <= WO_OFF
    S.op("sp", lambda e: e.dma_start(out=fg, in_=fg_d), w=["fg"], dma=True)
    S.op("dve", lambda e: e.memset(st_ss, 0.0), w=["st_ss"])
    def c_a(tt):
        sl = tt % 4
        xsl = tt % 8
        S.op("sp", lambda e: e.dma_start(out=xr[xsl], in_=x_d[128 * tt:128 * tt + 128, :]), w=[("xr", xsl)], dma=True)
        for nh in range(2):
            pb = (2 * tt + nh) % 8
            for c in range(8):
                S.op("pe", lambda e, c=c, nh=nh, pb=pb: e.matmul(bank(pb), lhsT=mixA[:, c, 128 * tt:128 * tt + 128], rhs=wo[:, c, 512 * nh:512 * nh + 512],
                                                                 start=(c == 0), stop=(c == 7)),
                     r=[("mixA", c)] + [("wo", 4 * nh + i) for i in range(4)], w=[("ps", pb)])
            S.op("dve", lambda e, nh=nh, pb=pb: e.tensor_tensor(out=yb[sl][:, 512 * nh:512 * nh + 512], in0=bank(pb), in1=xr[xsl][:, 512 * nh:512 * nh + 512], op=ALU.add),
                 r=[("xr", xsl)], w=[("ps", pb), ("yb", sl, nh)])
        S.op("act", lambda e: e.activation(out=junk2s[tt % 4], in_=yb[sl], func=AF.Square, accum_out=st_ss[:, tt:tt + 1]),
             r=[("yb", sl, 0), ("yb", sl, 1), "st_ss"], w=[("junk2", tt % 4), ("ss", tt)])

    def c_b(tt):
        S.op("dve", lambda e: e.tensor_scalar(out=st_ms[:, tt:tt + 1], in0=st_ss[:, tt:tt + 1], scalar1=1.0 / DM, scalar2=1e-6,
                                              op0=ALU.mult, op1=ALU.add), r=[("ss", tt)], w=[("ms", tt)])
        S.op("pool", lambda e: e.tensor_tensor(out=st_rs[:, tt:tt + 1], in0=st_ms[:, tt:tt + 1], in1=mhalf[:, 0:1], op=ALU.pow),
             r=[("ms", tt), "mhalf"], w=[("rs", tt)])

    def c_c(tt):
        sl = tt % 4
        S.op("dve", lambda e: e.scalar_tensor_tensor(out=yb[sl], in0=yb[sl], scalar=st_rs[:, tt:tt + 1], in1=fg, op0=ALU.mult, op1=ALU.mult),
             r=[("yb", sl, 0), ("yb", sl, 1), ("rs", tt), "fg"], w=[("yb", sl, 0), ("yb", sl, 1)])
        S.op("pool", lambda e: e.dma_start(out=out_d[128 * tt:128 * tt + 128, :], in_=yb[sl]), r=[("yb", sl, 0), ("yb", sl, 1)], w=[("out", tt)], dma=True)

    for step in range(32 + 2):
        if step < 32:
            c_a(step)
        if 1 <= step < 33:
            c_b(step - 1)
        if step >= 2:
            c_c(step - 2)
    S.emit(sems, dsem)
    return nc, dump_list


_CACHE = {}


def _prep_inputs(x, norm_gain, w_in, ret_gn_gain, w_out, final_gain):
    c = _host_constants()
    w_in_p = np.ascontiguousarray(np.asarray(w_in, np.float32)[0][:, _win_cols()])
    shared = dict(c)
    shared["w_in"] = w_in_p
    shared["w_out"] = np.ascontiguousarray(np.asarray(w_out, np.float32)[0])
    shared["ng"] = np.ascontiguousarray(np.asarray(norm_gain, np.float32)[0].reshape(8, 128).T)
    shared["gn"] = np.ascontiguousarray(np.asarray(ret_gn_gain, np.float32)[0].reshape(4, 128).T)
    shared["fg"] = np.ascontiguousarray(np.broadcast_to(np.asarray(final_gain, np.float32)[None, :], (128, DM)))
    x = np.asarray(x, np.float32)
    return [dict(shared, x=np.ascontiguousarray(x[b])) for b in range(x.shape[0])]


def kernel(x, norm_gain, w_in, ret_gn_gain, w_out, final_gain):
    in_maps = _prep_inputs(x, norm_gain, w_in, ret_gn_gain, w_out, final_gain)
    nc, _ = build()
    res = run_bass_kernel_spmd(nc, in_maps, core_ids=list(range(NCORES)))
    return np.stack([np.asarray(r["out"], np.float32) for r in res.results], axis=0)
```

```python
import numpy as np
import ml_dtypes
import concourse.bass as bass
import concourse.mybir as mybir
from concourse.bass_utils import run_bass_kernel_spmd

dt = mybir.dt
F32 = dt.float32
BF16 = dt.bfloat16
AF = mybir.ActivationFunctionType
ALU = mybir.AluOpType
AX = mybir.AxisListType


class Sched:
    COMPUTE = ("pe", "act", "dve", "pool")
    EPOCH = 400
    STRICT = True

    def __init__(self, nc, dma_ring=None):
        self.nc = nc
        self.ops = []
        self.lastw = {}
        self.readers = {}
        self.last_on = {}
        self.extra = {}
        self.dma_live = []
        self.dma_ring = dma_ring or {"sp": 24, "pool": 12, "act": 6}

    def op(self, eng, fn, r=(), w=(), dma=False):
        i = len(self.ops)
        deps = set(self.extra.pop(eng, ()))
        for k in r:
            s = self.lastw.get(k)
            if s is not None:
                deps.add((s, 0))
        for k in w:
            s = self.lastw.get(k)
            if s is not None:
                deps.add((s, 1))
            for s in self.readers.get(k, ()):
                deps.add((s, 2))
        keep = set()
        for (s, kind) in deps:
            so = self.ops[s]
            if so["dma"] or dma:
                keep.add(s)
            elif so["eng"] == eng:
                if eng != "pe" and (kind == 0 or self.STRICT):
                    keep.add(s)
            else:
                keep.add(s)
        keep.discard(i)
        self.ops.append(dict(eng=eng, fn=fn, deps=sorted(keep), dma=dma, signal=dma))
        for s in keep:
            self.ops[s]["signal"] = True
        for k in w:
            self.lastw[k] = i
            self.readers[k] = []
        for k in r:
            lst = self.readers.setdefault(k, [])
            if not dma:
                lst[:] = [s for s in lst if self.ops[s]["dma"] or self.ops[s]["eng"] != eng]
            lst.append(i)
        self.last_on[eng] = i
        if dma:
            self.dma_live.append(i)
        return i

    def barrier(self):
        srcs = set(self.last_on.values()) | set(self.dma_live)
        self.dma_live = []
        for e in ("pe", "act", "dve", "pool", "sp"):
            self.extra.setdefault(e, set())
            for s in srcs:
                so = self.ops[s]
                if so["eng"] == e and not so["dma"]:
                    continue
                self.extra[e].add((s, 0))
                so["signal"] = True

    def emit(self, sems, dma_sems):
        nc = self.nc
        cnt = {e: 0 for e in self.COMPUTE}
        dcnt = {q: 0 for q in dma_sems}
        dval = {q: [0] * len(dma_sems[q]) for q in dma_sems}
        for o in self.ops:
            if o["dma"]:
                q = o["eng"]
                j = dcnt[q] % len(dma_sems[q])
                dcnt[q] += 1
                o["ring_prev"] = (dma_sems[q][j], dval[q][j]) if dval[q][j] > 0 else None
                dval[q][j] += 16
                o["done"] = (dma_sems[q][j], dval[q][j])
            elif o["signal"]:
                k = cnt[o["eng"]]
                cnt[o["eng"]] += 1
                o["done"] = (sems[o["eng"]][k // self.EPOCH], k % self.EPOCH + 1)
        by_eng = {}
        for o in self.ops:
            by_eng.setdefault(o["eng"], []).append(o)
        final_waits = []
        for q in dma_sems:
            for j, s in enumerate(dma_sems[q]):
                if dval[q][j] > 0:
                    final_waits.append((s, dval[q][j]))
        handles = {"pe": nc.tensor, "act": nc.scalar, "dve": nc.vector, "pool": nc.gpsimd, "sp": nc.sync}
        ops_all = self.ops

        def run(eng, e):
            waited = {}

            def wait(sem, val):
                key = id(sem)
                if waited.get(key, 0) < val:
                    e.wait_ge(sem, val)
                    waited[key] = val

            for o in by_eng.get(eng, []):
                for s in o["deps"]:
                    sem, val = ops_all[s]["done"]
                    wait(sem, val)
                if o["dma"] and o["ring_prev"] is not None:
                    wait(*o["ring_prev"])
                ins = o["fn"](e)
                if o["signal"]:
                    ins.then_inc(o["done"][0], 16 if o["dma"] else 1)
            if eng == "sp":
                for (s, v) in final_waits:
                    wait(s, v)

        with nc.Block() as block:
            @block.sync
            def _(e):
                run("sp", e)

            @block.tensor
            def _(e):
                run("pe", e)

            @block.scalar
            def _(e):
                run("act", e)

            @block.vector
            def _(e):
                run("dve", e)

            @block.gpsimd
            def _(e):
                run("pool", e)


SEQ = 4096
DM = 1024
NCORES = 8
THETA = 10000.0
_bf = ml_dtypes.bfloat16


def _host_constants():
    c = {}
    c["ident"] = np.eye(128, dtype=np.float32).astype(_bf)
    permm = np.zeros((128, 128), np.float32)
    for m in range(128):
        k = m + 32 if (m % 64) < 32 else m - 32
        permm[k, m] = 1.0
    c["permm"] = permm.astype(_bf)
    kk = np.arange(128)[:, None]
    qq = np.arange(128)[None, :]
    prev = (kk >= qq).astype(np.float32)
    own = (qq >= kk).astype(np.float32)
    c["amask"] = np.tile(np.concatenate([own, prev], 1), (1, 4)).astype(_bf)
    c["rmask"] = np.tile(own, (1, 4)).astype(_bf)
    pos = np.arange(SEQ, dtype=np.float64)
    j = np.arange(128) % 32
    sign = np.where((np.arange(128) % 64) >= 32, 1.0, -1.0)[:, None]
    inv = THETA ** (-(np.arange(32, dtype=np.float64)) / 32.0)
    ang = pos[None, :] * inv[j][:, None]
    c["ropeA"] = np.stack([np.cos(ang), np.sin(ang) * sign]).astype(np.float32)
    invr = 1.0 / (THETA ** np.linspace(0.0, 1.0, 32, dtype=np.float64))
    angr = pos[None, :] * invr[j][:, None]
    cosR = np.cos(angr)
    sinR = np.sin(angr) * sign
    lg = np.log1p(-(2.0 ** (-5.0 - np.arange(4, dtype=np.float64))))
    i_in = (np.arange(SEQ) % 128).astype(np.float64)
    ropeR = np.zeros((8, 128, SEQ), np.float32)
    for ft in range(2):
        for hp in range(2):
            h = 2 * ft + hp
            rows = slice(64 * hp, 64 * hp + 64)
            dq = np.exp((i_in + 1.0) * lg[h])[None, :]
            dk = (np.exp(-(i_in + 1.0) * lg[h]) / 8.0)[None, :]
            ropeR[ft * 4 + 0, rows] = cosR[rows] * dq
            ropeR[ft * 4 + 1, rows] = sinR[rows] * dq
            ropeR[ft * 4 + 2, rows] = cosR[rows] * dk
            ropeR[ft * 4 + 3, rows] = sinR[rows] * dk
    c["ropeR"] = ropeR
    gC = np.exp(128.0 * lg)
    gck = np.zeros((128, 256), np.float32)
    for h in range(4):
        gck[:, h * 64:(h + 1) * 64] = gC[h]
    c["gck"] = gck
    gsc = np.zeros((128, 2), np.float32)
    for cb in range(2):
        gsc[:64, cb] = gC[2 * cb]
        gsc[64:, cb] = gC[2 * cb + 1]
    c["gsc"] = gsc
    return c


def _win_cols():
    cols = []
    for p in range(4):
        for base in (0, 512, 1024, 1536):
            cols += list(range(base + 128 * p, base + 128 * p + 128))
    permh = [2 * i for i in range(32)] + [2 * i + 1 for i in range(32)]
    for base in (2048, 2304):
        for h in range(4):
            cols += [base + 64 * h + cc for cc in permh]
    cols += list(range(2560, 3584))
    return np.asarray(cols)


def _tok(buf, b, n):
    if b == 0:
        return buf[:, 128 * n:128 * n + 128]
    if b == 1:
        r, m = n // 8, n % 8
        s = r + 512 * m
        return buf[:, s:s + 509:4]
    r, m = n // 2, n % 2
    s = r + 2048 * m
    return buf[:, s:s + 2033:16]


def _tok2(buf, b, n, nblk):
    cntk = 128 * nblk
    if b == 0:
        return buf[:, 128 * n:128 * n + cntk]
    if b == 1:
        r, m = n // 8, n % 8
        s = r + 512 * m
        return buf[:, s:s + 4 * (cntk - 1) + 1:4]
    r, m = n // 2, n % 2
    s = r + 2048 * m
    return buf[:, s:s + 16 * (cntk - 1) + 1:16]


def _tok_keys(name, b, n):
    if b == 0:
        return [(name, n // 4)]
    if b == 1:
        return [(name, n % 8)]
    return [(name, 4 * (n % 2) + i) for i in range(4)]


BPC = (32, 8, 2)


def build(stop_after=None, dumps=()):
    nc = bass.Bass("TRN2", target_bir_lowering=False)
    from contextlib import ExitStack
    es = ExitStack()

    def din(name, shape, d=F32):
        return nc.dram_tensor(name, shape, d, kind="ExternalInput").ap()

    x_d = din("x", [SEQ, DM])
    win_d = din("w_in", [DM, 3584])
    wout_d = din("w_out", [DM, DM])
    ng_d = din("ng", [128, 8])
    gn_d = din("gn", [128, 4])
    fg_d = din("fg", [128, DM])
    ident_d = din("ident", [128, 128], BF16)
    permm_d = din("permm", [128, 128], BF16)
    amask_d = din("amask", [128, 1024], BF16)
    rmask_d = din("rmask", [128, 512], BF16)
    ropeA_d = din("ropeA", [2, 128, SEQ])
    ropeR_d = din("ropeR", [8, 128, SEQ])
    gck_d = din("gck", [128, 256])
    gsc_d = din("gsc", [128, 2])
    out_d = nc.dram_tensor("out", [SEQ, DM], F32, kind="ExternalOutput").ap()
    mix_d = nc.dram_tensor("mixs", [8, 128, SEQ], BF16, kind="Internal").ap()
    grs_d = nc.dram_tensor("grs", [4, 128, SEQ], BF16, kind="Internal").ap()
    dn_d = nc.dram_tensor("dns", [2, SEQ], F32, kind="Internal").ap()
    rn_d = nc.dram_tensor("rns", [2, SEQ], F32, kind="Internal").ap()
    win_v = win_d.rearrange("(c p) f -> p c f", p=128)
    wout_v = wout_d.rearrange("(c p) f -> p c f", p=128)

    ARENA_BYTES = 211968
    arena = es.enter_context(nc.sbuf_tensor("arena", [128, ARENA_BYTES // 4], F32))
    psum = es.enter_context(nc.psum_tensor("psum", [128, 4096], F32))
    sems = {e: [es.enter_context(nc.semaphore(f"s_{e}{i}")) for i in range(n)]
            for e, n in (("pe", 12), ("act", 6), ("dve", 12), ("pool", 4))}
    dsem = {q: [es.enter_context(nc.semaphore(f"d_{q}{i}")) for i in range(n)] for q, n in (("sp", 28), ("pool", 8))}
    S = Sched(nc)

    class Mem:
        def __init__(self, base):
            self.off = base

        def f32(self, n):
            o = self.off
            self.off += 4 * n
            assert self.off <= ARENA_BYTES, self.off
            return arena[:, o // 4:o // 4 + n]

        def bf(self, n):
            o = self.off
            self.off += 2 * n
            assert self.off % 4 == 0 and self.off <= ARENA_BYTES, self.off
            return arena[:, o // 4:o // 4 + n // 2].bitcast(BF16)

    def bank(i, n=512):
        return psum[:, 512 * i:512 * i + n]

    def bankbf(i):
        return psum[:, 512 * i:512 * i + 512].bitcast(BF16)

    dump_list = []

    def dump(name, ap, shape, d, keys):
        if name not in dumps:
            return
        t = nc.dram_tensor("dbg_" + name, shape, d, kind="ExternalOutput").ap()
        S.op("sp", lambda e, t=t, ap=ap: e.dma_start(out=t, in_=ap), r=keys, dma=True)
        dump_list.append(name)

    M = Mem(0)
    hT = M.bf(8 * SEQ).rearrange("p (c t) -> p c t", c=8)
    ident = M.bf(128)
    permm = M.bf(128)
    amask = M.bf(1024)
    rmask = M.bf(512)
    gck = M.f32(256)
    ng = M.f32(8)
    gn = M.f32(4)
    gsc = M.f32(2)
    st_ss = M.f32(32)
    st_ms = M.f32(32)
    st_rs = M.f32(32)
    state = M.f32(256)
    state_bf = M.bf(256)
    bnst = M.f32(24)
    bnmv = M.f32(8)
    rstd4 = M.f32(4)
    mhalf = M.f32(4)
    dtile = [M.f32(32) for _ in range(2)]
    PBASE = (M.off + 63) // 64 * 64

    S.op("sp", lambda e: e.dma_start(out=ident, in_=ident_d), w=["ident"], dma=True)
    def late_consts():
        S.op("sp", lambda e: e.dma_start(out=permm, in_=permm_d), w=["permm"], dma=True)
        S.op("sp", lambda e: e.dma_start(out=amask, in_=amask_d), w=["amask"], dma=True)
        S.op("sp", lambda e: e.dma_start(out=rmask, in_=rmask_d), w=["rmask"], dma=True)
        S.op("sp", lambda e: e.dma_start(out=gck, in_=gck_d), w=["gck"], dma=True)
        S.op("sp", lambda e: e.dma_start(out=ng, in_=ng_d), w=["ng"], dma=True)
        S.op("sp", lambda e: e.dma_start(out=gn, in_=gn_d), w=["gn"], dma=True)
        S.op("sp", lambda e: e.dma_start(out=gsc, in_=gsc_d), w=["gsc"], dma=True)
    S.op("dve", lambda e: e.memset(st_ss, 0.0), w=["st_ss"])
    S.op("dve", lambda e: e.memset(mhalf, -0.5), w=["mhalf"])

    M = Mem(PBASE)
    xs = [M.f32(DM) for _ in range(8)]
    junks = [M.bf(DM) for _ in range(4)]
    xn = [M.bf(DM) for _ in range(4)]
    def pro_a(tt):
        sl = tt % 8
        S.op("sp", lambda e: e.dma_start(out=xs[sl], in_=x_d[128 * tt:128 * tt + 128, :]), w=[("xs", sl)], dma=True)
        S.op("act", lambda e: e.activation(out=junks[tt % 4], in_=xs[sl], func=AF.Square, accum_out=st_ss[:, tt:tt + 1]),
             r=[("xs", sl), "st_ss"], w=[("junk", tt % 4), ("ss", tt)])
        S.op("dve", lambda e: e.tensor_scalar(out=st_ms[:, tt:tt + 1], in0=st_ss[:, tt:tt + 1], scalar1=1.0 / DM, scalar2=1e-6,
                                              op0=ALU.mult, op1=ALU.add), r=[("ss", tt)], w=[("ms", tt)])
        S.op("pool", lambda e: e.tensor_tensor(out=st_rs[:, tt:tt + 1], in0=st_ms[:, tt:tt + 1], in1=mhalf[:, 0:1], op=ALU.pow),
             r=[("ms", tt), "mhalf"], w=[("rs", tt)])

    def pro_b(tt):
        xsl = tt % 8
        sl = tt % 4
        pb = tt % 4
        S.op("act", lambda e: e.activation(out=xn[sl], in_=xs[xsl], func=AF.Copy, scale=st_rs[:, tt:tt + 1]),
             r=[("xs", xsl), ("rs", tt)], w=[("xn", sl)])
        for c in range(8):
            S.op("pe", lambda e, c=c: e.transpose(out=bankbf(pb)[:, 128 * c:128 * c + 128], in_=xn[sl][:, 128 * c:128 * c + 128], identity=ident),
                 r=[("xn", sl), "ident"], w=[("ps", pb)])
        S.op("dve", lambda e: e.tensor_copy(out=hT[:, :, 128 * tt:128 * tt + 128], in_=bankbf(pb).rearrange("p (c t) -> p c t", c=8)),
             w=[("ps", pb), ("hT", tt // 4)])

    for step in range(32 + 2):
        if step < 32:
            pro_a(step)
        if step == 5:
            late_consts()
        if step >= 2:
            pro_b(step - 2)
    dump("hT", hT, [128, 8, SEQ], BF16, [("hT", i) for i in range(8)])
    if stop_after == "prologue":
        S.emit(sems, dsem)
        return nc, dump_list

    S.barrier()

    cnt = {"w": 0, "rope": 0, "pb": 0, "vt": 0, "sg": 0, "pt": 0, "ob": 0}

    def sub(ap, extra):
        return bass.AP(ap.tensor, ap.offset + extra[0], [list(ap.ap[0])] + [list(x) for x in extra[1]])

    def load_w(wst, wbf, col0, ncols, tag, src, scale=True, extra_w=()):
        for j in range(ncols // 128):
            sl = cnt["w"] % len(wst)
            cnt["w"] += 1
            S.op("sp", lambda e, sl=sl, j=j: e.dma_start(out=wst[sl], in_=src[:, :, col0 + 128 * j:col0 + 128 * j + 128]),
                 w=[("wst", sl)] + list(extra_w), dma=True)
            if scale:
                S.op("pool", lambda e, sl=sl, j=j: e.tensor_tensor(out=wbf[:, :, 128 * j:128 * j + 128], in0=wst[sl],
                                                                    in1=ng.unsqueeze(2).to_broadcast([128, 8, 128]), op=ALU.mult),
                     r=[("wst", sl), "ng"], w=[(tag, j)] + list(extra_w))
            else:
                S.op("pool", lambda e, sl=sl, j=j: e.tensor_copy(out=wbf[:, :, 128 * j:128 * j + 128], in_=wst[sl]),
                     r=[("wst", sl)], w=[(tag, j)])

    def rope(pb, dst, dstkey, cos, sin, ckeys, qraw, t1, t2):
        sl = cnt["rope"] % 2
        cnt["rope"] += 1
        rb = 4 + sl
        aeng = "pool" if (cnt["rope"] // 2) % 2 == 0 else "dve"
        S.op("act", lambda e: e.activation(out=qraw[sl], in_=bank(pb), func=AF.Copy), w=[("ps", pb), ("qraw", sl)])
        S.op("dve", lambda e: e.tensor_tensor(out=t1[sl], in0=bank(pb), in1=cos, op=ALU.mult), r=[ckeys[0]], w=[("ps", pb), ("t1", sl)])

        def stage_b():
            S.op("pe", lambda e: e.matmul(bank(rb), lhsT=permm, rhs=qraw[sl], start=True, stop=True), r=["permm", ("qraw", sl)], w=[("ps", rb)])
            S.op("dve", lambda e: e.tensor_tensor(out=t2[sl], in0=bank(rb), in1=sin, op=ALU.mult), r=[ckeys[1]], w=[("ps", rb), ("t2", sl)])
            S.op(aeng, lambda e: e.tensor_tensor(out=dst, in0=t1[sl], in1=t2[sl], op=ALU.add), r=[("t1", sl), ("t2", sl)], w=[dstkey])
        return stage_b

    pend = []

    def flush():
        while pend:
            pend.pop(0)()

    M = Mem(PBASE)
    wst = [M.f32(1024).rearrange("p (c f) -> p c f", c=8) for _ in range(1)]
    wbf = M.bf(8 * 512).rearrange("p (c f) -> p c f", c=8)
    cosT = [M.f32(512) for _ in range(2)]
    sinT = [M.f32(512) for _ in range(2)]
    qraw = [M.bf(512) for _ in range(2)]
    t1 = [M.f32(512) for _ in range(2)]
    t2 = [M.f32(512) for _ in range(2)]
    qT = M.bf(SEQ)
    kT = M.bf(SEQ)
    vT = M.bf(SEQ)
    gTs = [M.bf(SEQ) for _ in range(2)]
    Vtok = [M.bf(32 * 192) for _ in range(2)]
    PT = [M.bf(1024) for _ in range(4)]
    accs = [M.f32(SEQ), M.f32(SEQ)]
    PHASEA_END = M.off

    for vs in range(2):
        S.op("pool", lambda e, vs=vs: e.memset(sub(Vtok[vs], (64, [[192, 32], [1, 64]])), 1.0), w=[("Vtok", vs)])

    def acc_view(acc, b, g):
        if b == 0:
            return acc[:, 512 * g:512 * g + 512], [("acc", g)], None
        if b == 1:
            r4, m0 = g // 2, 4 * (g % 2)
            s = r4 + 512 * m0
            return acc[:, s:s + 2045:4], [("acc", m0 + i) for i in range(4)], None
        return sub(acc, (2 * g, [[1, 2], [16, 256]])), [("acc", i) for i in range(8)], "p (r l) -> p r l"

    load_w(wst, wbf, 0, 512, "wbf", win_v)
    tails = []
    for p in range(4):
        gT = gTs[p % 2]
        gkey = "gT%d" % (p % 2)
        for tt in range(8):
            tsl = tt % 2
            S.op("sp", lambda e, tt=tt, tsl=tsl: e.dma_start(out=cosT[tsl], in_=ropeA_d[0, :, 512 * tt:512 * tt + 512]), w=[("cos", tsl)], dma=True)
            S.op("sp", lambda e, tt=tt, tsl=tsl: e.dma_start(out=sinT[tsl], in_=ropeA_d[1, :, 512 * tt:512 * tt + 512]), w=[("sin", tsl)], dma=True)
            for f in range(4):
                pb = cnt["pb"] % 4
                cnt["pb"] += 1
                for c in range(8):
                    S.op("pe", lambda e, c=c, f=f, tt=tt, pb=pb: e.matmul(bank(pb), lhsT=wbf[:, c, 128 * f:128 * f + 128],
                                                                          rhs=hT[:, c, 512 * tt:512 * tt + 512], start=(c == 0), stop=(c == 7)),
                         r=[("wbf", f), ("hT", tt)], w=[("ps", pb)])
                flush()
                tsel = slice(512 * tt, 512 * tt + 512)
                if f < 2:
                    pend.append(rope(pb, (qT if f == 0 else kT)[:, tsel], ("qT" if f == 0 else "kT", tt), cosT[tsl], sinT[tsl],
                                     [("cos", tsl), ("sin", tsl)], qraw, t1, t2))
                elif f == 2:
                    S.op("dve", lambda e, pb=pb, tsel=tsel: e.tensor_copy(out=vT[:, tsel], in_=bank(pb)), w=[("ps", pb), ("vT", tt)])
                else:
                    S.op("act", lambda e, pb=pb, tsel=tsel, gT=gT: e.activation(out=gT[:, tsel], in_=bank(pb), func=AF.Silu), w=[("ps", pb), (gkey, tt)])
            if tt == 2:
                while tails:
                    tails.pop(0)()
        flush()
        if p == 0:
            dump("qT0", qT, [128, SEQ], BF16, [("qT", i) for i in range(8)])
            dump("kT0", kT, [128, SEQ], BF16, [("kT", i) for i in range(8)])
            dump("vT0", vT, [128, SEQ], BF16, [("vT", i) for i in range(8)])
            dump("gT0", gT, [128, SEQ], BF16, [(gkey, i) for i in range(8)])
            if stop_after == "proj0":
                S.emit(sems, dsem)
                return nc, dump_list
        if p < 3:
            load_w(wst, wbf, 512 * (p + 1), 512, "wbf", win_v)
        else:
            load_w(wst, wbf, 2048, 512, "wbf", win_v)

        def vtrans_ops(b, vs, tbs):
            ops = []
            for g8 in range(4):
                tb = tbs[g8]
                grp = []
                for j in range(8):
                    n = 8 * g8 + j
                    grp.append(("pe", lambda e, j=j, n=n, tb=tb, b=b: e.transpose(out=bankbf(tb)[:, 128 * j:128 * j + 128], in_=_tok(vT, b, n), identity=ident),
                                _tok_keys("vT", b, n) + ["ident"], [("ps", tb)]))
                grp.append(("dve", lambda e, g8=g8, tb=tb, vs=vs: e.tensor_copy(out=sub(Vtok[vs], (192 * 8 * g8, [[192, 8], [128, 2], [1, 64]])),
                                                                          in_=bankbf(tb).rearrange("p (n h d) -> p n h d", n=8, h=2)),
                            [], [("ps", tb), ("Vtok", vs)]))
                ops.append(grp)
            return ops

        def emit_ops(lst):
            for (eng, fn, r, w) in lst:
                S.op(eng, fn, r=r, w=w)

        groups = []
        for b in range(3):
            vs = b % 2
            for h in range(2):
                for g in range(8):
                    groups.append((b, h, g, vs))
        vt_pending = {}
        emit_ops([o for grp in vtrans_ops(0, 0, (1, 3, 5, 1)) for o in grp])

        def s_ops(i):
            b, h, g, vs = groups[i]
            r0 = 64 * h
            sg = i % 3
            Sps = psum[:, 1024 * sg:1024 * sg + 1024]
            for j in range(4):
                kb = 4 * g + j
                last = (kb % BPC[b] == BPC[b] - 1)
                sb_ = ("ps", 2 * sg + j // 2)
                if not last:
                    S.op("pe", lambda e, j=j, kb=kb, Sps=Sps, b=b, r0=r0: e.matmul(Sps[:, 256 * j:256 * j + 256], lhsT=_tok(kT[r0:r0 + 64], b, kb),
                                                                            rhs=_tok2(qT[r0:r0 + 64], b, kb, 2), start=True, stop=True),
                         r=_tok_keys("kT", b, kb) + _tok_keys("qT", b, kb) + _tok_keys("qT", b, kb + 1), w=[sb_])
                else:
                    S.op("pe", lambda e, j=j, kb=kb, Sps=Sps, b=b, r0=r0: e.matmul(Sps[:, 256 * j:256 * j + 128], lhsT=_tok(kT[r0:r0 + 64], b, kb),
                                                                            rhs=_tok(qT[r0:r0 + 64], b, kb), start=True, stop=True),
                         r=_tok_keys("kT", b, kb) + _tok_keys("qT", b, kb), w=[sb_])

        def ew_ops(i):
            b, h, g, vs = groups[i]
            sg = i % 3
            pt = i % 4
            Sps = psum[:, 1024 * sg:1024 * sg + 1024]
            S.op("act", lambda e, Sps=Sps, pt=pt: e.activation(out=PT[pt], in_=Sps, func=AF.Exp, scale=0.125),
                 w=[("ps", 2 * sg), ("ps", 2 * sg + 1), ("PT", pt)])
            meng = "dve"
            S.op(meng, lambda e, pt=pt: e.tensor_tensor(out=PT[pt], in0=PT[pt], in1=amask, op=ALU.mult), r=["amask", ("PT", pt)], w=[("PT", pt)])

        def pv_ops(i):
            b, h, g, vs = groups[i]
            vc0 = 0 if h == 0 else 64
            pt = i % 4
            ob = 6 + i % 2
            acc = accs[h]
            n0 = 4 * g
            started = False

            def vaug(n):
                return Vtok[vs][:, 192 * n + vc0:192 * n + vc0 + 128]

            def mm(o0, ncols, n, rhs, keys, stop):
                nonlocal started
                st = not started
                started = True
                S.op("pe", lambda e: e.matmul(bank(ob)[:, o0:o0 + ncols], lhsT=vaug(n), rhs=rhs, start=st, stop=stop, skip_group_check=True),
                     r=keys + [("Vtok", vs)], w=[("ps", ob)])

            if n0 % BPC[b] != 0:
                ptp = (i - 1) % 4
                mm(0, 128, n0 - 1, PT[ptp][:, 896:1024], [("PT", ptp)], False)
            for j in range(4):
                kb = n0 + j
                last = (kb % BPC[b] == BPC[b] - 1)
                if (not last) and j < 3:
                    mm(128 * j, 256, kb, PT[pt][:, 256 * j:256 * j + 256], [("PT", pt)], False)
                else:
                    mm(128 * j, 128, kb, PT[pt][:, 256 * j:256 * j + 128], [("PT", pt)], j == 3)
            view, akeys, rr = acc_view(acc, b, g)
            akeys = [(k[0] + str(h), k[1]) for k in akeys]
            src = bank(ob) if rr is None else bank(ob).rearrange(rr, r=2)
            if b == 0:
                S.op("act", lambda e, view=view, src=src: e.activation(out=view, in_=src, func=AF.Copy), w=[("ps", ob)] + akeys)
            else:
                S.op("dve", lambda e, view=view, src=src: e.tensor_tensor(out=view, in0=src, in1=view, op=ALU.add),
                     r=akeys, w=[("ps", ob)] + akeys)

        def den_ops(h):
            drow = 64 if h == 0 else 0
            ak = [("acc" + str(h), i) for i in range(8)]
            S.op("pool", lambda e, h=h, drow=drow: e.dma_start(out=dn_d[h:h + 1, :], in_=accs[h][drow:drow + 1, :]), r=ak, w=[("dn", h)], dma=True)
            S.op("pool", lambda e, h=h: e.dma_start(out=dtile[h], in_=dn_d[h].rearrange("(p j) -> p j", j=32)), r=[("dn", h)], w=[("dt", h)], dma=True)
            S.op("dve", lambda e, h=h: e.reciprocal(out=dtile[h], in_=dtile[h]), r=[("dt", h)], w=[("dt", h)])
            S.op("pool", lambda e, h=h: e.dma_start(out=rn_d[h].rearrange("(p j) -> p j", j=32), in_=dtile[h]), r=[("dt", h)], w=[("rn", h)], dma=True)

        NG = len(groups)
        LOOK = 3
        for i in range(min(LOOK, NG)):
            s_ops(i)
        for i in range(min(LOOK - 1, NG)):
            ew_ops(i)
        for i in range(NG):
            b, h, g, vs = groups[i]
            pv_ops(i)
            if h == 1 and b < 2 and g % 2 == 0:
                if g == 0:
                    vt_pending[b + 1] = vtrans_ops(b + 1, (b + 1) % 2, tuple(2 * ((i + 2 * k) % 3) + 1 for k in range(4)))
                emit_ops(vt_pending[b + 1][g // 2])
            if i + LOOK < NG:
                s_ops(i + LOOK)
            if i + LOOK - 1 < NG:
                ew_ops(i + LOOK - 1)
            if b == 2 and g == 7:
                den_ops(h)
        if p == 0:
            dump("acc0", accs[0], [128, SEQ], F32, [("acc0", i) for i in range(8)])
            dump("acc1", accs[1], [128, SEQ], F32, [("acc1", i) for i in range(8)])
        for h in range(2):
            o = 1 - h
            rows = slice(0, 64) if h == 0 else slice(64, 128)
            ok = [("acc" + str(o), i) for i in range(8)]
            S.op("pool", lambda e, h=h, o=o, rows=rows: e.dma_start(out=accs[o][rows, :], in_=rn_d[h:h + 1, :].to_broadcast([64, SEQ])),
                 r=[("rn", h)], w=ok, dma=True)
        def make_tail(p, gT, gkey):
            def tail():
                for half in range(2):
                    cs = slice(2048 * half, 2048 * half + 2048)
                    k0 = [("acc0", 4 * half + i) for i in range(4)]
                    k1 = [("acc1", 4 * half + i) for i in range(4)]
                    gk = [(gkey, 4 * half + i) for i in range(4)]
                    S.op("dve", lambda e, cs=cs: e.tensor_tensor(out=accs[0][:, cs], in0=accs[0][:, cs], in1=accs[1][:, cs], op=ALU.mult), r=k0 + k1, w=k0)
                    S.op("dve", lambda e, cs=cs: e.tensor_tensor(out=gT[:, cs], in0=accs[0][:, cs], in1=gT[:, cs], op=ALU.mult), r=k0 + gk, w=gk)
                    S.op("sp", lambda e, cs=cs: e.dma_start(out=mix_d[p][:, cs], in_=gT[:, cs]), r=gk, w=[("mixd", p)], dma=True)
            return tail

        tails.append(make_tail(p, gT, gkey))
        if p == 3 or stop_after == "attn0":
            while tails:
                tails.pop(0)()
        if p == 0:
            dump("mixT0", gT, [128, SEQ], BF16, [(gkey, i) for i in range(8)])
            if stop_after == "attn0":
                S.emit(sems, dsem)
                return nc, dump_list
    S.barrier()
    for pp in range(4):
        dump("mixd%d" % pp, mix_d[pp], [128, SEQ], BF16, [("mixd", pp)])
    if stop_after == "phaseA":
        S.emit(sems, dsem)
        return nc, dump_list
    M = Mem(PBASE)
    wstB = [M.f32(1024).rearrange("p (c f) -> p c f", c=8) for _ in range(2)]
    wbfB = M.bf(8 * 512).rearrange("p (c f) -> p c f", c=8)
    TB = M.off
    tabs = [[M.f32(512) for _ in range(2)] for _ in range(4)]
    qrawB = [M.bf(512) for _ in range(2)]
    t1B = [M.f32(512) for _ in range(2)]
    t2B = [M.f32(512) for _ in range(2)]
    grt = [M.bf(4 * 512).rearrange("p (f t) -> p f t", f=4) for _ in range(2)]
    TB_END = M.off
    qrT = [M.bf(SEQ) for _ in range(2)]
    krT = [M.bf(SEQ) for _ in range(2)]
    Vr = M.bf(32 * 512).rearrange("p (n v) -> p n v", n=32)
    WO_OFF = M.off
    wo = M.bf(8 * DM).rearrange("p (c f) -> p c f", c=8)

    for tt in range(8):
        for ft in range(2):
            tsl = (2 * tt + ft) % 2
            for k4 in range(4):
                S.op("sp", lambda e, tt=tt, ft=ft, k4=k4, tsl=tsl: e.dma_start(out=tabs[k4][tsl], in_=ropeR_d[4 * ft + k4, :, 512 * tt:512 * tt + 512]),
                     w=[("tab", k4, tsl)], dma=True)
            for qk in range(2):
                f = 2 * qk + ft
                pb = cnt["pb"] % 4
                cnt["pb"] += 1
                for c in range(8):
                    S.op("pe", lambda e, c=c, f=f, tt=tt, pb=pb: e.matmul(bank(pb), lhsT=wbf[:, c, 128 * f:128 * f + 128],
                                                                          rhs=hT[:, c, 512 * tt:512 * tt + 512], start=(c == 0), stop=(c == 7)),
                         r=[("wbf", f), ("hT", tt)], w=[("ps", pb)])
                flush()
                dst = (qrT if qk == 0 else krT)[ft][:, 512 * tt:512 * tt + 512]
                pend.append(rope(pb, dst, ("qrT" if qk == 0 else "krT", ft, tt), tabs[2 * qk][tsl], tabs[2 * qk + 1][tsl],
                                 [("tab", 2 * qk, tsl), ("tab", 2 * qk + 1, tsl)], qrawB, t1B, t2B))
    flush()
    dump("qrT0", qrT[0], [128, SEQ], BF16, [("qrT", 0, i) for i in range(8)])
    dump("krT0", krT[0], [128, SEQ], BF16, [("krT", 0, i) for i in range(8)])
    if stop_after == "B1":
        S.emit(sems, dsem)
        return nc, dump_list
    load_w(wstB, wbfB, 2560, 512, "wbfB", win_v, extra_w=[("wbf", f) for f in range(4)])
    for n in range(32):
        pb = cnt["pb"] % 4
        cnt["pb"] += 1
        for c in range(8):
            S.op("pe", lambda e, c=c, n=n, pb=pb: e.matmul(bank(pb), lhsT=hT[:, c, 128 * n:128 * n + 128], rhs=wbfB[:, c, :], start=(c == 0), stop=(c == 7)),
                 r=[("wbfB", i) for i in range(4)] + [("hT", n // 4)], w=[("ps", pb)])
        if n % 2 == 0:
            S.op("dve", lambda e, n=n, pb=pb: e.tensor_copy(out=Vr[:, n, :], in_=bank(pb)), w=[("ps", pb), ("Vr", n)])
        else:
            S.op("act", lambda e, n=n, pb=pb: e.activation(out=Vr[:, n, :], in_=bank(pb), func=AF.Copy), w=[("ps", pb), ("Vr", n)])
    load_w(wstB, wbfB, 3072, 512, "wbfB", win_v)
    for tt in range(8):
        gs = tt % 2
        for ft in range(4):
            pb = cnt["pb"] % 4
            cnt["pb"] += 1
            for c in range(8):
                S.op("pe", lambda e, c=c, ft=ft, tt=tt, pb=pb: e.matmul(bank(pb), lhsT=wbfB[:, c, 128 * ft:128 * ft + 128],
                                                                        rhs=hT[:, c, 512 * tt:512 * tt + 512], start=(c == 0), stop=(c == 7)),
                     r=[("wbfB", ft), ("hT", tt)], w=[("ps", pb)])
            S.op("act", lambda e, ft=ft, pb=pb, gs=gs: e.activation(out=grt[gs][:, ft, :], in_=bank(pb), func=AF.Silu), w=[("ps", pb), ("grt", gs, ft)])
            S.op("dve", lambda e, ft=ft, gs=gs: e.tensor_scalar(out=grt[gs][:, ft, :], in0=grt[gs][:, ft, :], scalar1=gn[:, ft:ft + 1], scalar2=None, op0=ALU.mult),
                 r=[("grt", gs, ft), "gn"], w=[("grt", gs, ft)])
        S.op("sp", lambda e, tt=tt, gs=gs: e.dma_start(out=grs_d[:, :, 512 * tt:512 * tt + 512].rearrange("f p t -> p f t"), in_=grt[gs]),
             r=[("grt", gs, ft) for ft in range(4)], w=[("grs", tt)], dma=True)
    if stop_after == "B3":
        S.emit(sems, dsem)
        return nc, dump_list
    S.barrier()

    M = Mem(TB)
    Ktok = [M.bf(256) for _ in range(2)]
    sT = [M.bf(512) for _ in range(2)]
    onb = [M.bf(512) for _ in range(2)]
    grl = [M.bf(4 * 512).rearrange("p (f t) -> p f t", f=4) for _ in range(2)]
    mixst = [M.bf(4 * 512).rearrange("p (f t) -> p f t", f=4) for _ in range(2)]
    assert M.off <= TB_END - 256
    mixA = Mem(0).bf(8 * SEQ).rearrange("p (c t) -> p c t", c=8)
    def mixa_load(c):
        S.op("sp", lambda e: e.dma_start(out=mixA[:, c, :], in_=mix_d[c]), r=[("mixd", c)], w=[("mixA", c)] + [("hT", i) for i in range(8)], dma=True)
    dump("mixa0e", mixA[:, 0:4, 0:512], [128, 4, 512], BF16, [("mixA", h) for h in range(4)])
    for pp in range(4):
        dump("mixe%d" % pp, mix_d[pp], [128, SEQ], BF16, [("mixd", pp)])
    def wo_chunk(j):
        sl = cnt["w"] % len(wstB)
        cnt["w"] += 1
        S.op("sp", lambda e: e.dma_start(out=wstB[sl], in_=wout_v[:, :, 128 * j:128 * j + 128]), w=[("wst", sl)], dma=True)
        S.op("pool", lambda e: e.tensor_copy(out=wo[:, :, 128 * j:128 * j + 128], in_=wstB[sl]), r=[("wst", sl)], w=[("wo", j)])
    S.op("dve", lambda e: e.memset(state, 0.0), w=["state"])
    nbias = bnst[:, 0:4]
    bnst2 = [bnst, Mem(TB_END - 256).f32(24)]
    bnmv2 = [bnmv, Mem(TB_END - 128).f32(8)]
    rstd2 = [rstd4, Mem(TB_END - 64).f32(4)]
    nb2 = [Mem(TB_END - 32).f32(4), Mem(TB_END - 16).f32(4)]

    OB = (2, 3, 6)

    def r_1(n):
        tsel = slice(128 * n, 128 * n + 128)
        ks = n % 2
        grp, sub4 = n // 4, n % 4
        gl = grp % 2
        if sub4 == 0:
            S.op("sp", lambda e: e.dma_start(out=grl[gl], in_=grs_d[:, :, 512 * grp:512 * grp + 512].rearrange("f p t -> p f t")),
                 r=[("grs", grp)], w=[("grl", gl)], dma=True)
        for ft in range(2):
            S.op("pe", lambda e, ft=ft: e.transpose(out=bankbf(0)[:, 128 * ft:128 * ft + 128], in_=krT[ft][:, tsel], identity=ident),
                 r=[("krT", ft, n // 4), "ident"], w=[("ps", 0)])
        S.op("act", lambda e: e.activation(out=Ktok[ks], in_=bankbf(0)[:, 0:256], func=AF.Copy), w=[("ps", 0), ("Ktok", ks)])
        S.op("pool", lambda e: e.tensor_tensor(out=Ktok[ks], in0=Ktok[ks], in1=gck, op=ALU.mult), r=["gck", ("Ktok", ks)], w=[("Ktok", ks)])
        for h in (0, 2, 1, 3):
            r0 = 64 * (h % 2)
            sbk = 1 if h % 2 == 0 else 7
            S.op("pe", lambda e, h=h, r0=r0, sbk=sbk: e.matmul(bank(sbk)[:, 128 * (h // 2):128 * (h // 2) + 128], lhsT=krT[h // 2][r0:r0 + 64, tsel],
                                                          rhs=qrT[h // 2][r0:r0 + 64, tsel], start=True, stop=True),
                 r=[("krT", h // 2, n // 4), ("qrT", h // 2, n // 4)], w=[("ps", sbk)])
        for hp, sbk in ((0, 1), (1, 7)):
            S.op("dve", lambda e, hp=hp, sbk=sbk: e.tensor_tensor(out=sT[ks][:, 256 * hp:256 * hp + 256], in0=bank(sbk)[:, 0:256], in1=rmask[:, 0:256], op=ALU.mult),
                 r=["rmask"], w=[("ps", sbk), ("sT", ks)])

    def r_2(n):
        tsel = slice(128 * n, 128 * n + 128)
        ks = n % 2
        ob = OB[n % 3]
        for h in range(4):
            r0 = 64 * (h % 2)
            S.op("pe", lambda e, h=h, r0=r0: e.matmul(bank(4)[r0:r0 + 64, 128 * (h // 2):128 * (h // 2) + 128], lhsT=Ktok[ks][:, 64 * h:64 * h + 64],
                                                 rhs=Vr[:, n, 128 * h:128 * h + 128], start=True, stop=True),
                 r=[("Ktok", ks), ("Vr", n)], w=[("ps", 4)])
        for h in range(4):
            r0 = 64 * (h % 2)
            sblk = 2 * (h % 2) + h // 2
            S.op("pe", lambda e, h=h, sblk=sblk: e.matmul(bank(ob)[:, 128 * h:128 * h + 128], lhsT=sT[ks][:, 128 * sblk:128 * sblk + 128],
                                                     rhs=Vr[:, n, 128 * h:128 * h + 128], start=True, stop=(n == 0)),
                 r=[("sT", ks), ("Vr", n)], w=[("ps", ob)])
            if n > 0:
                S.op("pe", lambda e, h=h, r0=r0: e.matmul(bank(ob)[:, 128 * h:128 * h + 128], lhsT=qrT[h // 2][r0:r0 + 64, tsel],
                                                     rhs=state_bf[r0:r0 + 64, 128 * (h // 2):128 * (h // 2) + 128], start=False, stop=True),
                     r=[("qrT", h // 2, n // 4), "state_bf"], w=[("ps", ob)])
        for cb in range(2):
            S.op("dve", lambda e, cb=cb: e.scalar_tensor_tensor(out=state[:, 128 * cb:128 * cb + 128], in0=state[:, 128 * cb:128 * cb + 128],
                                                                scalar=gsc[:, cb:cb + 1], in1=bank(4)[:, 128 * cb:128 * cb + 128], op0=ALU.mult, op1=ALU.add),
                 r=["state", "gsc"], w=["state", ("ps", 4)])
        S.op("act", lambda e: e.activation(out=state_bf, in_=state, func=AF.Copy), r=["state"], w=["state_bf"])

    def r_3(n):
        ks = n % 2
        ob = OB[n % 3]
        for h in range(4):
            S.op("dve", lambda e, h=h: e.bn_stats(out=bnst2[ks][:, 6 * h:6 * h + 6], in_=bank(ob)[:, 128 * h:128 * h + 128]), w=[("ps", ob), ("bnst", ks, h)])
        for h in range(4):
            S.op("dve", lambda e, h=h: e.bn_aggr(out=bnmv2[ks][:, 2 * h:2 * h + 2], in_=bnst2[ks][:, 6 * h:6 * h + 6]), r=[("bnst", ks, h)], w=[("bnmv", ks, h)])
        S.op("pool", lambda e: e.tensor_scalar(out=rstd2[ks], in0=bnmv2[ks][:, 1:8:2], scalar1=1e-5, scalar2=None, op0=ALU.add), r=[("bnmv", ks, h) for h in range(4)], w=[("rstd", ks)])
        S.op("pool", lambda e: e.tensor_tensor(out=rstd2[ks], in0=rstd2[ks], in1=mhalf, op=ALU.pow), r=[("rstd", ks), "mhalf"], w=[("rstd", ks)])
        S.op("pool", lambda e: e.tensor_tensor(out=nb2[ks], in0=bnmv2[ks][:, 0:8:2], in1=rstd2[ks], op=ALU.mult), r=[("bnmv", ks, h) for h in range(4)] + [("rstd", ks)], w=[("nb", ks)])
        S.op("pool", lambda e: e.tensor_scalar(out=nb2[ks], in0=nb2[ks], scalar1=-1.0, scalar2=None, op0=ALU.mult), r=[("nb", ks)], w=[("nb", ks)])

    def r_4(n):
        ks = n % 2
        ob = OB[n % 3]
        for h in range(4):
            S.op("act", lambda e, h=h: e.activation(out=onb[ks][:, 128 * h:128 * h + 128], in_=bank(ob)[:, 128 * h:128 * h + 128], func=AF.Identity,
                                                    bias=nb2[ks][:, h:h + 1], scale=rstd2[ks][:, h:h + 1]),
                 r=[("nb", ks), ("rstd", ks)], w=[("ps", ob), ("onb", ks, h)])

    def r_5(n):
        ks = n % 2
        grp, sub4 = n // 4, n % 4
        gl = grp % 2
        for h in range(4):
            S.op("pe", lambda e, h=h: e.transpose(out=bankbf(5)[:, 128 * h:128 * h + 128], in_=onb[ks][:, 128 * h:128 * h + 128], identity=ident),
                 r=[("onb", ks, h), "ident"], w=[("ps", 5)])
        S.op("dve", lambda e: e.tensor_tensor(out=mixA[:, 4:8, 128 * n:128 * n + 128], in0=bankbf(5)[:, 0:512].rearrange("p (h t) -> p h t", h=4),
                                              in1=grl[gl][:, :, 128 * sub4:128 * sub4 + 128], op=ALU.mult),
             r=[("grl", gl)], w=[("ps", 5)] + [("mixA", 4 + h) for h in range(4)])

    stages = ((r_1, 0), (r_2, 1), (r_3, 2), (r_4, 3), (r_5, 4))
    for step in range(32 + 4):
        for fn, sk in stages:
            if 0 <= step - sk < 32:
                fn(step - sk)
        if step % 4 == 2 and step // 4 < 8:
            wo_chunk(step // 4)
        if step % 4 == 1 and step // 4 < 4:
            mixa_load(step // 4)
    dump("mixr0", mixA[:, 4:8, 0:512], [128, 4, 512], BF16, [("mixA", 4 + h) for h in range(4)])
    dump("mixa0", mixA[:, 0:4, 0:512], [128, 4, 512], BF16, [("mixA", h) for h in range(4)])
    dump("wo", wo, [128, 8, DM], BF16, [("wo", j) for j in range(8)])
    if stop_after == "ret":
        S.emit(sems, dsem)
        return nc, dump_list
    S.barrier()

    M = Mem(PBASE + 16384)
    xr = [M.f32(DM) for _ in range(8)]
    yb = [M.f32(DM) for _ in range(4)]
    junk2s = [M.bf(DM) for _ in range(4)]
    fg = M.f32(DM)
    assert M.off <= WO_OFF
    S.op("sp", lambda e: e.dma_start(out=fg, in_=fg_d), w=["fg"], dma=True)
    S.op("dve", lambda e: e.memset(st_ss, 0.0), w=["st_ss"])
    def c_a(tt):
        sl = tt % 4
        xsl = tt % 8
        S.op("sp", lambda e: e.dma_start(out=xr[xsl], in_=x_d[128 * tt:128 * tt + 128, :]), w=[("xr", xsl)], dma=True)
        for nh in range(2):
            pb = (2 * tt + nh) % 8
            for c in range(8):
                S.op("pe", lambda e, c=c, nh=nh, pb=pb: e.matmul(bank(pb), lhsT=mixA[:, c, 128 * tt:128 * tt + 128], rhs=wo[:, c, 512 * nh:512 * nh + 512],
                                                                 start=(c == 0), stop=(c == 7)),
                     r=[("mixA", c)] + [("wo", 4 * nh + i) for i in range(4)], w=[("ps", pb)])
            S.op("dve", lambda e, nh=nh, pb=pb: e.tensor_tensor(out=yb[sl][:, 512 * nh:512 * nh + 512], in0=bank(pb), in1=xr[xsl][:, 512 * nh:512 * nh + 512], op=ALU.add),
                 r=[("xr", xsl)], w=[("ps", pb), ("yb", sl, nh)])
        S.op("act", lambda e: e.activation(out=junk2s[tt % 4], in_=yb[sl], func=AF.Square, accum_out=st_ss[:, tt:tt + 1]),
             r=[("yb", sl, 0), ("yb", sl, 1), "st_ss"], w=[("junk2", tt % 4), ("ss", tt)])

    def c_b(tt):
        S.op("dve", lambda e: e.tensor_scalar(out=st_ms[:, tt:tt + 1], in0=st_ss[:, tt:tt + 1], scalar1=1.0 / DM, scalar2=1e-6,
                                              op0=ALU.mult, op1=ALU.add), r=[("ss", tt)], w=[("ms", tt)])
        S.op("pool", lambda e: e.tensor_tensor(out=st_rs[:, tt:tt + 1], in0=st_ms[:, tt:tt + 1], in1=mhalf[:, 0:1], op=ALU.pow),
             r=[("ms", tt), "mhalf"], w=[("rs", tt)])

    def c_c(tt):
        sl = tt % 4
        S.op("dve", lambda e: e.scalar_tensor_tensor(out=yb[sl], in0=yb[sl], scalar=st_rs[:, tt:tt + 1], in1=fg, op0=ALU.mult, op1=ALU.mult),
             r=[("yb", sl, 0), ("yb", sl, 1), ("rs", tt), "fg"], w=[("yb", sl, 0), ("yb", sl, 1)])
        S.op("pool", lambda e: e.dma_start(out=out_d[128 * tt:128 * tt + 128, :], in_=yb[sl]), r=[("yb", sl, 0), ("yb", sl, 1)], w=[("out", tt)], dma=True)

    for step in range(32 + 2):
        if step < 32:
            c_a(step)
        if 1 <= step < 33:
            c_b(step - 1)
        if step >= 2:
            c_c(step - 2)
    S.emit(sems, dsem)
    return nc, dump_list


_CACHE = {}


def _prep_inputs(x, norm_gain, w_in, ret_gn_gain, w_out, final_gain):
    c = _host_constants()
    w_in_p = np.ascontiguousarray(np.asarray(w_in, np.float32)[0][:, _win_cols()])
    shared = dict(c)
    shared["w_in"] = w_in_p
    shared["w_out"] = np.ascontiguousarray(np.asarray(w_out, np.float32)[0])
    shared["ng"] = np.ascontiguousarray(np.asarray(norm_gain, np.float32)[0].reshape(8, 128).T)
    shared["gn"] = np.ascontiguousarray(np.asarray(ret_gn_gain, np.float32)[0].reshape(4, 128).T)
    shared["fg"] = np.ascontiguousarray(np.broadcast_to(np.asarray(final_gain, np.float32)[None, :], (128, DM)))
    x = np.asarray(x, np.float32)
    return [dict(shared, x=np.ascontiguousarray(x[b])) for b in range(x.shape[0])]


def kernel(x, norm_gain, w_in, ret_gn_gain, w_out, final_gain):
    in_maps = _prep_inputs(x, norm_gain, w_in, ret_gn_gain, w_out, final_gain)
    nc, _ = build()
    res = run_bass_kernel_spmd(nc, in_maps, core_ids=list(range(NCORES)))
    return np.stack([np.asarray(r["out"], np.float32) for r in res.results], axis=0)
```

```python
import numpy as np
import ml_dtypes
import concourse.bass as bass
import concourse.mybir as mybir
from concourse.bass_utils import run_bass_kernel_spmd

dt = mybir.dt
F32 = dt.float32
BF16 = dt.bfloat16
AF = mybir.ActivationFunctionType
ALU = mybir.AluOpType
AX = mybir.AxisListType


class Sched:
    COMPUTE = ("pe", "act", "dve", "pool")
    EPOCH = 400
    STRICT = True

    def __init__(self, nc, dma_ring=None):
        self.nc = nc
        self.ops = []
        self.lastw = {}
        self.readers = {}
        self.last_on = {}
        self.extra = {}
        self.dma_live = []
        self.dma_ring = dma_ring or {"sp": 24, "pool": 12, "act": 6}

    def op(self, eng, fn, r=(), w=(), dma=False):
        i = len(self.ops)
        deps = set(self.extra.pop(eng, ()))
        for k in r:
            s = self.lastw.get(k)
            if s is not None:
                deps.add((s, 0))
        for k in w:
            s = self.lastw.get(k)
            if s is not None:
                deps.add((s, 1))
            for s in self.readers.get(k, ()):
                deps.add((s, 2))
        keep = set()
        for (s, kind) in deps:
            so = self.ops[s]
            if so["dma"] or dma:
                keep.add(s)
            elif so["eng"] == eng:
                if eng != "pe" and (kind == 0 or self.STRICT):
                    keep.add(s)
            else:
                keep.add(s)
        keep.discard(i)
        self.ops.append(dict(eng=eng, fn=fn, deps=sorted(keep), dma=dma, signal=dma))
        for s in keep:
            self.ops[s]["signal"] = True
        for k in w:
            self.lastw[k] = i
            self.readers[k] = []
        for k in r:
            lst = self.readers.setdefault(k, [])
            if not dma:
                lst[:] = [s for s in lst if self.ops[s]["dma"] or self.ops[s]["eng"] != eng]
            lst.append(i)
        self.last_on[eng] = i
        if dma:
            self.dma_live.append(i)
        return i

    def barrier(self):
        srcs = set(self.last_on.values()) | set(self.dma_live)
        self.dma_live = []
        for e in ("pe", "act", "dve", "pool", "sp"):
            self.extra.setdefault(e, set())
            for s in srcs:
                so = self.ops[s]
                if so["eng"] == e and not so["dma"]:
                    continue
                self.extra[e].add((s, 0))
                so["signal"] = True

    def emit(self, sems, dma_sems):
        nc = self.nc
        cnt = {e: 0 for e in self.COMPUTE}
        dcnt = {q: 0 for q in dma_sems}
        dval = {q: [0] * len(dma_sems[q]) for q in dma_sems}
        for o in self.ops:
            if o["dma"]:
                q = o["eng"]
                j = dcnt[q] % len(dma_sems[q])
                dcnt[q] += 1
                o["ring_prev"] = (dma_sems[q][j], dval[q][j]) if dval[q][j] > 0 else None
                dval[q][j] += 16
                o["done"] = (dma_sems[q][j], dval[q][j])
            elif o["signal"]:
                k = cnt[o["eng"]]
                cnt[o["eng"]] += 1
                o["done"] = (sems[o["eng"]][k // self.EPOCH], k % self.EPOCH + 1)
        by_eng = {}
        for o in self.ops:
            by_eng.setdefault(o["eng"], []).append(o)
        final_waits = []
        for q in dma_sems:
            for j, s in enumerate(dma_sems[q]):
                if dval[q][j] > 0:
                    final_waits.append((s, dval[q][j]))
        handles = {"pe": nc.tensor, "act": nc.scalar, "dve": nc.vector, "pool": nc.gpsimd, "sp": nc.sync}
        ops_all = self.ops

        def run(eng, e):
            waited = {}

            def wait(sem, val):
                key = id(sem)
                if waited.get(key, 0) < val:
                    e.wait_ge(sem, val)
                    waited[key] = val

            for o in by_eng.get(eng, []):
                for s in o["deps"]:
                    sem, val = ops_all[s]["done"]
                    wait(sem, val)
                if o["dma"] and o["ring_prev"] is not None:
                    wait(*o["ring_prev"])
                ins = o["fn"](e)
                if o["signal"]:
                    ins.then_inc(o["done"][0], 16 if o["dma"] else 1)
            if eng == "sp":
                for (s, v) in final_waits:
                    wait(s, v)

        with nc.Block() as block:
            @block.sync
            def _(e):
                run("sp", e)

            @block.tensor
            def _(e):
                run("pe", e)

            @block.scalar
            def _(e):
                run("act", e)

            @block.vector
            def _(e):
                run("dve", e)

            @block.gpsimd
            def _(e):
                run("pool", e)


SEQ = 4096
DM = 1024
NCORES = 8
THETA = 10000.0
_bf = ml_dtypes.bfloat16


def _host_constants():
    c = {}
    c["ident"] = np.eye(128, dtype=np.float32).astype(_bf)
    permm = np.zeros((128, 128), np.float32)
    for m in range(128):
        k = m + 32 if (m % 64) < 32 else m - 32
        permm[k, m] = 1.0
    c["permm"] = permm.astype(_bf)
    kk = np.arange(128)[:, None]
    qq = np.arange(128)[None, :]
    prev = (kk >= qq).astype(np.float32)
    own = (qq >= kk).astype(np.float32)
    c["amask"] = np.tile(np.concatenate([own, prev], 1), (1, 4)).astype(_bf)
    c["rmask"] = np.tile(own, (1, 4)).astype(_bf)
    pos = np.arange(SEQ, dtype=np.float64)
    j = np.arange(128) % 32
    sign = np.where((np.arange(128) % 64) >= 32, 1.0, -1.0)[:, None]
    inv = THETA ** (-(np.arange(32, dtype=np.float64)) / 32.0)
    ang = pos[None, :] * inv[j][:, None]
    c["ropeA"] = np.stack([np.cos(ang), np.sin(ang) * sign]).astype(np.float32)
    invr = 1.0 / (THETA ** np.linspace(0.0, 1.0, 32, dtype=np.float64))
    angr = pos[None, :] * invr[j][:, None]
    cosR = np.cos(angr)
    sinR = np.sin(angr) * sign
    lg = np.log1p(-(2.0 ** (-5.0 - np.arange(4, dtype=np.float64))))
    i_in = (np.arange(SEQ) % 128).astype(np.float64)
    ropeR = np.zeros((8, 128, SEQ), np.float32)
    for ft in range(2):
        for hp in range(2):
            h = 2 * ft + hp
            rows = slice(64 * hp, 64 * hp + 64)
            dq = np.exp((i_in + 1.0) * lg[h])[None, :]
            dk = (np.exp(-(i_in + 1.0) * lg[h]) / 8.0)[None, :]
            ropeR[ft * 4 + 0, rows] = cosR[rows] * dq
            ropeR[ft * 4 + 1, rows] = sinR[rows] * dq
            ropeR[ft * 4 + 2, rows] = cosR[rows] * dk
            ropeR[ft * 4 + 3, rows] = sinR[rows] * dk
    c["ropeR"] = ropeR
    gC = np.exp(128.0 * lg)
    gck = np.zeros((128, 256), np.float32)
    for h in range(4):
        gck[:, h * 64:(h + 1) * 64] = gC[h]
    c["gck"] = gck
    gsc = np.zeros((128, 2), np.float32)
    for cb in range(2):
        gsc[:64, cb] = gC[2 * cb]
        gsc[64:, cb] = gC[2 * cb + 1]
    c["gsc"] = gsc
    return c


def _win_cols():
    cols = []
    for p in range(4):
        for base in (0, 512, 1024, 1536):
            cols += list(range(base + 128 * p, base + 128 * p + 128))
    permh = [2 * i for i in range(32)] + [2 * i + 1 for i in range(32)]
    for base in (2048, 2304):
        for h in range(4):
            cols += [base + 64 * h + cc for cc in permh]
    cols += list(range(2560, 3584))
    return np.asarray(cols)


def _tok(buf, b, n):
    if b == 0:
        return buf[:, 128 * n:128 * n + 128]
    if b == 1:
        r, m = n // 8, n % 8
        s = r + 512 * m
        return buf[:, s:s + 509:4]
    r, m = n // 2, n % 2
    s = r + 2048 * m
    return buf[:, s:s + 2033:16]


def _tok2(buf, b, n, nblk):
    cntk = 128 * nblk
    if b == 0:
        return buf[:, 128 * n:128 * n + cntk]
    if b == 1:
        r, m = n // 8, n % 8
        s = r + 512 * m
        return buf[:, s:s + 4 * (cntk - 1) + 1:4]
    r, m = n // 2, n % 2
    s = r + 2048 * m
    return buf[:, s:s + 16 * (cntk - 1) + 1:16]


def _tok_keys(name, b, n):
    if b == 0:
        return [(name, n // 4)]
    if b == 1:
        return [(name, n % 8)]
    return [(name, 4 * (n % 2) + i) for i in range(4)]


BPC = (32, 8, 2)


def build(stop_after=None, dumps=()):
    nc = bass.Bass("TRN2", target_bir_lowering=False)
    from contextlib import ExitStack
    es = ExitStack()

    def din(name, shape, d=F32):
        return nc.dram_tensor(name, shape, d, kind="ExternalInput").ap()

    x_d = din("x", [SEQ, DM])
    win_d = din("w_in", [DM, 3584])
    wout_d = din("w_out", [DM, DM])
    ng_d = din("ng", [128, 8])
    gn_d = din("gn", [128, 4])
    fg_d = din("fg", [128, DM])
    ident_d = din("ident", [128, 128], BF16)
    permm_d = din("permm", [128, 128], BF16)
    amask_d = din("amask", [128, 1024], BF16)
    rmask_d = din("rmask", [128, 512], BF16)
    ropeA_d = din("ropeA", [2, 128, SEQ])
    ropeR_d = din("ropeR", [8, 128, SEQ])
    gck_d = din("gck", [128, 256])
    gsc_d = din("gsc", [128, 2])
    out_d = nc.dram_tensor("out", [SEQ, DM], F32, kind="ExternalOutput").ap()
    mix_d = nc.dram_tensor("mixs", [8, 128, SEQ], BF16, kind="Internal").ap()
    grs_d = nc.dram_tensor("grs", [4, 128, SEQ], BF16, kind="Internal").ap()
    dn_d = nc.dram_tensor("dns", [2, SEQ], F32, kind="Internal").ap()
    rn_d = nc.dram_tensor("rns", [2, SEQ], F32, kind="Internal").ap()
    win_v = win_d.rearrange("(c p) f -> p c f", p=128)
    wout_v = wout_d.rearrange("(c p) f -> p c f", p=128)

    ARENA_BYTES = 211968
    arena = es.enter_context(nc.sbuf_tensor("arena", [128, ARENA_BYTES // 4], F32))
    psum = es.enter_context(nc.psum_tensor("psum", [128, 4096], F32))
    sems = {e: [es.enter_context(nc.semaphore(f"s_{e}{i}")) for i in range(n)]
            for e, n in (("pe", 12), ("act", 6), ("dve", 12), ("pool", 4))}
    dsem = {q: [es.enter_context(nc.semaphore(f"d_{q}{i}")) for i in range(n)] for q, n in (("sp", 28), ("pool", 8))}
    S = Sched(nc)

    class Mem:
        def __init__(self, base):
            self.off = base

        def f32(self, n):
            o = self.off
            self.off += 4 * n
            assert self.off <= ARENA_BYTES, self.off
            return arena[:, o // 4:o // 4 + n]

        def bf(self, n):
            o = self.off
            self.off += 2 * n
            assert self.off % 4 == 0 and self.off <= ARENA_BYTES, self.off
            return arena[:, o // 4:o // 4 + n // 2].bitcast(BF16)

    def bank(i, n=512):
        return psum[:, 512 * i:512 * i + n]

    def bankbf(i):
        return psum[:, 512 * i:512 * i + 512].bitcast(BF16)

    dump_list = []

    def dump(name, ap, shape, d, keys):
        if name not in dumps:
            return
        t = nc.dram_tensor("dbg_" + name, shape, d, kind="ExternalOutput").ap()
        S.op("sp", lambda e, t=t, ap=ap: e.dma_start(out=t, in_=ap), r=keys, dma=True)
        dump_list.append(name)

    M = Mem(0)
    hT = M.bf(8 * SEQ).rearrange("p (c t) -> p c t", c=8)
    ident = M.bf(128)
    permm = M.bf(128)
    amask = M.bf(1024)
    rmask = M.bf(512)
    gck = M.f32(256)
    ng = M.f32(8)
    gn = M.f32(4)
    gsc = M.f32(2)
    st_ss = M.f32(32)
    st_ms = M.f32(32)
    st_rs = M.f32(32)
    state = M.f32(256)
    state_bf = M.bf(256)
    bnst = M.f32(24)
    bnmv = M.f32(8)
    rstd4 = M.f32(4)
    mhalf = M.f32(4)
    dtile = [M.f32(32) for _ in range(2)]
    PBASE = (M.off + 63) // 64 * 64

    S.op("sp", lambda e: e.dma_start(out=ident, in_=ident_d), w=["ident"], dma=True)
    S.op("sp", lambda e: e.dma_start(out=permm, in_=permm_d), w=["permm"], dma=True)
    S.op("sp", lambda e: e.dma_start(out=amask, in_=amask_d), w=["amask"], dma=True)
    S.op("sp", lambda e: e.dma_start(out=rmask, in_=rmask_d), w=["rmask"], dma=True)
    S.op("sp", lambda e: e.dma_start(out=gck, in_=gck_d), w=["gck"], dma=True)
    S.op("sp", lambda e: e.dma_start(out=ng, in_=ng_d), w=["ng"], dma=True)
    S.op("sp", lambda e: e.dma_start(out=gn, in_=gn_d), w=["gn"], dma=True)
    S.op("sp", lambda e: e.dma_start(out=gsc, in_=gsc_d), w=["gsc"], dma=True)
    S.op("dve", lambda e: e.memset(st_ss, 0.0), w=["st_ss"])
    S.op("dve", lambda e: e.memset(mhalf, -0.5), w=["mhalf"])

    M = Mem(PBASE)
    xs = [M.f32(DM) for _ in range(8)]
    junks = [M.bf(DM) for _ in range(4)]
    xn = [M.bf(DM) for _ in range(4)]
    def pro_a(tt):
        sl = tt % 8
        S.op("sp", lambda e: e.dma_start(out=xs[sl], in_=x_d[128 * tt:128 * tt + 128, :]), w=[("xs", sl)], dma=True)
        S.op("act", lambda e: e.activation(out=junks[tt % 4], in_=xs[sl], func=AF.Square, accum_out=st_ss[:, tt:tt + 1]),
             r=[("xs", sl), "st_ss"], w=[("junk", tt % 4), ("ss", tt)])
        S.op("dve", lambda e: e.tensor_scalar(out=st_ms[:, tt:tt + 1], in0=st_ss[:, tt:tt + 1], scalar1=1.0 / DM, scalar2=1e-6,
                                              op0=ALU.mult, op1=ALU.add), r=[("ss", tt)], w=[("ms", tt)])
        S.op("pool", lambda e: e.tensor_tensor(out=st_rs[:, tt:tt + 1], in0=st_ms[:, tt:tt + 1], in1=mhalf[:, 0:1], op=ALU.pow),
             r=[("ms", tt), "mhalf"], w=[("rs", tt)])

    def pro_b(tt):
        xsl = tt % 8
        sl = tt % 4
        pb = tt % 4
        S.op("act", lambda e: e.activation(out=xn[sl], in_=xs[xsl], func=AF.Copy, scale=st_rs[:, tt:tt + 1]),
             r=[("xs", xsl), ("rs", tt)], w=[("xn", sl)])
        for c in range(8):
            S.op("pe", lambda e, c=c: e.transpose(out=bankbf(pb)[:, 128 * c:128 * c + 128], in_=xn[sl][:, 128 * c:128 * c + 128], identity=ident),
                 r=[("xn", sl), "ident"], w=[("ps", pb)])
        S.op("dve", lambda e: e.tensor_copy(out=hT[:, :, 128 * tt:128 * tt + 128], in_=bankbf(pb).rearrange("p (c t) -> p c t", c=8)),
             w=[("ps", pb), ("hT", tt // 4)])

    for step in range(32 + 2):
        if step < 32:
            pro_a(step)
        if step >= 2:
            pro_b(step - 2)
    dump("hT", hT, [128, 8, SEQ], BF16, [("hT", i) for i in range(8)])
    if stop_after == "prologue":
        S.emit(sems, dsem)
        return nc, dump_list

    S.barrier()

    cnt = {"w": 0, "rope": 0, "pb": 0, "vt": 0, "sg": 0, "pt": 0, "ob": 0}

    def sub(ap, extra):
        return bass.AP(ap.tensor, ap.offset + extra[0], [list(ap.ap[0])] + [list(x) for x in extra[1]])

    def load_w(wst, wbf, col0, ncols, tag, src, scale=True, extra_w=()):
        for j in range(ncols // 128):
            sl = cnt["w"] % len(wst)
            cnt["w"] += 1
            S.op("sp", lambda e, sl=sl, j=j: e.dma_start(out=wst[sl], in_=src[:, :, col0 + 128 * j:col0 + 128 * j + 128]),
                 w=[("wst", sl)] + list(extra_w), dma=True)
            if scale:
                S.op("pool", lambda e, sl=sl, j=j: e.tensor_tensor(out=wbf[:, :, 128 * j:128 * j + 128], in0=wst[sl],
                                                                    in1=ng.unsqueeze(2).to_broadcast([128, 8, 128]), op=ALU.mult),
                     r=[("wst", sl), "ng"], w=[(tag, j)] + list(extra_w))
            else:
                S.op("pool", lambda e, sl=sl, j=j: e.tensor_copy(out=wbf[:, :, 128 * j:128 * j + 128], in_=wst[sl]),
                     r=[("wst", sl)], w=[(tag, j)])

    def rope(pb, dst, dstkey, cos, sin, ckeys, qraw, t1, t2):
        sl = cnt["rope"] % 2
        cnt["rope"] += 1
        rb = 4 + sl
        aeng = "pool" if (cnt["rope"] // 2) % 2 == 0 else "dve"
        S.op("act", lambda e: e.activation(out=qraw[sl], in_=bank(pb), func=AF.Copy), w=[("ps", pb), ("qraw", sl)])
        S.op("dve", lambda e: e.tensor_tensor(out=t1[sl], in0=bank(pb), in1=cos, op=ALU.mult), r=[ckeys[0]], w=[("ps", pb), ("t1", sl)])

        def stage_b():
            S.op("pe", lambda e: e.matmul(bank(rb), lhsT=permm, rhs=qraw[sl], start=True, stop=True), r=["permm", ("qraw", sl)], w=[("ps", rb)])
            S.op("dve", lambda e: e.tensor_tensor(out=t2[sl], in0=bank(rb), in1=sin, op=ALU.mult), r=[ckeys[1]], w=[("ps", rb), ("t2", sl)])
            S.op(aeng, lambda e: e.tensor_tensor(out=dst, in0=t1[sl], in1=t2[sl], op=ALU.add), r=[("t1", sl), ("t2", sl)], w=[dstkey])
        return stage_b

    pend = []

    def flush():
        while pend:
            pend.pop(0)()

    M = Mem(PBASE)
    wst = [M.f32(1024).rearrange("p (c f) -> p c f", c=8) for _ in range(1)]
    wbf = M.bf(8 * 512).rearrange("p (c f) -> p c f", c=8)
    ROPE_OFF = M.off
    cosT = [M.f32(512) for _ in range(2)]
    sinT = [M.f32(512) for _ in range(2)]
    qraw = [M.bf(512) for _ in range(2)]
    t1 = [M.f32(512) for _ in range(2)]
    t2 = [M.f32(512) for _ in range(2)]
    qT = M.bf(SEQ)
    kT = M.bf(SEQ)
    vT = M.bf(SEQ)
    gTs = [M.bf(SEQ) for _ in range(2)]
    Vtok = [M.bf(32 * 192) for _ in range(2)]
    PT = [M.bf(1024) for _ in range(4)]
    accs = [M.f32(SEQ), M.f32(SEQ)]
    PHASEA_END = M.off
    Q1 = Mem(ROPE_OFF).bf(SEQ)
    Q2 = Mem(ROPE_OFF + 8192).bf(SEQ)
    QK = {1: [("cos", 0), ("cos", 1), ("sin", 0), ("sin", 1)],
          2: [("qraw", 0), ("qraw", 1), ("t1", 0), ("t1", 1), ("t2", 0)]}

    for vs in range(2):
        S.op("pool", lambda e, vs=vs: e.memset(sub(Vtok[vs], (64, [[192, 32], [1, 64]])), 1.0), w=[("Vtok", vs)])

    def acc_view(acc, b, g):
        if b == 0:
            return acc[:, 512 * g:512 * g + 512], [("acc", g)], None
        if b == 1:
            r4, m0 = g // 2, 4 * (g % 2)
            s = r4 + 512 * m0
            return acc[:, s:s + 2045:4], [("acc", m0 + i) for i in range(4)], None
        return sub(acc, (2 * g, [[1, 2], [16, 256]])), [("acc", i) for i in range(8)], "p (r l) -> p r l"

    load_w(wst, wbf, 0, 512, "wbf", win_v)
    tails = []
    for p in range(4):
        gT = gTs[p % 2]
        gkey = "gT%d" % (p % 2)
        for tt in range(8):
            tsl = tt % 2
            S.op("sp", lambda e, tt=tt, tsl=tsl: e.dma_start(out=cosT[tsl], in_=ropeA_d[0, :, 512 * tt:512 * tt + 512]), w=[("cos", tsl)], dma=True)
            S.op("sp", lambda e, tt=tt, tsl=tsl: e.dma_start(out=sinT[tsl], in_=ropeA_d[1, :, 512 * tt:512 * tt + 512]), w=[("sin", tsl)], dma=True)
            for f in range(4):
                pb = cnt["pb"] % 4
                cnt["pb"] += 1
                for c in range(8):
                    S.op("pe", lambda e, c=c, f=f, tt=tt, pb=pb: e.matmul(bank(pb), lhsT=wbf[:, c, 128 * f:128 * f + 128],
                                                                          rhs=hT[:, c, 512 * tt:512 * tt + 512], start=(c == 0), stop=(c == 7)),
                         r=[("wbf", f), ("hT", tt)], w=[("ps", pb)])
                flush()
                tsel = slice(512 * tt, 512 * tt + 512)
                if f < 2:
                    pend.append(rope(pb, (qT if f == 0 else kT)[:, tsel], ("qT" if f == 0 else "kT", tt), cosT[tsl], sinT[tsl],
                                     [("cos", tsl), ("sin", tsl)], qraw, t1, t2))
                elif f == 2:
                    S.op("dve", lambda e, pb=pb, tsel=tsel: e.tensor_copy(out=vT[:, tsel], in_=bank(pb)), w=[("ps", pb), ("vT", tt)])
                else:
                    S.op("act", lambda e, pb=pb, tsel=tsel, gT=gT: e.activation(out=gT[:, tsel], in_=bank(pb), func=AF.Silu), w=[("ps", pb), (gkey, tt)])
            if tt == 2:
                while tails:
                    tails.pop(0)()
        flush()
        if p == 0:
            dump("qT0", qT, [128, SEQ], BF16, [("qT", i) for i in range(8)])
            dump("kT0", kT, [128, SEQ], BF16, [("kT", i) for i in range(8)])
            dump("vT0", vT, [128, SEQ], BF16, [("vT", i) for i in range(8)])
            dump("gT0", gT, [128, SEQ], BF16, [(gkey, i) for i in range(8)])
            if stop_after == "proj0":
                S.emit(sems, dsem)
                return nc, dump_list
        if p < 3:
            load_w(wst, wbf, 512 * (p + 1), 512, "wbf", win_v)
        else:
            load_w(wst, wbf, 2048, 512, "wbf", win_v)

        def vtrans_ops(b, vs, tbs):
            ops = []
            for g8 in range(4):
                tb = tbs[g8]
                grp = []
                for j in range(8):
                    n = 8 * g8 + j
                    grp.append(("pe", lambda e, j=j, n=n, tb=tb, b=b: e.transpose(out=bankbf(tb)[:, 128 * j:128 * j + 128], in_=_tok(vT, b, n), identity=ident),
                                _tok_keys("vT", b, n) + ["ident"], [("ps", tb)]))
                grp.append(("dve", lambda e, g8=g8, tb=tb, vs=vs: e.tensor_copy(out=sub(Vtok[vs], (192 * 8 * g8, [[192, 8], [128, 2], [1, 64]])),
                                                                          in_=bankbf(tb).rearrange("p (n h d) -> p n h d", n=8, h=2)),
                            [], [("ps", tb), ("Vtok", vs)]))
                ops.append(grp)
            return ops

        def emit_ops(lst):
            for (eng, fn, r, w) in lst:
                S.op(eng, fn, r=r, w=w)

        groups = []
        for b in range(3):
            vs = b % 2
            for h in range(2):
                for g in range(8):
                    groups.append((b, h, g, vs))
        vt_pending = {}
        emit_ops([o for grp in vtrans_ops(0, 0, (1, 3, 5, 1)) for o in grp])

        def s_ops(i):
            b, h, g, vs = groups[i]
            r0 = 64 * h
            sg = i % 3
            Sps = psum[:, 1024 * sg:1024 * sg + 1024]
            for j in range(4):
                kb = 4 * g + j
                last = (kb % BPC[b] == BPC[b] - 1)
                sb_ = ("ps", 2 * sg + j // 2)
                if b >= 1:
                    nq = 1 if last else 2
                    Qd = Q1 if b == 1 else Q2
                    S.op("pe", lambda e, j=j, kb=kb, Sps=Sps, b=b, r0=r0, nq=nq, Qd=Qd: e.matmul(
                        Sps[:, 256 * j:256 * j + 128 * nq], lhsT=_tok(kT[r0:r0 + 64], b, kb),
                        rhs=Qd[r0:r0 + 64, 128 * kb:128 * kb + 128 * nq], start=True, stop=True),
                         r=_tok_keys("kT", b, kb) + QK[b], w=[sb_])
                elif not last:
                    S.op("pe", lambda e, j=j, kb=kb, Sps=Sps, b=b, r0=r0: e.matmul(Sps[:, 256 * j:256 * j + 256], lhsT=_tok(kT[r0:r0 + 64], b, kb),
                                                                            rhs=_tok2(qT[r0:r0 + 64], b, kb, 2), start=True, stop=True),
                         r=_tok_keys("kT", b, kb) + _tok_keys("qT", b, kb) + _tok_keys("qT", b, kb + 1), w=[sb_])
                else:
                    S.op("pe", lambda e, j=j, kb=kb, Sps=Sps, b=b, r0=r0: e.matmul(Sps[:, 256 * j:256 * j + 128], lhsT=_tok(kT[r0:r0 + 64], b, kb),
                                                                            rhs=_tok(qT[r0:r0 + 64], b, kb), start=True, stop=True),
                         r=_tok_keys("kT", b, kb) + _tok_keys("qT", b, kb), w=[sb_])

        def ew_ops(i):
            b, h, g, vs = groups[i]
            sg = i % 3
            pt = i % 4
            Sps = psum[:, 1024 * sg:1024 * sg + 1024]
            S.op("act", lambda e, Sps=Sps, pt=pt: e.activation(out=PT[pt], in_=Sps, func=AF.Exp, scale=0.125),
                 w=[("ps", 2 * sg), ("ps", 2 * sg + 1), ("PT", pt)])
            meng = "dve"
            S.op(meng, lambda e, pt=pt: e.tensor_tensor(out=PT[pt], in0=PT[pt], in1=amask, op=ALU.mult), r=["amask", ("PT", pt)], w=[("PT", pt)])

        def pv_ops(i):
            b, h, g, vs = groups[i]
            vc0 = 0 if h == 0 else 64
            pt = i % 4
            ob = 6 + i % 2
            acc = accs[h]
            n0 = 4 * g
            started = False

            def vaug(n):
                return Vtok[vs][:, 192 * n + vc0:192 * n + vc0 + 128]

            def mm(o0, ncols, n, rhs, keys, stop):
                nonlocal started
                st = not started
                started = True
                S.op("pe", lambda e: e.matmul(bank(ob)[:, o0:o0 + ncols], lhsT=vaug(n), rhs=rhs, start=st, stop=stop, skip_group_check=True),
                     r=keys + [("Vtok", vs)], w=[("ps", ob)])

            if n0 % BPC[b] != 0:
                ptp = (i - 1) % 4
                mm(0, 128, n0 - 1, PT[ptp][:, 896:1024], [("PT", ptp)], False)
            for j in range(4):
                kb = n0 + j
                last = (kb % BPC[b] == BPC[b] - 1)
                if (not last) and j < 3:
                    mm(128 * j, 256, kb, PT[pt][:, 256 * j:256 * j + 256], [("PT", pt)], False)
                else:
                    mm(128 * j, 128, kb, PT[pt][:, 256 * j:256 * j + 128], [("PT", pt)], j == 3)
            view, akeys, rr = acc_view(acc, b, g)
            akeys = [(k[0] + str(h), k[1]) for k in akeys]
            src = bank(ob) if rr is None else bank(ob).rearrange(rr, r=2)
            if b == 0:
                S.op("act", lambda e, view=view, src=src: e.activation(out=view, in_=src, func=AF.Copy), w=[("ps", ob)] + akeys)
            else:
                S.op("dve", lambda e, view=view, src=src: e.tensor_tensor(out=view, in0=src, in1=view, op=ALU.add),
                     r=akeys, w=[("ps", ob)] + akeys)

        def den_ops(h):
            drow = 64 if h == 0 else 0
            ak = [("acc" + str(h), i) for i in range(8)]
            S.op("pool", lambda e, h=h, drow=drow: e.dma_start(out=dn_d[h:h + 1, :], in_=accs[h][drow:drow + 1, :]), r=ak, w=[("dn", h)], dma=True)
            S.op("pool", lambda e, h=h: e.dma_start(out=dtile[h], in_=dn_d[h].rearrange("(p j) -> p j", j=32)), r=[("dn", h)], w=[("dt", h)], dma=True)
            S.op("dve", lambda e, h=h: e.reciprocal(out=dtile[h], in_=dtile[h]), r=[("dt", h)], w=[("dt", h)])
            S.op("pool", lambda e, h=h: e.dma_start(out=rn_d[h].rearrange("(p j) -> p j", j=32), in_=dtile[h]), r=[("dt", h)], w=[("rn", h)], dma=True)

        NG = len(groups)
        LOOK = 3
        for i in range(min(LOOK, NG)):
            s_ops(i)
        for i in range(min(LOOK - 1, NG)):
            ew_ops(i)
        for i in range(NG):
            b, h, g, vs = groups[i]
            pv_ops(i)
            if i == 3 or i == 10:
                bq = 1 if i == 3 else 2
                rr_ = 4 if bq == 1 else 16
                S.op("dve", lambda e, bq=bq, rr_=rr_: e.tensor_copy(out=(Q1 if bq == 1 else Q2).rearrange("p (r l) -> p r l", r=rr_),
                                                              in_=qT.rearrange("p (l r) -> p r l", r=rr_)),
                     r=[("qT", tt) for tt in range(8)], w=QK[bq])
            if h == 1 and b < 2 and g % 2 == 0:
                if g == 0:
                    vt_pending[b + 1] = vtrans_ops(b + 1, (b + 1) % 2, tuple(2 * ((i + 2 * k) % 3) + 1 for k in range(4)))
                emit_ops(vt_pending[b + 1][g // 2])
            if i + LOOK < NG:
                s_ops(i + LOOK)
            if i + LOOK - 1 < NG:
                ew_ops(i + LOOK - 1)
            if b == 2 and g == 7:
                den_ops(h)
        if p == 0:
            dump("acc0", accs[0], [128, SEQ], F32, [("acc0", i) for i in range(8)])
            dump("acc1", accs[1], [128, SEQ], F32, [("acc1", i) for i in range(8)])
        for h in range(2):
            o = 1 - h
            rows = slice(0, 64) if h == 0 else slice(64, 128)
            ok = [("acc" + str(o), i) for i in range(8)]
            S.op("pool", lambda e, h=h, o=o, rows=rows: e.dma_start(out=accs[o][rows, :], in_=rn_d[h:h + 1, :].to_broadcast([64, SEQ])),
                 r=[("rn", h)], w=ok, dma=True)
        def make_tail(p, gT, gkey):
            def tail():
                for half in range(2):
                    cs = slice(2048 * half, 2048 * half + 2048)
                    k0 = [("acc0", 4 * half + i) for i in range(4)]
                    k1 = [("acc1", 4 * half + i) for i in range(4)]
                    gk = [(gkey, 4 * half + i) for i in range(4)]
                    S.op("dve", lambda e, cs=cs: e.tensor_tensor(out=accs[0][:, cs], in0=accs[0][:, cs], in1=accs[1][:, cs], op=ALU.mult), r=k0 + k1, w=k0)
                    S.op("dve", lambda e, cs=cs: e.tensor_tensor(out=gT[:, cs], in0=accs[0][:, cs], in1=gT[:, cs], op=ALU.mult), r=k0 + gk, w=gk)
                    S.op("sp", lambda e, cs=cs: e.dma_start(out=mix_d[p][:, cs], in_=gT[:, cs]), r=gk, w=[("mixd", p)], dma=True)
            return tail

        tails.append(make_tail(p, gT, gkey))
        if p == 3 or stop_after == "attn0":
            while tails:
                tails.pop(0)()
        if p == 0:
            dump("mixT0", gT, [128, SEQ], BF16, [(gkey, i) for i in range(8)])
            if stop_after == "attn0":
                S.emit(sems, dsem)
                return nc, dump_list
    S.barrier()
    for pp in range(4):
        dump("mixd%d" % pp, mix_d[pp], [128, SEQ], BF16, [("mixd", pp)])
    if stop_after == "phaseA":
        S.emit(sems, dsem)
        return nc, dump_list
    M = Mem(PBASE)
    wstB = [M.f32(1024).rearrange("p (c f) -> p c f", c=8) for _ in range(2)]
    wbfB = M.bf(8 * 512).rearrange("p (c f) -> p c f", c=8)
    TB = M.off
    tabs = [[M.f32(512) for _ in range(2)] for _ in range(4)]
    qrawB = [M.bf(512) for _ in range(2)]
    t1B = [M.f32(512) for _ in range(2)]
    t2B = [M.f32(512) for _ in range(2)]
    grt = [M.bf(4 * 512).rearrange("p (f t) -> p f t", f=4) for _ in range(2)]
    TB_END = M.off
    qrT = [M.bf(SEQ) for _ in range(2)]
    krT = [M.bf(SEQ) for _ in range(2)]
    Vr = M.bf(32 * 512).rearrange("p (n v) -> p n v", n=32)
    WO_OFF = M.off
    wo = M.bf(8 * DM).rearrange("p (c f) -> p c f", c=8)

    for tt in range(8):
        for ft in range(2):
            tsl = (2 * tt + ft) % 2
            for k4 in range(4):
                S.op("sp", lambda e, tt=tt, ft=ft, k4=k4, tsl=tsl: e.dma_start(out=tabs[k4][tsl], in_=ropeR_d[4 * ft + k4, :, 512 * tt:512 * tt + 512]),
                     w=[("tab", k4, tsl)], dma=True)
            for qk in range(2):
                f = 2 * qk + ft
                pb = cnt["pb"] % 4
                cnt["pb"] += 1
                for c in range(8):
                    S.op("pe", lambda e, c=c, f=f, tt=tt, pb=pb: e.matmul(bank(pb), lhsT=wbf[:, c, 128 * f:128 * f + 128],
                                                                          rhs=hT[:, c, 512 * tt:512 * tt + 512], start=(c == 0), stop=(c == 7)),
                         r=[("wbf", f), ("hT", tt)], w=[("ps", pb)])
                flush()
                dst = (qrT if qk == 0 else krT)[ft][:, 512 * tt:512 * tt + 512]
                pend.append(rope(pb, dst, ("qrT" if qk == 0 else "krT", ft, tt), tabs[2 * qk][tsl], tabs[2 * qk + 1][tsl],
                                 [("tab", 2 * qk, tsl), ("tab", 2 * qk + 1, tsl)], qrawB, t1B, t2B))
    flush()
    dump("qrT0", qrT[0], [128, SEQ], BF16, [("qrT", 0, i) for i in range(8)])
    dump("krT0", krT[0], [128, SEQ], BF16, [("krT", 0, i) for i in range(8)])
    if stop_after == "B1":
        S.emit(sems, dsem)
        return nc, dump_list
    load_w(wstB, wbfB, 2560, 512, "wbfB", win_v, extra_w=[("wbf", f) for f in range(4)])
    for n in range(32):
        pb = cnt["pb"] % 4
        cnt["pb"] += 1
        for c in range(8):
            S.op("pe", lambda e, c=c, n=n, pb=pb: e.matmul(bank(pb), lhsT=hT[:, c, 128 * n:128 * n + 128], rhs=wbfB[:, c, :], start=(c == 0), stop=(c == 7)),
                 r=[("wbfB", i) for i in range(4)] + [("hT", n // 4)], w=[("ps", pb)])
        if n % 2 == 0:
            S.op("dve", lambda e, n=n, pb=pb: e.tensor_copy(out=Vr[:, n, :], in_=bank(pb)), w=[("ps", pb), ("Vr", n)])
        else:
            S.op("act", lambda e, n=n, pb=pb: e.activation(out=Vr[:, n, :], in_=bank(pb), func=AF.Copy), w=[("ps", pb), ("Vr", n)])
    load_w(wstB, wbfB, 3072, 512, "wbfB", win_v)
    for tt in range(8):
        gs = tt % 2
        for ft in range(4):
            pb = cnt["pb"] % 4
            cnt["pb"] += 1
            for c in range(8):
                S.op("pe", lambda e, c=c, ft=ft, tt=tt, pb=pb: e.matmul(bank(pb), lhsT=wbfB[:, c, 128 * ft:128 * ft + 128],
                                                                        rhs=hT[:, c, 512 * tt:512 * tt + 512], start=(c == 0), stop=(c == 7)),
                     r=[("wbfB", ft), ("hT", tt)], w=[("ps", pb)])
            S.op("act", lambda e, ft=ft, pb=pb, gs=gs: e.activation(out=grt[gs][:, ft, :], in_=bank(pb), func=AF.Silu), w=[("ps", pb), ("grt", gs, ft)])
            S.op("dve", lambda e, ft=ft, gs=gs: e.tensor_scalar(out=grt[gs][:, ft, :], in0=grt[gs][:, ft, :], scalar1=gn[:, ft:ft + 1], scalar2=None, op0=ALU.mult),
                 r=[("grt", gs, ft), "gn"], w=[("grt", gs, ft)])
        S.op("sp", lambda e, tt=tt, gs=gs: e.dma_start(out=grs_d[:, :, 512 * tt:512 * tt + 512].rearrange("f p t -> p f t"), in_=grt[gs]),
             r=[("grt", gs, ft) for ft in range(4)], w=[("grs", tt)], dma=True)
    if stop_after == "B3":
        S.emit(sems, dsem)
        return nc, dump_list
    S.barrier()

    M = Mem(TB)
    Ktok = [M.bf(256) for _ in range(2)]
    sT = [M.bf(512) for _ in range(2)]
    onb = [M.bf(512) for _ in range(2)]
    grl = [M.bf(4 * 512).rearrange("p (f t) -> p f t", f=4) for _ in range(2)]
    mixst = [M.bf(4 * 512).rearrange("p (f t) -> p f t", f=4) for _ in range(2)]
    assert M.off <= TB_END - 256
    mixA = Mem(0).bf(8 * SEQ).rearrange("p (c t) -> p c t", c=8)
    def mixa_load(c):
        S.op("sp", lambda e: e.dma_start(out=mixA[:, c, :], in_=mix_d[c]), r=[("mixd", c)], w=[("mixA", c)] + [("hT", i) for i in range(8)], dma=True)
    dump("mixa0e", mixA[:, 0:4, 0:512], [128, 4, 512], BF16, [("mixA", h) for h in range(4)])
    for pp in range(4):
        dump("mixe%d" % pp, mix_d[pp], [128, SEQ], BF16, [("mixd", pp)])
    def wo_chunk(j):
        sl = cnt["w"] % len(wstB)
        cnt["w"] += 1
        S.op("sp", lambda e: e.dma_start(out=wstB[sl], in_=wout_v[:, :, 128 * j:128 * j + 128]), w=[("wst", sl)], dma=True)
        S.op("pool", lambda e: e.tensor_copy(out=wo[:, :, 128 * j:128 * j + 128], in_=wstB[sl]), r=[("wst", sl)], w=[("wo", j)])
    S.op("dve", lambda e: e.memset(state, 0.0), w=["state"])
    nbias = bnst[:, 0:4]
    bnst2 = [bnst, Mem(TB_END - 256).f32(24)]
    bnmv2 = [bnmv, Mem(TB_END - 128).f32(8)]
    rstd2 = [rstd4, Mem(TB_END - 64).f32(4)]
    nb2 = [Mem(TB_END - 32).f32(4), Mem(TB_END - 16).f32(4)]

    OB = (2, 3, 6)

    def r_1(n):
        tsel = slice(128 * n, 128 * n + 128)
        ks = n % 2
        grp, sub4 = n // 4, n % 4
        gl = grp % 2
        if sub4 == 0:
            S.op("sp", lambda e: e.dma_start(out=grl[gl], in_=grs_d[:, :, 512 * grp:512 * grp + 512].rearrange("f p t -> p f t")),
                 r=[("grs", grp)], w=[("grl", gl)], dma=True)
        for ft in range(2):
            S.op("pe", lambda e, ft=ft: e.transpose(out=bankbf(0)[:, 128 * ft:128 * ft + 128], in_=krT[ft][:, tsel], identity=ident),
                 r=[("krT", ft, n // 4), "ident"], w=[("ps", 0)])
        S.op("act", lambda e: e.activation(out=Ktok[ks], in_=bankbf(0)[:, 0:256], func=AF.Copy), w=[("ps", 0), ("Ktok", ks)])
        S.op("pool", lambda e: e.tensor_tensor(out=Ktok[ks], in0=Ktok[ks], in1=gck, op=ALU.mult), r=["gck", ("Ktok", ks)], w=[("Ktok", ks)])
        for h in (0, 2, 1, 3):
            r0 = 64 * (h % 2)
            sbk = 1 if h % 2 == 0 else 7
            S.op("pe", lambda e, h=h, r0=r0, sbk=sbk: e.matmul(bank(sbk)[:, 128 * (h // 2):128 * (h // 2) + 128], lhsT=krT[h // 2][r0:r0 + 64, tsel],
                                                          rhs=qrT[h // 2][r0:r0 + 64, tsel], start=True, stop=True),
                 r=[("krT", h // 2, n // 4), ("qrT", h // 2, n // 4)], w=[("ps", sbk)])
        for hp, sbk in ((0, 1), (1, 7)):
            S.op("dve", lambda e, hp=hp, sbk=sbk: e.tensor_tensor(out=sT[ks][:, 256 * hp:256 * hp + 256], in0=bank(sbk)[:, 0:256], in1=rmask[:, 0:256], op=ALU.mult),
                 r=["rmask"], w=[("ps", sbk), ("sT", ks)])

    def r_2(n):
        tsel = slice(128 * n, 128 * n + 128)
        ks = n % 2
        ob = OB[n % 3]
        for h in range(4):
            r0 = 64 * (h % 2)
            S.op("pe", lambda e, h=h, r0=r0: e.matmul(bank(4)[r0:r0 + 64, 128 * (h // 2):128 * (h // 2) + 128], lhsT=Ktok[ks][:, 64 * h:64 * h + 64],
                                                 rhs=Vr[:, n, 128 * h:128 * h + 128], start=True, stop=True),
                 r=[("Ktok", ks), ("Vr", n)], w=[("ps", 4)])
        for h in range(4):
            r0 = 64 * (h % 2)
            sblk = 2 * (h % 2) + h // 2
            S.op("pe", lambda e, h=h, sblk=sblk: e.matmul(bank(ob)[:, 128 * h:128 * h + 128], lhsT=sT[ks][:, 128 * sblk:128 * sblk + 128],
                                                     rhs=Vr[:, n, 128 * h:128 * h + 128], start=True, stop=(n == 0)),
                 r=[("sT", ks), ("Vr", n)], w=[("ps", ob)])
            if n > 0:
                S.op("pe", lambda e, h=h, r0=r0: e.matmul(bank(ob)[:, 128 * h:128 * h + 128], lhsT=qrT[h // 2][r0:r0 + 64, tsel],
                                                     rhs=state_bf[r0:r0 + 64, 128 * (h // 2):128 * (h // 2) + 128], start=False, stop=True),
                     r=[("qrT", h // 2, n // 4), "state_bf"], w=[("ps", ob)])
        for cb in range(2):
            S.op("dve", lambda e, cb=cb: e.scalar_tensor_tensor(out=state[:, 128 * cb:128 * cb + 128], in0=state[:, 128 * cb:128 * cb + 128],
                                                                scalar=gsc[:, cb:cb + 1], in1=bank(4)[:, 128 * cb:128 * cb + 128], op0=ALU.mult, op1=ALU.add),
                 r=["state", "gsc"], w=["state", ("ps", 4)])
        S.op("act", lambda e: e.activation(out=state_bf, in_=state, func=AF.Copy), r=["state"], w=["state_bf"])

    def r_3(n):
        ks = n % 2
        ob = OB[n % 3]
        for h in range(4):
            S.op("dve", lambda e, h=h: e.bn_stats(out=bnst2[ks][:, 6 * h:6 * h + 6], in_=bank(ob)[:, 128 * h:128 * h + 128]), w=[("ps", ob), ("bnst", ks, h)])
        for h in range(4):
            S.op("dve", lambda e, h=h: e.bn_aggr(out=bnmv2[ks][:, 2 * h:2 * h + 2], in_=bnst2[ks][:, 6 * h:6 * h + 6]), r=[("bnst", ks, h)], w=[("bnmv", ks, h)])
        S.op("pool", lambda e: e.tensor_scalar(out=rstd2[ks], in0=bnmv2[ks][:, 1:8:2], scalar1=1e-5, scalar2=None, op0=ALU.add), r=[("bnmv", ks, h) for h in range(4)], w=[("rstd", ks)])
        S.op("pool", lambda e: e.tensor_tensor(out=rstd2[ks], in0=rstd2[ks], in1=mhalf, op=ALU.pow), r=[("rstd", ks), "mhalf"], w=[("rstd", ks)])
        S.op("pool", lambda e: e.tensor_tensor(out=nb2[ks], in0=bnmv2[ks][:, 0:8:2], in1=rstd2[ks], op=ALU.mult), r=[("bnmv", ks, h) for h in range(4)] + [("rstd", ks)], w=[("nb", ks)])
        S.op("pool", lambda e: e.tensor_scalar(out=nb2[ks], in0=nb2[ks], scalar1=-1.0, scalar2=None, op0=ALU.mult), r=[("nb", ks)], w=[("nb", ks)])

    def r_4(n):
        ks = n % 2
        ob = OB[n % 3]
        for h in range(4):
            S.op("act", lambda e, h=h: e.activation(out=onb[ks][:, 128 * h:128 * h + 128], in_=bank(ob)[:, 128 * h:128 * h + 128], func=AF.Identity,
                                                    bias=nb2[ks][:, h:h + 1], scale=rstd2[ks][:, h:h + 1]),
                 r=[("nb", ks), ("rstd", ks)], w=[("ps", ob), ("onb", ks, h)])

    def r_5(n):
        ks = n % 2
        grp, sub4 = n // 4, n % 4
        gl = grp % 2
        for h in range(4):
            S.op("pe", lambda e, h=h: e.transpose(out=bankbf(5)[:, 128 * h:128 * h + 128], in_=onb[ks][:, 128 * h:128 * h + 128], identity=ident),
                 r=[("onb", ks, h), "ident"], w=[("ps", 5)])
        S.op("dve", lambda e: e.tensor_tensor(out=mixA[:, 4:8, 128 * n:128 * n + 128], in0=bankbf(5)[:, 0:512].rearrange("p (h t) -> p h t", h=4),
                                              in1=grl[gl][:, :, 128 * sub4:128 * sub4 + 128], op=ALU.mult),
             r=[("grl", gl)], w=[("ps", 5)] + [("mixA", 4 + h) for h in range(4)])

    stages = ((r_1, 0), (r_2, 1), (r_3, 2), (r_4, 3), (r_5, 4))
    for step in range(32 + 4):
        for fn, sk in stages:
            if 0 <= step - sk < 32:
                fn(step - sk)
        if step % 4 == 2 and step // 4 < 8:
            wo_chunk(step // 4)
        if step % 4 == 1 and step // 4 < 4:
            mixa_load(step // 4)
    dump("mixr0", mixA[:, 4:8, 0:512], [128, 4, 512], BF16, [("mixA", 4 + h) for h in range(4)])
    dump("mixa0", mixA[:, 0:4, 0:512], [128, 4, 512], BF16, [("mixA", h) for h in range(4)])
    dump("wo", wo, [128, 8, DM], BF16, [("wo", j) for j in range(8)])
    if stop_after == "ret":
        S.emit(sems, dsem)
        return nc, dump_list
    S.barrier()

    M = Mem(PBASE + 16384)
    xr = [M.f32(DM) for _ in range(8)]
    yb = [M.f32(DM) for _ in range(4)]
    junk2s = [M.bf(DM) for _ in range(4)]
    fg = M.f32(DM)
    assert M.off <= WO_OFF
    S.op("sp", lambda e: e.dma_start(out=fg, in_=fg_d), w=["fg"], dma=True)
    S.op("dve", lambda e: e.memset(st_ss, 0.0), w=["st_ss"])
    def c_a(tt):
        sl = tt % 4
        xsl = tt % 8
        S.op("sp", lambda e: e.dma_start(out=xr[xsl], in_=x_d[128 * tt:128 * tt + 128, :]), w=[("xr", xsl)], dma=True)
        for nh in range(2):
            pb = (2 * tt + nh) % 8
            for c in range(8):
                S.op("pe", lambda e, c=c, nh=nh, pb=pb: e.matmul(bank(pb), lhsT=mixA[:, c, 128 * tt:128 * tt + 128], rhs=wo[:, c, 512 * nh:512 * nh + 512],
                                                                 start=(c == 0), stop=(c == 7)),
                     r=[("mixA", c)] + [("wo", 4 * nh + i) for i in range(4)], w=[("ps", pb)])
            S.op("dve", lambda e, nh=nh, pb=pb: e.tensor_tensor(out=yb[sl][:, 512 * nh:512 * nh + 512], in0=bank(pb), in1=xr[xsl][:, 512 * nh:512 * nh + 512], op=ALU.add),
                 r=[("xr", xsl)], w=[("ps", pb), ("yb", sl, nh)])
        S.op("act", lambda e: e.activation(out=junk2s[tt % 4], in_=yb[sl], func=AF.Square, accum_out=st_ss[:, tt:tt + 1]),
             r=[("yb", sl, 0), ("yb", sl, 1), "st_ss"], w=[("junk2", tt % 4), ("ss", tt)])

    def c_b(tt):
        S.op("dve", lambda e: e.tensor_scalar(out=st_ms[:, tt:tt + 1], in0=st_ss[:, tt:tt + 1], scalar1=1.0 / DM, scalar2=1e-6,
                                              op0=ALU.mult, op1=ALU.add), r=[("ss", tt)], w=[("ms", tt)])
        S.op("pool", lambda e: e.tensor_tensor(out=st_rs[:, tt:tt + 1], in0=st_ms[:, tt:tt + 1], in1=mhalf[:, 0:1], op=ALU.pow),
             r=[("ms", tt), "mhalf"], w=[("rs", tt)])

    def c_c(tt):
        sl = tt % 4
        S.op("dve", lambda e: e.scalar_tensor_tensor(out=yb[sl], in0=yb[sl], scalar=st_rs[:, tt:tt + 1], in1=fg, op0=ALU.mult, op1=ALU.mult),
             r=[("yb", sl, 0), ("yb", sl, 1), ("rs", tt), "fg"], w=[("yb", sl, 0), ("yb", sl, 1)])
        S.op("pool", lambda e: e.dma_start(out=out_d[128 * tt:128 * tt + 128, :], in_=yb[sl]), r=[("yb", sl, 0), ("yb", sl, 1)], w=[("out", tt)], dma=True)

    for step in range(32 + 2):
        if step < 32:
            c_a(step)
        if 1 <= step < 33:
            c_b(step - 1)
        if step >= 2:
            c_c(step - 2)
    S.emit(sems, dsem)
    return nc, dump_list


_CACHE = {}


def _prep_inputs(x, norm_gain, w_in, ret_gn_gain, w_out, final_gain):
    c = _host_constants()
    w_in_p = np.ascontiguousarray(np.asarray(w_in, np.float32)[0][:, _win_cols()])
    shared = dict(c)
    shared["w_in"] = w_in_p
    shared["w_out"] = np.ascontiguousarray(np.asarray(w_out, np.float32)[0])
    shared["ng"] = np.ascontiguousarray(np.asarray(norm_gain, np.float32)[0].reshape(8, 128).T)
    shared["gn"] = np.ascontiguousarray(np.asarray(ret_gn_gain, np.float32)[0].reshape(4, 128).T)
    shared["fg"] = np.ascontiguousarray(np.broadcast_to(np.asarray(final_gain, np.float32)[None, :], (128, DM)))
    x = np.asarray(x, np.float32)
    return [dict(shared, x=np.ascontiguousarray(x[b])) for b in range(x.shape[0])]


def kernel(x, norm_gain, w_in, ret_gn_gain, w_out, final_gain):
    in_maps = _prep_inputs(x, norm_gain, w_in, ret_gn_gain, w_out, final_gain)
    nc, _ = build()
    res = run_bass_kernel_spmd(nc, in_maps, core_ids=list(range(NCORES)))
    return np.stack([np.asarray(r["out"], np.float32) for r in res.results], axis=0)
```

```python
import numpy as np
import ml_dtypes
import concourse.bass as bass
import concourse.mybir as mybir
from concourse.bass_utils import run_bass_kernel_spmd

dt = mybir.dt
F32 = dt.float32
BF16 = dt.bfloat16
AF = mybir.ActivationFunctionType
ALU = mybir.AluOpType
AX = mybir.AxisListType


class Sched:
    COMPUTE = ("pe", "act", "dve", "pool")
    EPOCH = 400
    STRICT = True

    def __init__(self, nc, dma_ring=None):
        self.nc = nc
        self.ops = []
        self.lastw = {}
        self.readers = {}
        self.last_on = {}
        self.extra = {}
        self.dma_live = []
        self.dma_ring = dma_ring or {"sp": 24, "pool": 12, "act": 6}

    def op(self, eng, fn, r=(), w=(), dma=False):
        i = len(self.ops)
        deps = set(self.extra.pop(eng, ()))
        for k in r:
            s = self.lastw.get(k)
            if s is not None:
                deps.add((s, 0))
        for k in w:
            s = self.lastw.get(k)
            if s is not None:
                deps.add((s, 1))
            for s in self.readers.get(k, ()):
                deps.add((s, 2))
        keep = set()
        for (s, kind) in deps:
            so = self.ops[s]
            if so["dma"] or dma:
                keep.add(s)
            elif so["eng"] == eng:
                if eng != "pe" and (kind == 0 or self.STRICT):
                    keep.add(s)
            else:
                keep.add(s)
        keep.discard(i)
        self.ops.append(dict(eng=eng, fn=fn, deps=sorted(keep), dma=dma, signal=dma))
        for s in keep:
            self.ops[s]["signal"] = True
        for k in w:
            self.lastw[k] = i
            self.readers[k] = []
        for k in r:
            lst = self.readers.setdefault(k, [])
            if not dma:
                lst[:] = [s for s in lst if self.ops[s]["dma"] or self.ops[s]["eng"] != eng]
            lst.append(i)
        self.last_on[eng] = i
        if dma:
            self.dma_live.append(i)
        return i

    def barrier(self):
        srcs = set(self.last_on.values()) | set(self.dma_live)
        self.dma_live = []
        for e in ("pe", "act", "dve", "pool", "sp"):
            self.extra.setdefault(e, set())
            for s in srcs:
                so = self.ops[s]
                if so["eng"] == e and not so["dma"]:
                    continue
                self.extra[e].add((s, 0))
                so["signal"] = True

    def emit(self, sems, dma_sems):
        nc = self.nc
        cnt = {e: 0 for e in self.COMPUTE}
        dcnt = {q: 0 for q in dma_sems}
        dval = {q: [0] * len(dma_sems[q]) for q in dma_sems}
        for o in self.ops:
            if o["dma"]:
                q = o["eng"]
                j = dcnt[q] % len(dma_sems[q])
                dcnt[q] += 1
                o["ring_prev"] = (dma_sems[q][j], dval[q][j]) if dval[q][j] > 0 else None
                dval[q][j] += 16
                o["done"] = (dma_sems[q][j], dval[q][j])
            elif o["signal"]:
                k = cnt[o["eng"]]
                cnt[o["eng"]] += 1
                o["done"] = (sems[o["eng"]][k // self.EPOCH], k % self.EPOCH + 1)
        by_eng = {}
        for o in self.ops:
            by_eng.setdefault(o["eng"], []).append(o)
        final_waits = []
        for q in dma_sems:
            for j, s in enumerate(dma_sems[q]):
                if dval[q][j] > 0:
                    final_waits.append((s, dval[q][j]))
        handles = {"pe": nc.tensor, "act": nc.scalar, "dve": nc.vector, "pool": nc.gpsimd, "sp": nc.sync}
        ops_all = self.ops

        def run(eng, e):
            waited = {}

            def wait(sem, val):
                key = id(sem)
                if waited.get(key, 0) < val:
                    e.wait_ge(sem, val)
                    waited[key] = val

            for o in by_eng.get(eng, []):
                for s in o["deps"]:
                    sem, val = ops_all[s]["done"]
                    wait(sem, val)
                if o["dma"] and o["ring_prev"] is not None:
                    wait(*o["ring_prev"])
                ins = o["fn"](e)
                if o["signal"]:
                    ins.then_inc(o["done"][0], 16 if o["dma"] else 1)
            if eng == "sp":
                for (s, v) in final_waits:
                    wait(s, v)

        with nc.Block() as block:
            @block.sync
            def _(e):
                run("sp", e)

            @block.tensor
            def _(e):
                run("pe", e)

            @block.scalar
            def _(e):
                run("act", e)

            @block.vector
            def _(e):
                run("dve", e)

            @block.gpsimd
            def _(e):
                run("pool", e)


SEQ = 4096
DM = 1024
NCORES = 8
THETA = 10000.0
_bf = ml_dtypes.bfloat16


def _host_constants():
    c = {}
    c["ident"] = np.eye(128, dtype=np.float32).astype(_bf)
    permm = np.zeros((128, 128), np.float32)
    for m in range(128):
        k = m + 32 if (m % 64) < 32 else m - 32
        permm[k, m] = 1.0
    c["permm"] = permm.astype(_bf)
    kk = np.arange(128)[:, None]
    qq = np.arange(128)[None, :]
    prev = (kk >= qq).astype(np.float32)
    own = (qq >= kk).astype(np.float32)
    c["amask"] = np.tile(np.concatenate([own, prev], 1), (1, 4)).astype(_bf)
    c["rmask"] = np.tile(own, (1, 4)).astype(_bf)
    pos = np.arange(SEQ, dtype=np.float64)
    j = np.arange(128) % 32
    sign = np.where((np.arange(128) % 64) >= 32, 1.0, -1.0)[:, None]
    inv = THETA ** (-(np.arange(32, dtype=np.float64)) / 32.0)
    ang = pos[None, :] * inv[j][:, None]
    c["ropeA"] = np.stack([np.cos(ang), np.sin(ang) * sign]).astype(np.float32)
    invr = 1.0 / (THETA ** np.linspace(0.0, 1.0, 32, dtype=np.float64))
    angr = pos[None, :] * invr[j][:, None]
    cosR = np.cos(angr)
    sinR = np.sin(angr) * sign
    lg = np.log1p(-(2.0 ** (-5.0 - np.arange(4, dtype=np.float64))))
    i_in = (np.arange(SEQ) % 128).astype(np.float64)
    ropeR = np.zeros((8, 128, SEQ), np.float32)
    for ft in range(2):
        for hp in range(2):
            h = 2 * ft + hp
            rows = slice(64 * hp, 64 * hp + 64)
            dq = np.exp((i_in + 1.0) * lg[h])[None, :]
            dk = (np.exp(-(i_in + 1.0) * lg[h]) / 8.0)[None, :]
            ropeR[ft * 4 + 0, rows] = cosR[rows] * dq
            ropeR[ft * 4 + 1, rows] = sinR[rows] * dq
            ropeR[ft * 4 + 2, rows] = cosR[rows] * dk
            ropeR[ft * 4 + 3, rows] = sinR[rows] * dk
    c["ropeR"] = ropeR
    gC = np.exp(128.0 * lg)
    gck = np.zeros((128, 256), np.float32)
    for h in range(4):
        gck[:, h * 64:(h + 1) * 64] = gC[h]
    c["gck"] = gck
    gsc = np.zeros((128, 2), np.float32)
    for cb in range(2):
        gsc[:64, cb] = gC[2 * cb]
        gsc[64:, cb] = gC[2 * cb + 1]
    c["gsc"] = gsc
    return c


def _win_cols():
    cols = []
    for p in range(4):
        for base in (0, 512, 1024, 1536):
            cols += list(range(base + 128 * p, base + 128 * p + 128))
    permh = [2 * i for i in range(32)] + [2 * i + 1 for i in range(32)]
    for base in (2048, 2304):
        for h in range(4):
            cols += [base + 64 * h + cc for cc in permh]
    cols += list(range(2560, 3584))
    return np.asarray(cols)


def _tok(buf, b, n):
    if b == 0:
        return buf[:, 128 * n:128 * n + 128]
    if b == 1:
        r, m = n // 8, n % 8
        s = r + 512 * m
        return buf[:, s:s + 509:4]
    r, m = n // 2, n % 2
    s = r + 2048 * m
    return buf[:, s:s + 2033:16]


def _tok2(buf, b, n, nblk):
    cntk = 128 * nblk
    if b == 0:
        return buf[:, 128 * n:128 * n + cntk]
    if b == 1:
        r, m = n // 8, n % 8
        s = r + 512 * m
        return buf[:, s:s + 4 * (cntk - 1) + 1:4]
    r, m = n // 2, n % 2
    s = r + 2048 * m
    return buf[:, s:s + 16 * (cntk - 1) + 1:16]


def _tok_keys(name, b, n):
    if b == 0:
        return [(name, n // 4)]
    if b == 1:
        return [(name, n % 8)]
    return [(name, 4 * (n % 2) + i) for i in range(4)]


BPC = (32, 8, 2)


def build(stop_after=None, dumps=()):
    nc = bass.Bass("TRN2", target_bir_lowering=False)
    from contextlib import ExitStack
    es = ExitStack()

    def din(name, shape, d=F32):
        return nc.dram_tensor(name, shape, d, kind="ExternalInput").ap()

    x_d = din("x", [SEQ, DM])
    win_d = din("w_in", [DM, 3584])
    wout_d = din("w_out", [DM, DM])
    ng_d = din("ng", [128, 8])
    gn_d = din("gn", [128, 4])
    fg_d = din("fg", [128, DM])
    ident_d = din("ident", [128, 128], BF16)
    permm_d = din("permm", [128, 128], BF16)
    amask_d = din("amask", [128, 1024], BF16)
    rmask_d = din("rmask", [128, 512], BF16)
    ropeA_d = din("ropeA", [2, 128, SEQ])
    ropeR_d = din("ropeR", [8, 128, SEQ])
    gck_d = din("gck", [128, 256])
    gsc_d = din("gsc", [128, 2])
    out_d = nc.dram_tensor("out", [SEQ, DM], F32, kind="ExternalOutput").ap()
    mix_d = nc.dram_tensor("mixs", [8, 128, SEQ], BF16, kind="Internal").ap()
    grs_d = nc.dram_tensor("grs", [4, 128, SEQ], BF16, kind="Internal").ap()
    dn_d = nc.dram_tensor("dns", [2, SEQ], F32, kind="Internal").ap()
    rn_d = nc.dram_tensor("rns", [2, SEQ], F32, kind="Internal").ap()
    win_v = win_d.rearrange("(c p) f -> p c f", p=128)
    wout_v = wout_d.rearrange("(c p) f -> p c f", p=128)

    ARENA_BYTES = 211968
    arena = es.enter_context(nc.sbuf_tensor("arena", [128, ARENA_BYTES // 4], F32))
    psum = es.enter_context(nc.psum_tensor("psum", [128, 4096], F32))
    sems = {e: [es.enter_context(nc.semaphore(f"s_{e}{i}")) for i in range(n)]
            for e, n in (("pe", 12), ("act", 6), ("dve", 12), ("pool", 4))}
    dsem = {q: [es.enter_context(nc.semaphore(f"d_{q}{i}")) for i in range(n)] for q, n in (("sp", 28), ("pool", 8))}
    S = Sched(nc)

    class Mem:
        def __init__(self, base):
            self.off = base

        def f32(self, n):
            o = self.off
            self.off += 4 * n
            assert self.off <= ARENA_BYTES, self.off
            return arena[:, o // 4:o // 4 + n]

        def bf(self, n):
            o = self.off
            self.off += 2 * n
            assert self.off % 4 == 0 and self.off <= ARENA_BYTES, self.off
            return arena[:, o // 4:o // 4 + n // 2].bitcast(BF16)

    def bank(i, n=512):
        return psum[:, 512 * i:512 * i + n]

    def bankbf(i):
        return psum[:, 512 * i:512 * i + 512].bitcast(BF16)

    dump_list = []

    def dump(name, ap, shape, d, keys):
        if name not in dumps:
            return
        t = nc.dram_tensor("dbg_" + name, shape, d, kind="ExternalOutput").ap()
        S.op("sp", lambda e, t=t, ap=ap: e.dma_start(out=t, in_=ap), r=keys, dma=True)
        dump_list.append(name)

    M = Mem(0)
    hT = M.bf(8 * SEQ).rearrange("p (c t) -> p c t", c=8)
    ident = M.bf(128)
    permm = M.bf(128)
    amask = M.bf(1024)
    rmask = M.bf(512)
    gck = M.f32(256)
    ng = M.f32(8)
    gn = M.f32(4)
    gsc = M.f32(2)
    st_ss = M.f32(32)
    st_ms = M.f32(32)
    st_rs = M.f32(32)
    state = M.f32(256)
    state_bf = M.bf(256)
    bnst = M.f32(24)
    bnmv = M.f32(8)
    rstd4 = M.f32(4)
    mhalf = M.f32(4)
    dtile = [M.f32(32) for _ in range(2)]
    PBASE = (M.off + 63) // 64 * 64

    S.op("sp", lambda e: e.dma_start(out=ident, in_=ident_d), w=["ident"], dma=True)
    S.op("sp", lambda e: e.dma_start(out=permm, in_=permm_d), w=["permm"], dma=True)
    S.op("sp", lambda e: e.dma_start(out=amask, in_=amask_d), w=["amask"], dma=True)
    S.op("sp", lambda e: e.dma_start(out=rmask, in_=rmask_d), w=["rmask"], dma=True)
    S.op("sp", lambda e: e.dma_start(out=gck, in_=gck_d), w=["gck"], dma=True)
    S.op("sp", lambda e: e.dma_start(out=ng, in_=ng_d), w=["ng"], dma=True)
    S.op("sp", lambda e: e.dma_start(out=gn, in_=gn_d), w=["gn"], dma=True)
    S.op("sp", lambda e: e.dma_start(out=gsc, in_=gsc_d), w=["gsc"], dma=True)
    S.op("dve", lambda e: e.memset(st_ss, 0.0), w=["st_ss"])
    S.op("dve", lambda e: e.memset(mhalf, -0.5), w=["mhalf"])

    M = Mem(PBASE)
    xs = [M.f32(DM) for _ in range(8)]
    junks = [M.bf(DM) for _ in range(4)]
    xn = [M.bf(DM) for _ in range(4)]
    def pro_a(tt):
        sl = tt % 8
        S.op("sp", lambda e: e.dma_start(out=xs[sl], in_=x_d[128 * tt:128 * tt + 128, :]), w=[("xs", sl)], dma=True)
        S.op("act", lambda e: e.activation(out=junks[tt % 4], in_=xs[sl], func=AF.Square, accum_out=st_ss[:, tt:tt + 1]),
             r=[("xs", sl), "st_ss"], w=[("junk", tt % 4), ("ss", tt)])
        S.op("dve", lambda e: e.tensor_scalar(out=st_ms[:, tt:tt + 1], in0=st_ss[:, tt:tt + 1], scalar1=1.0 / DM, scalar2=1e-6,
                                              op0=ALU.mult, op1=ALU.add), r=[("ss", tt)], w=[("ms", tt)])
        S.op("pool", lambda e: e.tensor_tensor(out=st_rs[:, tt:tt + 1], in0=st_ms[:, tt:tt + 1], in1=mhalf[:, 0:1], op=ALU.pow),
             r=[("ms", tt), "mhalf"], w=[("rs", tt)])

    def pro_b(tt):
        xsl = tt % 8
        sl = tt % 4
        pb = tt % 4
        if tt % 2 == 0:
            S.op("act", lambda e: e.activation(out=xn[sl], in_=xs[xsl], func=AF.Copy, scale=st_rs[:, tt:tt + 1]),
                 r=[("xs", xsl), ("rs", tt)], w=[("xn", sl)])
        else:
            S.op("dve", lambda e: e.tensor_scalar(out=xn[sl], in0=xs[xsl], scalar1=st_rs[:, tt:tt + 1], scalar2=None, op0=ALU.mult),
                 r=[("xs", xsl), ("rs", tt)], w=[("xn", sl)])
        for c in range(8):
            S.op("pe", lambda e, c=c: e.transpose(out=bankbf(pb)[:, 128 * c:128 * c + 128], in_=xn[sl][:, 128 * c:128 * c + 128], identity=ident),
                 r=[("xn", sl), "ident"], w=[("ps", pb)])
        S.op("dve", lambda e: e.tensor_copy(out=hT[:, :, 128 * tt:128 * tt + 128], in_=bankbf(pb).rearrange("p (c t) -> p c t", c=8)),
             w=[("ps", pb), ("hT", tt // 4)])

    for step in range(32 + 2):
        if step < 32:
            pro_a(step)
        if step >= 2:
            pro_b(step - 2)
    dump("hT", hT, [128, 8, SEQ], BF16, [("hT", i) for i in range(8)])
    if stop_after == "prologue":
        S.emit(sems, dsem)
        return nc, dump_list

    S.barrier()

    cnt = {"w": 0, "rope": 0, "pb": 0, "vt": 0, "sg": 0, "pt": 0, "ob": 0}

    def sub(ap, extra):
        return bass.AP(ap.tensor, ap.offset + extra[0], [list(ap.ap[0])] + [list(x) for x in extra[1]])

    def load_w(wst, wbf, col0, ncols, tag, src, scale=True, extra_w=()):
        for j in range(ncols // 128):
            sl = cnt["w"] % len(wst)
            cnt["w"] += 1
            S.op("sp", lambda e, sl=sl, j=j: e.dma_start(out=wst[sl], in_=src[:, :, col0 + 128 * j:col0 + 128 * j + 128]),
                 w=[("wst", sl)] + list(extra_w), dma=True)
            if scale:
                S.op("pool", lambda e, sl=sl, j=j: e.tensor_tensor(out=wbf[:, :, 128 * j:128 * j + 128], in0=wst[sl],
                                                                    in1=ng.unsqueeze(2).to_broadcast([128, 8, 128]), op=ALU.mult),
                     r=[("wst", sl), "ng"], w=[(tag, j)] + list(extra_w))
            else:
                S.op("pool", lambda e, sl=sl, j=j: e.tensor_copy(out=wbf[:, :, 128 * j:128 * j + 128], in_=wst[sl]),
                     r=[("wst", sl)], w=[(tag, j)])

    def rope(pb, dst, dstkey, cos, sin, ckeys, qraw, t1, t2):
        sl = cnt["rope"] % 2
        cnt["rope"] += 1
        rb = 4 + sl
        aeng = "pool" if (cnt["rope"] // 2) % 2 == 0 else "dve"
        S.op("act", lambda e: e.activation(out=qraw[sl], in_=bank(pb), func=AF.Copy), w=[("ps", pb), ("qraw", sl)])
        S.op("dve", lambda e: e.tensor_tensor(out=t1[sl], in0=bank(pb), in1=cos, op=ALU.mult), r=[ckeys[0]], w=[("ps", pb), ("t1", sl)])

        def stage_b():
            S.op("pe", lambda e: e.matmul(bank(rb), lhsT=permm, rhs=qraw[sl], start=True, stop=True), r=["permm", ("qraw", sl)], w=[("ps", rb)])
            S.op("dve", lambda e: e.tensor_tensor(out=t2[sl], in0=bank(rb), in1=sin, op=ALU.mult), r=[ckeys[1]], w=[("ps", rb), ("t2", sl)])
            S.op(aeng, lambda e: e.tensor_tensor(out=dst, in0=t1[sl], in1=t2[sl], op=ALU.add), r=[("t1", sl), ("t2", sl)], w=[dstkey])
        return stage_b

    pend = []

    def flush():
        while pend:
            pend.pop(0)()

    M = Mem(PBASE)
    wst = [M.f32(1024).rearrange("p (c f) -> p c f", c=8) for _ in range(1)]
    wbf = M.bf(8 * 512).rearrange("p (c f) -> p c f", c=8)
    ROPE_OFF = M.off
    cosT = [M.f32(512) for _ in range(2)]
    sinT = [M.f32(512) for _ in range(2)]
    qraw = [M.bf(512) for _ in range(2)]
    t1 = [M.f32(512) for _ in range(2)]
    t2 = [M.f32(512) for _ in range(2)]
    qT = M.bf(SEQ)
    kT = M.bf(SEQ)
    vT = M.bf(SEQ)
    gTs = [M.bf(SEQ) for _ in range(2)]
    Vtok = [M.bf(32 * 192) for _ in range(2)]
    PT = [M.bf(1024) for _ in range(4)]
    accs = [M.f32(SEQ), M.f32(SEQ)]
    PHASEA_END = M.off
    Q1 = Mem(ROPE_OFF).bf(SEQ)
    Q2 = Mem(ROPE_OFF + 8192).bf(SEQ)
    QK = {1: [("cos", 0), ("cos", 1), ("sin", 0), ("sin", 1)],
          2: [("qraw", 0), ("qraw", 1), ("t1", 0), ("t1", 1), ("t2", 0)]}

    for vs in range(2):
        S.op("pool", lambda e, vs=vs: e.memset(sub(Vtok[vs], (64, [[192, 32], [1, 64]])), 1.0), w=[("Vtok", vs)])

    def acc_view(acc, b, g):
        if b == 0:
            return acc[:, 512 * g:512 * g + 512], [("acc", g)], None
        if b == 1:
            r4, m0 = g // 2, 4 * (g % 2)
            s = r4 + 512 * m0
            return acc[:, s:s + 2045:4], [("acc", m0 + i) for i in range(4)], None
        return sub(acc, (2 * g, [[1, 2], [16, 256]])), [("acc", i) for i in range(8)], "p (r l) -> p r l"

    load_w(wst, wbf, 0, 512, "wbf", win_v)
    tails = []
    for p in range(4):
        gT = gTs[p % 2]
        gkey = "gT%d" % (p % 2)
        for tt in range(8):
            tsl = tt % 2
            S.op("sp", lambda e, tt=tt, tsl=tsl: e.dma_start(out=cosT[tsl], in_=ropeA_d[0, :, 512 * tt:512 * tt + 512]), w=[("cos", tsl)], dma=True)
            S.op("sp", lambda e, tt=tt, tsl=tsl: e.dma_start(out=sinT[tsl], in_=ropeA_d[1, :, 512 * tt:512 * tt + 512]), w=[("sin", tsl)], dma=True)
            for f in range(4):
                pb = cnt["pb"] % 4
                cnt["pb"] += 1
                for c in range(8):
                    S.op("pe", lambda e, c=c, f=f, tt=tt, pb=pb: e.matmul(bank(pb), lhsT=wbf[:, c, 128 * f:128 * f + 128],
                                                                          rhs=hT[:, c, 512 * tt:512 * tt + 512], start=(c == 0), stop=(c == 7)),
                         r=[("wbf", f), ("hT", tt)], w=[("ps", pb)])
                flush()
                tsel = slice(512 * tt, 512 * tt + 512)
                if f < 2:
                    pend.append(rope(pb, (qT if f == 0 else kT)[:, tsel], ("qT" if f == 0 else "kT", tt), cosT[tsl], sinT[tsl],
                                     [("cos", tsl), ("sin", tsl)], qraw, t1, t2))
                elif f == 2:
                    S.op("act", lambda e, pb=pb, tsel=tsel: e.activation(out=vT[:, tsel], in_=bank(pb), func=AF.Copy), w=[("ps", pb), ("vT", tt)])
                else:
                    S.op("act", lambda e, pb=pb, tsel=tsel, gT=gT: e.activation(out=gT[:, tsel], in_=bank(pb), func=AF.Silu), w=[("ps", pb), (gkey, tt)])
            if tt == 2:
                while tails:
                    tails.pop(0)()
        flush()
        if p == 0:
            dump("qT0", qT, [128, SEQ], BF16, [("qT", i) for i in range(8)])
            dump("kT0", kT, [128, SEQ], BF16, [("kT", i) for i in range(8)])
            dump("vT0", vT, [128, SEQ], BF16, [("vT", i) for i in range(8)])
            dump("gT0", gT, [128, SEQ], BF16, [(gkey, i) for i in range(8)])
            if stop_after == "proj0":
                S.emit(sems, dsem)
                return nc, dump_list
        if p < 3:
            load_w(wst, wbf, 512 * (p + 1), 512, "wbf", win_v)
        else:
            load_w(wst, wbf, 2048, 512, "wbf", win_v)

        def vtrans_ops(b, vs, tbs):
            ops = []
            for g8 in range(4):
                tb = tbs[g8]
                grp = []
                for j in range(8):
                    n = 8 * g8 + j
                    grp.append(("pe", lambda e, j=j, n=n, tb=tb, b=b: e.transpose(out=bankbf(tb)[:, 128 * j:128 * j + 128], in_=_tok(vT, b, n), identity=ident),
                                _tok_keys("vT", b, n) + ["ident"], [("ps", tb)]))
                grp.append(("dve", lambda e, g8=g8, tb=tb, vs=vs: e.tensor_copy(out=sub(Vtok[vs], (192 * 8 * g8, [[192, 8], [128, 2], [1, 64]])),
                                                                          in_=bankbf(tb).rearrange("p (n h d) -> p n h d", n=8, h=2)),
                            [], [("ps", tb), ("Vtok", vs)]))
                ops.append(grp)
            return ops

        def emit_ops(lst):
            for (eng, fn, r, w) in lst:
                S.op(eng, fn, r=r, w=w)

        groups = []
        for b in range(3):
            vs = b % 2
            for h in range(2):
                for g in range(8):
                    groups.append((b, h, g, vs))
        vt_pending = {}
        emit_ops([o for grp in vtrans_ops(0, 0, (1, 3, 5, 1)) for o in grp])

        def s_ops(i):
            b, h, g, vs = groups[i]
            r0 = 64 * h
            sg = i % 3
            Sps = psum[:, 1024 * sg:1024 * sg + 1024]
            for j in range(4):
                kb = 4 * g + j
                last = (kb % BPC[b] == BPC[b] - 1)
                sb_ = ("ps", 2 * sg + j // 2)
                if b >= 1:
                    nq = 1 if last else 2
                    Qd = Q1 if b == 1 else Q2
                    S.op("pe", lambda e, j=j, kb=kb, Sps=Sps, b=b, r0=r0, nq=nq, Qd=Qd: e.matmul(
                        Sps[:, 256 * j:256 * j + 128 * nq], lhsT=_tok(kT[r0:r0 + 64], b, kb),
                        rhs=Qd[r0:r0 + 64, 128 * kb:128 * kb + 128 * nq], start=True, stop=True),
                         r=_tok_keys("kT", b, kb) + QK[b], w=[sb_])
                elif not last:
                    S.op("pe", lambda e, j=j, kb=kb, Sps=Sps, b=b, r0=r0: e.matmul(Sps[:, 256 * j:256 * j + 256], lhsT=_tok(kT[r0:r0 + 64], b, kb),
                                                                            rhs=_tok2(qT[r0:r0 + 64], b, kb, 2), start=True, stop=True),
                         r=_tok_keys("kT", b, kb) + _tok_keys("qT", b, kb) + _tok_keys("qT", b, kb + 1), w=[sb_])
                else:
                    S.op("pe", lambda e, j=j, kb=kb, Sps=Sps, b=b, r0=r0: e.matmul(Sps[:, 256 * j:256 * j + 128], lhsT=_tok(kT[r0:r0 + 64], b, kb),
                                                                            rhs=_tok(qT[r0:r0 + 64], b, kb), start=True, stop=True),
                         r=_tok_keys("kT", b, kb) + _tok_keys("qT", b, kb), w=[sb_])

        def ew_ops(i):
            b, h, g, vs = groups[i]
            sg = i % 3
            pt = i % 4
            Sps = psum[:, 1024 * sg:1024 * sg + 1024]
            S.op("act", lambda e, Sps=Sps, pt=pt: e.activation(out=PT[pt], in_=Sps, func=AF.Exp, scale=0.125),
                 w=[("ps", 2 * sg), ("ps", 2 * sg + 1), ("PT", pt)])
            meng = "dve"
            S.op(meng, lambda e, pt=pt: e.tensor_tensor(out=PT[pt], in0=PT[pt], in1=amask, op=ALU.mult), r=["amask", ("PT", pt)], w=[("PT", pt)])

        def pv_ops(i):
            b, h, g, vs = groups[i]
            vc0 = 0 if h == 0 else 64
            pt = i % 4
            ob = 6 + i % 2
            acc = accs[h]
            n0 = 4 * g
            started = False

            def vaug(n):
                return Vtok[vs][:, 192 * n + vc0:192 * n + vc0 + 128]

            def mm(o0, ncols, n, rhs, keys, stop):
                nonlocal started
                st = not started
                started = True
                S.op("pe", lambda e: e.matmul(bank(ob)[:, o0:o0 + ncols], lhsT=vaug(n), rhs=rhs, start=st, stop=stop, skip_group_check=True),
                     r=keys + [("Vtok", vs)], w=[("ps", ob)])

            if n0 % BPC[b] != 0:
                ptp = (i - 1) % 4
                mm(0, 128, n0 - 1, PT[ptp][:, 896:1024], [("PT", ptp)], False)
            for j in range(4):
                kb = n0 + j
                last = (kb % BPC[b] == BPC[b] - 1)
                if (not last) and j < 3:
                    mm(128 * j, 256, kb, PT[pt][:, 256 * j:256 * j + 256], [("PT", pt)], False)
                else:
                    mm(128 * j, 128, kb, PT[pt][:, 256 * j:256 * j + 128], [("PT", pt)], j == 3)
            view, akeys, rr = acc_view(acc, b, g)
            akeys = [(k[0] + str(h), k[1]) for k in akeys]
            src = bank(ob) if rr is None else bank(ob).rearrange(rr, r=2)
            if b == 0:
                S.op("act", lambda e, view=view, src=src: e.activation(out=view, in_=src, func=AF.Copy), w=[("ps", ob)] + akeys)
            else:
                S.op("dve", lambda e, view=view, src=src: e.tensor_tensor(out=view, in0=src, in1=view, op=ALU.add),
                     r=akeys, w=[("ps", ob)] + akeys)

        def den_ops(h):
            drow = 64 if h == 0 else 0
            ak = [("acc" + str(h), i) for i in range(8)]
            S.op("pool", lambda e, h=h, drow=drow: e.dma_start(out=dn_d[h:h + 1, :], in_=accs[h][drow:drow + 1, :]), r=ak, w=[("dn", h)], dma=True)
            S.op("pool", lambda e, h=h: e.dma_start(out=dtile[h], in_=dn_d[h].rearrange("(p j) -> p j", j=32)), r=[("dn", h)], w=[("dt", h)], dma=True)
            S.op("dve", lambda e, h=h: e.reciprocal(out=dtile[h], in_=dtile[h]), r=[("dt", h)], w=[("dt", h)])
            S.op("pool", lambda e, h=h: e.dma_start(out=rn_d[h].rearrange("(p j) -> p j", j=32), in_=dtile[h]), r=[("dt", h)], w=[("rn", h)], dma=True)

        NG = len(groups)
        LOOK = 3
        for i in range(min(LOOK, NG)):
            s_ops(i)
        for i in range(min(LOOK - 1, NG)):
            ew_ops(i)
        for i in range(NG):
            b, h, g, vs = groups[i]
            pv_ops(i)
            if i == 3 or i == 10:
                bq = 1 if i == 3 else 2
                rr_ = 4 if bq == 1 else 16
                S.op("dve", lambda e, bq=bq, rr_=rr_: e.tensor_copy(out=(Q1 if bq == 1 else Q2).rearrange("p (r l) -> p r l", r=rr_),
                                                              in_=qT.rearrange("p (l r) -> p r l", r=rr_)),
                     r=[("qT", tt) for tt in range(8)], w=QK[bq])
            if h == 1 and b < 2 and g % 2 == 0:
                if g == 0:
                    vt_pending[b + 1] = vtrans_ops(b + 1, (b + 1) % 2, tuple(2 * ((i + 2 * k) % 3) + 1 for k in range(4)))
                emit_ops(vt_pending[b + 1][g // 2])
            if i + LOOK < NG:
                s_ops(i + LOOK)
            if i + LOOK - 1 < NG:
                ew_ops(i + LOOK - 1)
            if b == 2 and g == 7:
                den_ops(h)
        if p == 0:
            dump("acc0", accs[0], [128, SEQ], F32, [("acc0", i) for i in range(8)])
            dump("acc1", accs[1], [128, SEQ], F32, [("acc1", i) for i in range(8)])
        for h in range(2):
            o = 1 - h
            rows = slice(0, 64) if h == 0 else slice(64, 128)
            ok = [("acc" + str(o), i) for i in range(8)]
            S.op("pool", lambda e, h=h, o=o, rows=rows: e.dma_start(out=accs[o][rows, :], in_=rn_d[h:h + 1, :].to_broadcast([64, SEQ])),
                 r=[("rn", h)], w=ok, dma=True)
        def make_tail(p, gT, gkey):
            def tail():
                for half in range(2):
                    cs = slice(2048 * half, 2048 * half + 2048)
                    k0 = [("acc0", 4 * half + i) for i in range(4)]
                    k1 = [("acc1", 4 * half + i) for i in range(4)]
                    gk = [(gkey, 4 * half + i) for i in range(4)]
                    S.op("dve", lambda e, cs=cs: e.tensor_tensor(out=accs[0][:, cs], in0=accs[0][:, cs], in1=accs[1][:, cs], op=ALU.mult), r=k0 + k1, w=k0)
                    S.op("dve", lambda e, cs=cs: e.tensor_tensor(out=gT[:, cs], in0=accs[0][:, cs], in1=gT[:, cs], op=ALU.mult), r=k0 + gk, w=gk)
                    S.op("sp", lambda e, cs=cs: e.dma_start(out=mix_d[p][:, cs], in_=gT[:, cs]), r=gk, w=[("mixd", p)], dma=True)
            return tail

        tails.append(make_tail(p, gT, gkey))
        if p == 3 or stop_after == "attn0":
            while tails:
                tails.pop(0)()
        if p == 0:
            dump("mixT0", gT, [128, SEQ], BF16, [(gkey, i) for i in range(8)])
            if stop_after == "attn0":
                S.emit(sems, dsem)
                return nc, dump_list
    S.barrier()
    for pp in range(4):
        dump("mixd%d" % pp, mix_d[pp], [128, SEQ], BF16, [("mixd", pp)])
    if stop_after == "phaseA":
        S.emit(sems, dsem)
        return nc, dump_list
    M = Mem(PBASE)
    wstB = [M.f32(1024).rearrange("p (c f) -> p c f", c=8) for _ in range(2)]
    wbfB = M.bf(8 * 512).rearrange("p (c f) -> p c f", c=8)
    TB = M.off
    tabs = [[M.f32(512) for _ in range(2)] for _ in range(4)]
    qrawB = [M.bf(512) for _ in range(2)]
    t1B = [M.f32(512) for _ in range(2)]
    t2B = [M.f32(512) for _ in range(2)]
    grt = [M.bf(4 * 512).rearrange("p (f t) -> p f t", f=4) for _ in range(2)]
    TB_END = M.off
    qrT = [M.bf(SEQ) for _ in range(2)]
    krT = [M.bf(SEQ) for _ in range(2)]
    Vr = M.bf(32 * 512).rearrange("p (n v) -> p n v", n=32)
    WO_OFF = M.off
    wo = M.bf(8 * DM).rearrange("p (c f) -> p c f", c=8)

    for tt in range(8):
        for ft in range(2):
            tsl = (2 * tt + ft) % 2
            for k4 in range(4):
                S.op("sp", lambda e, tt=tt, ft=ft, k4=k4, tsl=tsl: e.dma_start(out=tabs[k4][tsl], in_=ropeR_d[4 * ft + k4, :, 512 * tt:512 * tt + 512]),
                     w=[("tab", k4, tsl)], dma=True)
            for qk in range(2):
                f = 2 * qk + ft
                pb = cnt["pb"] % 4
                cnt["pb"] += 1
                for c in range(8):
                    S.op("pe", lambda e, c=c, f=f, tt=tt, pb=pb: e.matmul(bank(pb), lhsT=wbf[:, c, 128 * f:128 * f + 128],
                                                                          rhs=hT[:, c, 512 * tt:512 * tt + 512], start=(c == 0), stop=(c == 7)),
                         r=[("wbf", f), ("hT", tt)], w=[("ps", pb)])
                flush()
                dst = (qrT if qk == 0 else krT)[ft][:, 512 * tt:512 * tt + 512]
                pend.append(rope(pb, dst, ("qrT" if qk == 0 else "krT", ft, tt), tabs[2 * qk][tsl], tabs[2 * qk + 1][tsl],
                                 [("tab", 2 * qk, tsl), ("tab", 2 * qk + 1, tsl)], qrawB, t1B, t2B))
    flush()
    dump("qrT0", qrT[0], [128, SEQ], BF16, [("qrT", 0, i) for i in range(8)])
    dump("krT0", krT[0], [128, SEQ], BF16, [("krT", 0, i) for i in range(8)])
    if stop_after == "B1":
        S.emit(sems, dsem)
        return nc, dump_list
    load_w(wstB, wbfB, 2560, 512, "wbfB", win_v, extra_w=[("wbf", f) for f in range(4)])
    for n in range(32):
        pb = cnt["pb"] % 4
        cnt["pb"] += 1
        for c in range(8):
            S.op("pe", lambda e, c=c, n=n, pb=pb: e.matmul(bank(pb), lhsT=hT[:, c, 128 * n:128 * n + 128], rhs=wbfB[:, c, :], start=(c == 0), stop=(c == 7)),
                 r=[("wbfB", i) for i in range(4)] + [("hT", n // 4)], w=[("ps", pb)])
        if n % 2 == 0:
            S.op("dve", lambda e, n=n, pb=pb: e.tensor_copy(out=Vr[:, n, :], in_=bank(pb)), w=[("ps", pb), ("Vr", n)])
        else:
            S.op("act", lambda e, n=n, pb=pb: e.activation(out=Vr[:, n, :], in_=bank(pb), func=AF.Copy), w=[("ps", pb), ("Vr", n)])
    load_w(wstB, wbfB, 3072, 512, "wbfB", win_v)
    for tt in range(8):
        gs = tt % 2
        for ft in range(4):
            pb = cnt["pb"] % 4
            cnt["pb"] += 1
            for c in range(8):
                S.op("pe", lambda e, c=c, ft=ft, tt=tt, pb=pb: e.matmul(bank(pb), lhsT=wbfB[:, c, 128 * ft:128 * ft + 128],
                                                                        rhs=hT[:, c, 512 * tt:512 * tt + 512], start=(c == 0), stop=(c == 7)),
                     r=[("wbfB", ft), ("hT", tt)], w=[("ps", pb)])
            S.op("act", lambda e, ft=ft, pb=pb, gs=gs: e.activation(out=grt[gs][:, ft, :], in_=bank(pb), func=AF.Silu), w=[("ps", pb), ("grt", gs, ft)])
            S.op("dve", lambda e, ft=ft, gs=gs: e.tensor_scalar(out=grt[gs][:, ft, :], in0=grt[gs][:, ft, :], scalar1=gn[:, ft:ft + 1], scalar2=None, op0=ALU.mult),
                 r=[("grt", gs, ft), "gn"], w=[("grt", gs, ft)])
        S.op("sp", lambda e, tt=tt, gs=gs: e.dma_start(out=grs_d[:, :, 512 * tt:512 * tt + 512].rearrange("f p t -> p f t"), in_=grt[gs]),
             r=[("grt", gs, ft) for ft in range(4)], w=[("grs", tt)], dma=True)
    if stop_after == "B3":
        S.emit(sems, dsem)
        return nc, dump_list
    S.barrier()

    M = Mem(TB)
    Ktok = [M.bf(256) for _ in range(2)]
    sT = [M.bf(512) for _ in range(2)]
    onb = [M.bf(512) for _ in range(2)]
    grl = [M.bf(4 * 512).rearrange("p (f t) -> p f t", f=4) for _ in range(2)]
    mixst = [M.bf(4 * 512).rearrange("p (f t) -> p f t", f=4) for _ in range(2)]
    assert M.off <= TB_END - 256
    mixA = Mem(0).bf(8 * SEQ).rearrange("p (c t) -> p c t", c=8)
    def mixa_load(c):
        S.op("sp", lambda e: e.dma_start(out=mixA[:, c, :], in_=mix_d[c]), r=[("mixd", c)], w=[("mixA", c)] + [("hT", i) for i in range(8)], dma=True)
    dump("mixa0e", mixA[:, 0:4, 0:512], [128, 4, 512], BF16, [("mixA", h) for h in range(4)])
    for pp in range(4):
        dump("mixe%d" % pp, mix_d[pp], [128, SEQ], BF16, [("mixd", pp)])
    def wo_chunk(j):
        sl = cnt["w"] % len(wstB)
        cnt["w"] += 1
        S.op("sp", lambda e: e.dma_start(out=wstB[sl], in_=wout_v[:, :, 128 * j:128 * j + 128]), w=[("wst", sl)], dma=True)
        S.op("pool", lambda e: e.tensor_copy(out=wo[:, :, 128 * j:128 * j + 128], in_=wstB[sl]), r=[("wst", sl)], w=[("wo", j)])
    S.op("dve", lambda e: e.memset(state, 0.0), w=["state"])
    nbias = bnst[:, 0:4]
    bnst2 = [bnst, Mem(TB_END - 256).f32(24)]
    bnmv2 = [bnmv, Mem(TB_END - 128).f32(8)]
    rstd2 = [rstd4, Mem(TB_END - 64).f32(4)]
    nb2 = [Mem(TB_END - 32).f32(4), Mem(TB_END - 16).f32(4)]

    OB = (2, 3, 6)

    def r_1(n):
        tsel = slice(128 * n, 128 * n + 128)
        ks = n % 2
        grp, sub4 = n // 4, n % 4
        gl = grp % 2
        if sub4 == 0:
            S.op("sp", lambda e: e.dma_start(out=grl[gl], in_=grs_d[:, :, 512 * grp:512 * grp + 512].rearrange("f p t -> p f t")),
                 r=[("grs", grp)], w=[("grl", gl)], dma=True)
        for ft in range(2):
            S.op("pe", lambda e, ft=ft: e.transpose(out=bankbf(0)[:, 128 * ft:128 * ft + 128], in_=krT[ft][:, tsel], identity=ident),
                 r=[("krT", ft, n // 4), "ident"], w=[("ps", 0)])
        S.op("act", lambda e: e.activation(out=Ktok[ks], in_=bankbf(0)[:, 0:256], func=AF.Copy), w=[("ps", 0), ("Ktok", ks)])
        S.op("pool", lambda e: e.tensor_tensor(out=Ktok[ks], in0=Ktok[ks], in1=gck, op=ALU.mult), r=["gck", ("Ktok", ks)], w=[("Ktok", ks)])
        for h in (0, 2, 1, 3):
            r0 = 64 * (h % 2)
            sbk = 1 if h % 2 == 0 else 7
            S.op("pe", lambda e, h=h, r0=r0, sbk=sbk: e.matmul(bank(sbk)[:, 128 * (h // 2):128 * (h // 2) + 128], lhsT=krT[h // 2][r0:r0 + 64, tsel],
                                                          rhs=qrT[h // 2][r0:r0 + 64, tsel], start=True, stop=True),
                 r=[("krT", h // 2, n // 4), ("qrT", h // 2, n // 4)], w=[("ps", sbk)])
        for hp, sbk in ((0, 1), (1, 7)):
            S.op("dve", lambda e, hp=hp, sbk=sbk: e.tensor_tensor(out=sT[ks][:, 256 * hp:256 * hp + 256], in0=bank(sbk)[:, 0:256], in1=rmask[:, 0:256], op=ALU.mult),
                 r=["rmask"], w=[("ps", sbk), ("sT", ks)])

    def r_2(n):
        tsel = slice(128 * n, 128 * n + 128)
        ks = n % 2
        ob = OB[n % 3]
        for h in range(4):
            r0 = 64 * (h % 2)
            S.op("pe", lambda e, h=h, r0=r0: e.matmul(bank(4)[r0:r0 + 64, 128 * (h // 2):128 * (h // 2) + 128], lhsT=Ktok[ks][:, 64 * h:64 * h + 64],
                                                 rhs=Vr[:, n, 128 * h:128 * h + 128], start=True, stop=True),
                 r=[("Ktok", ks), ("Vr", n)], w=[("ps", 4)])
        for h in range(4):
            r0 = 64 * (h % 2)
            sblk = 2 * (h % 2) + h // 2
            S.op("pe", lambda e, h=h, sblk=sblk: e.matmul(bank(ob)[:, 128 * h:128 * h + 128], lhsT=sT[ks][:, 128 * sblk:128 * sblk + 128],
                                                     rhs=Vr[:, n, 128 * h:128 * h + 128], start=True, stop=(n == 0)),
                 r=[("sT", ks), ("Vr", n)], w=[("ps", ob)])
            if n > 0:
                S.op("pe", lambda e, h=h, r0=r0: e.matmul(bank(ob)[:, 128 * h:128 * h + 128], lhsT=qrT[h // 2][r0:r0 + 64, tsel],
                                                     rhs=state_bf[r0:r0 + 64, 128 * (h // 2):128 * (h // 2) + 128], start=False, stop=True),
                     r=[("qrT", h // 2, n // 4), "state_bf"], w=[("ps", ob)])
        for cb in range(2):
            S.op("dve", lambda e, cb=cb: e.scalar_tensor_tensor(out=state[:, 128 * cb:128 * cb + 128], in0=state[:, 128 * cb:128 * cb + 128],
                                                                scalar=gsc[:, cb:cb + 1], in1=bank(4)[:, 128 * cb:128 * cb + 128], op0=ALU.mult, op1=ALU.add),
                 r=["state", "gsc"], w=["state", ("ps", 4)])
        S.op("act", lambda e: e.activation(out=state_bf, in_=state, func=AF.Copy), r=["state"], w=["state_bf"])

    def r_3(n):
        ks = n % 2
        ob = OB[n % 3]
        for h in range(4):
            S.op("dve", lambda e, h=h: e.bn_stats(out=bnst2[ks][:, 6 * h:6 * h + 6], in_=bank(ob)[:, 128 * h:128 * h + 128]), w=[("ps", ob), ("bnst", ks, h)])
        for h in range(4):
            S.op("dve", lambda e, h=h: e.bn_aggr(out=bnmv2[ks][:, 2 * h:2 * h + 2], in_=bnst2[ks][:, 6 * h:6 * h + 6]), r=[("bnst", ks, h)], w=[("bnmv", ks, h)])
        S.op("pool", lambda e: e.tensor_scalar(out=rstd2[ks], in0=bnmv2[ks][:, 1:8:2], scalar1=1e-5, scalar2=None, op0=ALU.add), r=[("bnmv", ks, h) for h in range(4)], w=[("rstd", ks)])
        S.op("pool", lambda e: e.tensor_tensor(out=rstd2[ks], in0=rstd2[ks], in1=mhalf, op=ALU.pow), r=[("rstd", ks), "mhalf"], w=[("rstd", ks)])
        S.op("pool", lambda e: e.tensor_tensor(out=nb2[ks], in0=bnmv2[ks][:, 0:8:2], in1=rstd2[ks], op=ALU.mult), r=[("bnmv", ks, h) for h in range(4)] + [("rstd", ks)], w=[("nb", ks)])
        S.op("pool", lambda e: e.tensor_scalar(out=nb2[ks], in0=nb2[ks], scalar1=-1.0, scalar2=None, op0=ALU.mult), r=[("nb", ks)], w=[("nb", ks)])

    def r_4(n):
        ks = n % 2
        ob = OB[n % 3]
        for h in range(4):
            S.op("act", lambda e, h=h: e.activation(out=onb[ks][:, 128 * h:128 * h + 128], in_=bank(ob)[:, 128 * h:128 * h + 128], func=AF.Identity,
                                                    bias=nb2[ks][:, h:h + 1], scale=rstd2[ks][:, h:h + 1]),
                 r=[("nb", ks), ("rstd", ks)], w=[("ps", ob), ("onb", ks, h)])

    def r_5(n):
        ks = n % 2
        grp, sub4 = n // 4, n % 4
        gl = grp % 2
        for h in range(4):
            S.op("pe", lambda e, h=h: e.transpose(out=bankbf(5)[:, 128 * h:128 * h + 128], in_=onb[ks][:, 128 * h:128 * h + 128], identity=ident),
                 r=[("onb", ks, h), "ident"], w=[("ps", 5)])
        S.op("dve", lambda e: e.tensor_tensor(out=mixA[:, 4:8, 128 * n:128 * n + 128], in0=bankbf(5)[:, 0:512].rearrange("p (h t) -> p h t", h=4),
                                              in1=grl[gl][:, :, 128 * sub4:128 * sub4 + 128], op=ALU.mult),
             r=[("grl", gl)], w=[("ps", 5)] + [("mixA", 4 + h) for h in range(4)])

    stages = ((r_1, 0), (r_2, 1), (r_3, 2), (r_4, 3), (r_5, 4))
    for step in range(32 + 4):
        for fn, sk in stages:
            if 0 <= step - sk < 32:
                fn(step - sk)
        if step % 4 == 2 and step // 4 < 8:
            wo_chunk(step // 4)
        if step % 4 == 1 and step // 4 < 4:
            mixa_load(step // 4)
    dump("mixr0", mixA[:, 4:8, 0:512], [128, 4, 512], BF16, [("mixA", 4 + h) for h in range(4)])
    dump("mixa0", mixA[:, 0:4, 0:512], [128, 4, 512], BF16, [("mixA", h) for h in range(4)])
    dump("wo", wo, [128, 8, DM], BF16, [("wo", j) for j in range(8)])
    if stop_after == "ret":
        S.emit(sems, dsem)
        return nc, dump_list
    S.barrier()

    M = Mem(PBASE + 16384)
    xr = [M.f32(DM) for _ in range(8)]
    yb = [M.f32(DM) for _ in range(4)]
    junk2s = [M.bf(DM) for _ in range(4)]
    fg = M.f32(DM)
    assert M.off <= WO_OFF
    S.op("sp", lambda e: e.dma_start(out=fg, in_=fg_d), w=["fg"], dma=True)
    S.op("dve", lambda e: e.memset(st_ss, 0.0), w=["st_ss"])
    def c_a(tt):
        sl = tt % 4
        xsl = tt % 8
        S.op("sp", lambda e: e.dma_start(out=xr[xsl], in_=x_d[128 * tt:128 * tt + 128, :]), w=[("xr", xsl)], dma=True)
        for nh in range(2):
            pb = (2 * tt + nh) % 8
            for c in range(8):
                S.op("pe", lambda e, c=c, nh=nh, pb=pb: e.matmul(bank(pb), lhsT=mixA[:, c, 128 * tt:128 * tt + 128], rhs=wo[:, c, 512 * nh:512 * nh + 512],
                                                                 start=(c == 0), stop=(c == 7)),
                     r=[("mixA", c)] + [("wo", 4 * nh + i) for i in range(4)], w=[("ps", pb)])
            S.op("dve", lambda e, nh=nh, pb=pb: e.tensor_tensor(out=yb[sl][:, 512 * nh:512 * nh + 512], in0=bank(pb), in1=xr[xsl][:, 512 * nh:512 * nh + 512], op=ALU.add),
                 r=[("xr", xsl)], w=[("ps", pb), ("yb", sl, nh)])
        S.op("act", lambda e: e.activation(out=junk2s[tt % 4], in_=yb[sl], func=AF.Square, accum_out=st_ss[:, tt:tt + 1]),
             r=[("yb", sl, 0), ("yb", sl, 1), "st_ss"], w=[("junk2", tt % 4), ("ss", tt)])

    def c_b(tt):
        S.op("dve", lambda e: e.tensor_scalar(out=st_ms[:, tt:tt + 1], in0=st_ss[:, tt:tt + 1], scalar1=1.0 / DM, scalar2=1e-6,
                                              op0=ALU.mult, op1=ALU.add), r=[("ss", tt)], w=[("ms", tt)])
        S.op("pool", lambda e: e.tensor_tensor(out=st_rs[:, tt:tt + 1], in0=st_ms[:, tt:tt + 1], in1=mhalf[:, 0:1], op=ALU.pow),
             r=[("ms", tt), "mhalf"], w=[("rs", tt)])

    def c_c(tt):
        sl = tt % 4
        S.op("dve", lambda e: e.scalar_tensor_tensor(out=yb[sl], in0=yb[sl], scalar=st_rs[:, tt:tt + 1], in1=fg, op0=ALU.mult, op1=ALU.mult),
             r=[("yb", sl, 0), ("yb", sl, 1), ("rs", tt), "fg"], w=[("yb", sl, 0), ("yb", sl, 1)])
        S.op("pool", lambda e: e.dma_start(out=out_d[128 * tt:128 * tt + 128, :], in_=yb[sl]), r=[("yb", sl, 0), ("yb", sl, 1)], w=[("out", tt)], dma=True)

    for step in range(32 + 2):
        if step < 32:
            c_a(step)
        if 1 <= step < 33:
            c_b(step - 1)
        if step >= 2:
            c_c(step - 2)
    S.emit(sems, dsem)
    return nc, dump_list


_CACHE = {}


def _prep_inputs(x, norm_gain, w_in, ret_gn_gain, w_out, final_gain):
    c = _host_constants()
    w_in_p = np.ascontiguousarray(np.asarray(w_in, np.float32)[0][:, _win_cols()])
    shared = dict(c)
    shared["w_in"] = w_in_p
    shared["w_out"] = np.ascontiguousarray(np.asarray(w_out, np.float32)[0])
    shared["ng"] = np.ascontiguousarray(np.asarray(norm_gain, np.float32)[0].reshape(8, 128).T)
    shared["gn"] = np.ascontiguousarray(np.asarray(ret_gn_gain, np.float32)[0].reshape(4, 128).T)
    shared["fg"] = np.ascontiguousarray(np.broadcast_to(np.asarray(final_gain, np.float32)[None, :], (128, DM)))
    x = np.asarray(x, np.float32)
    return [dict(shared, x=np.ascontiguousarray(x[b])) for b in range(x.shape[0])]


def kernel(x, norm_gain, w_in, ret_gn_gain, w_out, final_gain):
    in_maps = _prep_inputs(x, norm_gain, w_in, ret_gn_gain, w_out, final_gain)
    nc, _ = build()
    res = run_bass_kernel_spmd(nc, in_maps, core_ids=list(range(NCORES)))
    return np.stack([np.asarray(r["out"], np.float32) for r in res.results], axis=0)
```
